# Optimizing a Trainium2 kernel written in Bass

```python
import math
import jax, jax.numpy as jnp
from jax import lax
import numpy as np

D_MODEL = 2048
BATCH = 2
SEQ = 4096
DEPTH = 4

HEAD_DIM = 128
N_HEADS = D_MODEL // HEAD_DIM
WIDTH = N_HEADS * HEAD_DIM
N_MIXERS = 3
ROPE_THETA = 10000.0
EPS = 1e-6
ATTN_SCALE = HEAD_DIM ** -0.5
NEG = -1e30
FORCE = 1e30
TINY = 1e-30
MOBA_BLOCK = 256
MOBA_TOPK = 3
MOBA_Q_CHUNK = 16
SB_Q_BLOCK = 128
NSA_KV_HEADS = 4
NSA_HPG = N_HEADS // NSA_KV_HEADS
KV_WIDTH = NSA_KV_HEADS * HEAD_DIM
CMP_BLOCK = 32
CMP_STRIDE = 16
SLC_BLOCK = 64
SLC_TOPK = 16
WINDOW = 512
NSA_Q_CHUNK = 32
WIN_Q_BLOCK = 128
NSA_IN_COLS = WIDTH + 6 * KV_WIDTH + WIDTH + 3 * N_HEADS

kernel_name = 'hybrid_moba_stickbreak_nsa_trunk'


def rmsnorm(x, g):
    x32 = x.astype(jnp.float32)
    y = x32 * lax.rsqrt(jnp.mean(x32 * x32, axis=-1, keepdims=True) + EPS)
    return (y * g.astype(jnp.float32)).astype(x.dtype)


def rope(x, pos):
    half = x.shape[-1] // 2
    inv_freq = jnp.exp(-math.log(ROPE_THETA) * jnp.arange(half, dtype=jnp.float32) / half)
    ang = pos.astype(jnp.float32)[:, None] * inv_freq[None, :]
    cos, sin = jnp.cos(ang), jnp.sin(ang)
    x32 = x.astype(jnp.float32)
    x1, x2 = x32[..., :half], x32[..., half:]
    return jnp.concatenate([x1 * cos - x2 * sin, x2 * cos + x1 * sin], axis=-1).astype(x.dtype)


def split_heads(t, n):
    B, S, _ = t.shape
    return t.reshape(B, S, n, HEAD_DIM).transpose(0, 2, 1, 3)


def merge_heads(o):
    B, H, S, D = o.shape
    return o.transpose(0, 2, 1, 3).reshape(B, S, H * D)


def masked_softmax(logits, mask):
    logits = jnp.where(mask, logits, NEG)
    m = jnp.max(logits, axis=-1, keepdims=True)
    e = jnp.where(mask, jnp.exp(logits - m), 0.0)
    return e / jnp.maximum(jnp.sum(e, axis=-1, keepdims=True), TINY)


def gated_output(x, o, gate, w_out):
    y = merge_heads(o).astype(x.dtype) * jax.nn.silu(gate)
    return x + y @ w_out


def moba_attention(q, k, v):
    B, H, S, D = q.shape
    nb = -(-S // MOBA_BLOCK)
    pad = nb * MOBA_BLOCK - S
    kb = jnp.pad(k, ((0, 0), (0, 0), (0, pad), (0, 0))).reshape(B, H, nb, MOBA_BLOCK, D)
    vb = jnp.pad(v, ((0, 0), (0, 0), (0, pad), (0, 0))).reshape(B, H, nb, MOBA_BLOCK, D)
    k_mean = jnp.mean(kb.astype(jnp.float32), axis=3)
    n_sel = min(MOBA_TOPK, nb - 1)
    nc = S // MOBA_Q_CHUNK
    q_chunks = jnp.moveaxis(q.reshape(B, H, nc, MOBA_Q_CHUNK, D), 2, 0)
    bi = jnp.arange(B)[:, None, None, None]
    hi = jnp.arange(H)[None, :, None, None]
    blk_pos = jnp.arange(MOBA_BLOCK)
    blk_ids = jnp.arange(nb)

    def one_chunk(args):
        c, qc = args
        t = c * MOBA_Q_CHUNK + jnp.arange(MOBA_Q_CHUNK)
        cur = (c * MOBA_Q_CHUNK) // MOBA_BLOCK
        k_own = lax.dynamic_index_in_dim(kb, cur, axis=2, keepdims=False)
        v_own = lax.dynamic_index_in_dim(vb, cur, axis=2, keepdims=False)
        s_own = jnp.einsum('bhqd,bhkd->bhqk', qc, k_own).astype(jnp.float32) * ATTN_SCALE
        own_mask = (cur * MOBA_BLOCK + blk_pos)[None, :] <= t[:, None]
        s_own = jnp.where(own_mask, s_own, NEG)
        if n_sel == 0:
            p = jax.nn.softmax(s_own, axis=-1)
            return jnp.einsum('bhqk,bhkd->bhqd', p.astype(v.dtype), v_own)
        gate = jnp.einsum('bhqd,bhnd->bhqn', qc.astype(jnp.float32), k_mean)
        gate = jnp.where(blk_ids < cur, gate, NEG)
        _, idx = lax.top_k(gate, n_sel)
        sel_valid = idx < cur
        k_sel = kb[bi, hi, idx]
        v_sel = vb[bi, hi, idx]
        s_sel = jnp.einsum('bhqd,bhqnkd->bhqnk', qc, k_sel).astype(jnp.float32) * ATTN_SCALE
        s_sel = jnp.where(sel_valid[..., None], s_sel, NEG).reshape(B, H, MOBA_Q_CHUNK, n_sel * MOBA_BLOCK)
        p = jax.nn.softmax(jnp.concatenate([s_sel, s_own], axis=-1), axis=-1)
        p_sel = p[..., :n_sel * MOBA_BLOCK].reshape(B, H, MOBA_Q_CHUNK, n_sel, MOBA_BLOCK)
        p_own = p[..., n_sel * MOBA_BLOCK:]
        return (jnp.einsum('bhqnk,bhqnkd->bhqd', p_sel.astype(v.dtype), v_sel)
                + jnp.einsum('bhqk,bhkd->bhqd', p_own.astype(v.dtype), v_own))

    out = lax.map(one_chunk, (jnp.arange(nc), q_chunks))
    return jnp.moveaxis(out, 0, 2).reshape(B, H, S, D)


def moba_layer(x, norm, w_in, q_norm, k_norm, w_out):
    pos = jnp.arange(x.shape[1])
    h = rmsnorm(x, norm) @ w_in
    q, k, v, gate = jnp.split(h, [WIDTH, 2 * WIDTH, 3 * WIDTH], axis=-1)
    q = rope(rmsnorm(split_heads(q, N_HEADS), q_norm), pos)
    k = rope(rmsnorm(split_heads(k, N_HEADS), k_norm), pos)
    o = moba_attention(q, k, split_heads(v, N_HEADS))
    return gated_output(x, o, gate, w_out)


def stick_breaking_attention(q, k, v):
    B, H, S, D = q.shape
    nqb = S // SB_Q_BLOCK
    q_blocks = jnp.moveaxis(q.reshape(B, H, nqb, SB_Q_BLOCK, D), 2, 0)
    s_pos = jnp.arange(S)

    def one_block(args):
        c, qb = args
        t = c * SB_Q_BLOCK + jnp.arange(SB_Q_BLOCK)
        z = jnp.einsum('bhqd,bhkd->bhqk', qb, k).astype(jnp.float32) * ATTN_SCALE
        past = s_pos[None, :] < t[:, None]
        log_keep = jnp.where(past, jax.nn.log_sigmoid(-z), 0.0)
        log_after = lax.cumsum(log_keep, axis=3, reverse=True) - log_keep
        a = jnp.where(past, jnp.exp(jax.nn.log_sigmoid(z) + log_after), 0.0)
        return jnp.einsum('bhqk,bhkd->bhqd', a.astype(v.dtype), v)

    out = lax.map(one_block, (jnp.arange(nqb), q_blocks))
    return jnp.moveaxis(out, 0, 2).reshape(B, H, S, D)


def stick_breaking_layer(x, norm, w_in, w_out):
    h = rmsnorm(x, norm) @ w_in
    q, k, v, gate = jnp.split(h, [WIDTH, 2 * WIDTH, 3 * WIDTH], axis=-1)
    o = stick_breaking_attention(split_heads(q, N_HEADS), split_heads(k, N_HEADS), split_heads(v, N_HEADS))
    return gated_output(x, o, gate, w_out)


def nsa_compressed(qg, kc, vc, pos_c):
    S = qg.shape[3]
    t = jnp.arange(S)
    logits = jnp.einsum('bgrqd,bgnd->bgrqn', qg, kc).astype(jnp.float32) * ATTN_SCALE
    p = masked_softmax(logits, pos_c[None, :] <= t[:, None])
    o = jnp.einsum('bgrqn,bgnd->bgrqd', p.astype(vc.dtype), vc)
    return o, jnp.sum(p, axis=2)


def nsa_selected(qg, ks, vs, p_cmp):
    B, G, R, S, D = qg.shape
    n_cmp = p_cmp.shape[-1]
    n_slc = S // SLC_BLOCK
    c_start = jnp.arange(n_cmp) * CMP_STRIDE
    s_start = jnp.arange(n_slc) * SLC_BLOCK
    overlap = ((c_start[:, None] < s_start[None, :] + SLC_BLOCK)
               & (c_start[:, None] + CMP_BLOCK > s_start[None, :])).astype(jnp.float32)
    imp = jnp.einsum('bgqn,nj->bgqj', p_cmp, overlap)
    t = jnp.arange(S)
    cur = t // SLC_BLOCK
    j = jnp.arange(n_slc)
    forced = (j[None, :] == 0) | (j[None, :] == cur[:, None]) | (j[None, :] == cur[:, None] - 1)
    imp = jnp.where(forced, FORCE, imp)
    imp = jnp.where(j[None, :] <= cur[:, None], imp, NEG)
    n_sel = min(SLC_TOPK, n_slc)
    _, idx = lax.top_k(imp, n_sel)
    ksb = ks.reshape(B, G, n_slc, SLC_BLOCK, D)
    vsb = vs.reshape(B, G, n_slc, SLC_BLOCK, D)
    nc = S // NSA_Q_CHUNK
    q_chunks = jnp.moveaxis(qg.reshape(B, G, R, nc, NSA_Q_CHUNK, D), 3, 0)
    idx_chunks = jnp.moveaxis(idx.reshape(B, G, nc, NSA_Q_CHUNK, n_sel), 2, 0)
    bi = jnp.arange(B)[:, None, None, None]
    gi = jnp.arange(G)[None, :, None, None]
    blk_pos = jnp.arange(SLC_BLOCK)

    def one_chunk(args):
        c, qc, ic = args
        tc = c * NSA_Q_CHUNK + jnp.arange(NSA_Q_CHUNK)
        k_sel = ksb[bi, gi, ic]
        v_sel = vsb[bi, gi, ic]
        key_pos = ic[..., None] * SLC_BLOCK + blk_pos
        mask = key_pos <= tc[None, None, :, None, None]
        logits = jnp.einsum('bgrqd,bgqnkd->bgrqnk', qc, k_sel).astype(jnp.float32) * ATTN_SCALE
        logits = jnp.where(mask[:, :, None], logits, NEG).reshape(B, G, R, NSA_Q_CHUNK, n_sel * SLC_BLOCK)
        p = jax.nn.softmax(logits, axis=-1).reshape(B, G, R, NSA_Q_CHUNK, n_sel, SLC_BLOCK)
        return jnp.einsum('bgrqnk,bgqnkd->bgrqd', p.astype(vs.dtype), v_sel)

    out = lax.map(one_chunk, (jnp.arange(nc), q_chunks, idx_chunks))
    return jnp.moveaxis(out, 0, 3).reshape(B, G, R, S, D)


def nsa_window(qg, kw, vw):
    B, G, R, S, D = qg.shape
    nqb = S // WIN_Q_BLOCK
    span = WINDOW + WIN_Q_BLOCK
    kp = jnp.pad(kw, ((0, 0), (0, 0), (WINDOW, 0), (0, 0)))
    vp = jnp.pad(vw, ((0, 0), (0, 0), (WINDOW, 0), (0, 0)))
    widx = jnp.arange(nqb)[:, None] * WIN_Q_BLOCK + jnp.arange(span)[None, :]
    k_blk = kp[:, :, widx]
    v_blk = vp[:, :, widx]
    qb = qg.reshape(B, G, R, nqb, WIN_Q_BLOCK, D)
    logits = jnp.einsum('bgrcqd,bgckd->bgrcqk', qb, k_blk).astype(jnp.float32) * ATTN_SCALE
    s_pos = widx - WINDOW
    t_pos = jnp.arange(nqb)[:, None] * WIN_Q_BLOCK + jnp.arange(WIN_Q_BLOCK)[None, :]
    diff = t_pos[:, :, None] - s_pos[:, None, :]
    mask = (diff >= 0) & (diff < WINDOW) & (s_pos[:, None, :] >= 0)
    p = jax.nn.softmax(jnp.where(mask, logits, NEG), axis=-1)
    o = jnp.einsum('bgrcqk,bgckd->bgrcqd', p.astype(vw.dtype), v_blk)
    return o.reshape(B, G, R, S, D)


def nsa_layer(x, norm, w_in, q_norm, kc_norm, ks_norm, kw_norm, cmp_wk, cmp_wv, cmp_pos, w_out):
    B, S, _ = x.shape
    pos = jnp.arange(S)
    h = rmsnorm(x, norm) @ w_in
    cuts = [WIDTH + i * KV_WIDTH for i in range(7)] + [2 * WIDTH + 6 * KV_WIDTH]
    q, kc, vc, ks, vs, kw, vw, gate, bgate = jnp.split(h, cuts, axis=-1)
    q = rope(rmsnorm(split_heads(q, N_HEADS), q_norm), pos)
    ks = rope(rmsnorm(split_heads(ks, NSA_KV_HEADS), ks_norm), pos)
    kw = rope(rmsnorm(split_heads(kw, NSA_KV_HEADS), kw_norm), pos)
    kc, vc = split_heads(kc, NSA_KV_HEADS), split_heads(vc, NSA_KV_HEADS)
    vs, vw = split_heads(vs, NSA_KV_HEADS), split_heads(vw, NSA_KV_HEADS)
    n_cmp = (S - CMP_BLOCK) // CMP_STRIDE + 1
    cidx = jnp.arange(n_cmp)[:, None] * CMP_STRIDE + jnp.arange(CMP_BLOCK)[None, :]
    kc_c = jnp.einsum('bgnld,lde->bgne', kc[:, :, cidx] + cmp_pos, cmp_wk)
    vc_c = jnp.einsum('bgnld,lde->bgne', vc[:, :, cidx] + cmp_pos, cmp_wv)
    pos_c = cidx[:, -1]
    kc_c = rope(rmsnorm(kc_c, kc_norm), pos_c)
    qg = q.reshape(B, NSA_KV_HEADS, NSA_HPG, S, HEAD_DIM)
    o_cmp, p_cmp = nsa_compressed(qg, kc_c, vc_c, pos_c)
    o_slc = nsa_selected(qg, ks, vs, p_cmp)
    o_win = nsa_window(qg, kw, vw)
    g = jax.nn.sigmoid(bgate.astype(jnp.float32)).reshape(B, S, 3, N_HEADS).transpose(2, 0, 3, 1)[..., None]
    shp = (B, N_HEADS, S, HEAD_DIM)
    o = g[0] * o_cmp.reshape(shp) + g[1] * o_slc.reshape(shp) + g[2] * o_win.reshape(shp)
    return gated_output(x, o, gate, w_out)


def setup_inputs(seed: int = 0) -> dict:
    key = jax.random.key(seed)
    keys = iter(jax.random.split(key, 64))

    def normal(shape, scale):
        return jax.random.normal(next(keys), shape, jnp.float32) * scale

    def gain(n):
        return 1.0 + 0.1 * normal((n,), 1.0)

    inputs = {'x': normal((BATCH, SEQ, D_MODEL), 1.0)}
    for i in range(DEPTH):
        p = 'l%d_' % i
        kind = i % N_MIXERS
        inputs[p + 'norm'] = gain(D_MODEL)
        if kind == 0:
            inputs[p + 'w_in'] = normal((D_MODEL, 4 * WIDTH), D_MODEL ** -0.5)
            inputs[p + 'q_norm'] = gain(HEAD_DIM)
            inputs[p + 'k_norm'] = gain(HEAD_DIM)
        elif kind == 1:
            inputs[p + 'w_in'] = normal((D_MODEL, 4 * WIDTH), D_MODEL ** -0.5)
        else:
            inputs[p + 'w_in'] = normal((D_MODEL, NSA_IN_COLS), D_MODEL ** -0.5)
            inputs[p + 'q_norm'] = gain(HEAD_DIM)
            inputs[p + 'kc_norm'] = gain(HEAD_DIM)
            inputs[p + 'ks_norm'] = gain(HEAD_DIM)
            inputs[p + 'kw_norm'] = gain(HEAD_DIM)
            inputs[p + 'cmp_wk'] = normal((CMP_BLOCK, HEAD_DIM, HEAD_DIM), (CMP_BLOCK * HEAD_DIM) ** -0.5)
            inputs[p + 'cmp_wv'] = normal((CMP_BLOCK, HEAD_DIM, HEAD_DIM), (CMP_BLOCK * HEAD_DIM) ** -0.5)
            inputs[p + 'cmp_pos'] = normal((CMP_BLOCK, HEAD_DIM), 0.1)
        inputs[p + 'w_out'] = normal((WIDTH, D_MODEL), WIDTH ** -0.5)
    return inputs


def reference(x,
              l0_norm, l0_w_in, l0_q_norm, l0_k_norm, l0_w_out,
              l1_norm, l1_w_in, l1_w_out,
              l2_norm, l2_w_in, l2_q_norm, l2_kc_norm, l2_ks_norm, l2_kw_norm,
              l2_cmp_wk, l2_cmp_wv, l2_cmp_pos, l2_w_out,
              l3_norm, l3_w_in, l3_q_norm, l3_k_norm, l3_w_out):
    layer_params = [
        dict(norm=l0_norm, w_in=l0_w_in, q_norm=l0_q_norm, k_norm=l0_k_norm, w_out=l0_w_out),
        dict(norm=l1_norm, w_in=l1_w_in, w_out=l1_w_out),
        dict(norm=l2_norm, w_in=l2_w_in, q_norm=l2_q_norm, kc_norm=l2_kc_norm, ks_norm=l2_ks_norm,
             kw_norm=l2_kw_norm, cmp_wk=l2_cmp_wk, cmp_wv=l2_cmp_wv, cmp_pos=l2_cmp_pos, w_out=l2_w_out),
        dict(norm=l3_norm, w_in=l3_w_in, q_norm=l3_q_norm, k_norm=l3_k_norm, w_out=l3_w_out),
    ]
    mixers = (moba_layer, stick_breaking_layer, nsa_layer)
    for i in range(DEPTH):
        x = mixers[i % N_MIXERS](x, **layer_params[i])
    return x
```

```python
import math
from contextlib import ExitStack

import numpy as np
import ml_dtypes

import concourse.bass as bass
import concourse.mybir as mybir
from concourse.bass_utils import run_bass_kernel_spmd

F32 = mybir.dt.float32
BF16 = mybir.dt.bfloat16
AF = mybir.ActivationFunctionType
ALU = mybir.AluOpType
AX = mybir.AxisListType
NPBF = ml_dtypes.bfloat16

D_MODEL = 2048
BATCH = 2
SEQ = 4096
HD = 128
NH = 16
EPS = 1e-6
SCALE = HD ** -0.5
NCORES = 8


class Res:
    __slots__ = ("w", "r")

    def __init__(self):
        self.w = None
        self.r = {}


class Prog:
    COMPUTE = ("pe", "act", "dve", "pool")
    NDMA_SEMS = 8

    def __init__(self, nc):
        self.nc = nc
        self.es = ExitStack()
        self.sems = {}
        self.cnt = {}
        self.q = {e: [] for e in ("pe", "act", "dve", "pool", "sp")}
        self.known = {e: {} for e in self.q}
        for e in self.COMPUTE:
            self.sems[e] = nc.alloc_semaphore(name="s_" + e)
            self.cnt[e] = 0
        self.dma_rr = {"sp": 0, "pool": 0}
        for qn in ("sp", "pool"):
            for i in range(self.NDMA_SEMS):
                k = "d_%s%d" % (qn, i)
                self.sems[k] = nc.alloc_semaphore(name=k)
                self.cnt[k] = 0
        self.n_sb = 0
        self.n_ps = 0
        self.phase = 0
        self.sems["cc"] = nc.alloc_semaphore(name="s_cc")
        self.cnt["cc"] = 0

    def sb(self, shape, dt, name=None):
        self.n_sb += 1
        return self.es.enter_context(self.nc.sbuf_tensor("sb%d_" % self.phase + (name or ("t%d" % self.n_sb)), list(shape), dt))

    def ps(self, shape, dt, name=None):
        self.n_ps += 1
        return self.es.enter_context(self.nc.psum_tensor("ps%d_" % self.phase + (name or ("t%d" % self.n_ps)), list(shape), dt))

    def _deps(self, eng, reads, writes):
        deps = {}

        def add(k, v):
            if v > deps.get(k, 0):
                deps[k] = v

        for r in reads:
            if r.w is not None:
                add(*r.w)
        for w in writes:
            if w.w is not None:
                add(*w.w)
            for k, v in w.r.items():
                add(k, v)
        waits = []
        kn = self.known[eng]
        for k, v in deps.items():
            if eng == "pe" and k == "pe":
                continue
            if kn.get(k, 0) >= v:
                continue
            kn[k] = v
            waits.append((k, v))
        return waits

    def op(self, eng, fn, reads=(), writes=()):
        waits = self._deps(eng, reads, writes)
        self.cnt[eng] += 1
        n = self.cnt[eng]
        for r in reads:
            r.r[eng] = n
        for w in writes:
            w.w = (eng, n)
            w.r = {}
        self.q[eng].append((waits, fn, eng, 1))

    def dma(self, fn, reads=(), writes=(), queue="sp"):
        i = self.dma_rr[queue]
        self.dma_rr[queue] = (i + 1) % self.NDMA_SEMS
        k = "d_%s%d" % (queue, i)
        waits = self._deps(queue, reads, writes)
        kn = self.known[queue]
        prev = self.cnt[k]
        if prev > 0 and kn.get(k, 0) < prev:
            kn[k] = prev
            waits.append((k, prev))
        self.cnt[k] += 16
        n = self.cnt[k]
        for r in reads:
            r.r[k] = n
        for w in writes:
            w.w = (k, n)
            w.r = {}
        self.q[queue].append((waits, fn, k, 16))

    def cc(self, fn, reads=(), writes=()):
        waits = self._deps("pool", reads, writes)
        kn = self.known["pool"]
        prev = self.cnt["cc"]
        if prev > 0 and kn.get("cc", 0) < prev:
            kn["cc"] = prev
            waits.append(("cc", prev))
        self.cnt["cc"] += 1
        n = self.cnt["cc"]
        for r in reads:
            r.r["cc"] = n
        for w in writes:
            w.w = ("cc", n)
            w.r = {}
        self.q["pool"].append((waits, fn, "cc", 1))

    def barrier(self):
        allw = [(k, v) for k, v in self.cnt.items() if v > 0]
        for eng in self.q:
            kn = self.known[eng]
            waits = []
            for k, v in allw:
                if eng == "pe" and k == "pe":
                    continue
                if kn.get(k, 0) < v:
                    kn[k] = v
                    waits.append((k, v))
            if waits:
                self.q[eng].append((waits, None, None, 0))

    def flush(self):
        nc = self.nc
        sems = self.sems
        q = self.q

        def run(eng, lst):
            for waits, fn, k, inc in lst:
                for (wk, wv) in waits:
                    eng.wait_ge(sems[wk], wv)
                if fn is not None:
                    ins = fn(eng)
                    ins.then_inc(sems[k], inc)

        with nc.Block() as block:
            @block.tensor
            def _(e):
                run(e, q["pe"])

            @block.scalar
            def _(e):
                run(e, q["act"])

            @block.vector
            def _(e):
                run(e, q["dve"])

            @block.gpsimd
            def _(e):
                run(e, q["pool"])

            @block.sync
            def _(e):
                run(e, q["sp"])
        self.q = {e: [] for e in q}

    def end_phase(self):
        self.barrier()
        self.flush()
        self.es.close()
        self.es = ExitStack()
        self.phase += 1

    def finish(self):
        self.end_phase()
        nc = self.nc
        nc.all_engine_barrier()
        nc.clear_and_free_semaphores(list(self.sems.values()))
        nc.all_engine_barrier()


def emit_outproj(P, ntok, w, x_tile, out_tile, yT=None, yG=None, sel=None):
    wbf = P.sb([128, 16, 2048], BF16, "wbf")
    r_w = [Res() for _ in range(16)]
    stg = [P.sb([128, 2048], F32, "stg%d" % i) for i in range(2)]
    r_stg = [Res() for _ in range(2)]
    ysb = P.sb([128, 16, ntok], BF16, "ysb")
    if yG is None:
        r_y = [Res()] * 16
        P.dma(lambda e: e.dma_start(out=ysb[:], in_=yT.rearrange("(c p) t -> p c t", p=128)),
              writes=[r_y[0]], queue="pool")
    else:
        r_y = [Res() for _ in range(16)]
        sels = P.sb([128, 4], F32, "sels")
        r_sel = Res()
        P.dma(lambda e: e.dma_start(out=sels[:], in_=sel), writes=[r_sel], queue="pool")
        cand = [P.sb([128, 4, ntok], BF16, "cand%d" % i) for i in range(2)]
        r_cand = [Res() for _ in range(2)]
        for kc in range(16):
            cb = kc % 2
            for j in range(4):
                P.dma(lambda e, kc=kc, cb=cb, j=j: e.dma_start(out=cand[cb][:, j, :], in_=yG[j][kc * 128:(kc + 1) * 128, :]),
                      writes=[r_cand[cb]], queue="pool")
            eng = "dve"
            P.op(eng, lambda e, kc=kc, cb=cb: e.tensor_scalar(
                out=ysb[:, kc, :], in0=cand[cb][:, 0, :], scalar1=sels[:, 0:1], scalar2=None, op0=ALU.mult),
                reads=[r_cand[cb], r_sel], writes=[r_y[kc]])
            for j in range(1, 4):
                P.op(eng, lambda e, kc=kc, cb=cb, j=j: e.scalar_tensor_tensor(
                    out=ysb[:, kc, :], in0=cand[cb][:, j, :], scalar=sels[:, j:j + 1], in1=ysb[:, kc, :],
                    op0=ALU.mult, op1=ALU.add), reads=[r_cand[cb], r_sel, r_y[kc]], writes=[r_y[kc]])
    for kc in range(16):
        s = kc % 2
        P.dma(lambda e, kc=kc, s=s: e.dma_start(out=stg[s][:], in_=w[kc * 128:(kc + 1) * 128, :]),
              writes=[r_stg[s]])
        eng = "dve" if kc % 2 == 0 else "act"
        if eng == "dve":
            P.op("dve", lambda e, kc=kc, s=s: e.tensor_copy(out=wbf[:, kc, :], in_=stg[s][:]),
                 reads=[r_stg[s]], writes=[r_w[kc]])
        else:
            P.op("act", lambda e, kc=kc, s=s: e.copy(out=wbf[:, kc, :], in_=stg[s][:]),
                 reads=[r_stg[s]], writes=[r_w[kc]])
    pss = [P.ps([128, 512], F32, "pso%d" % i) for i in range(4)]
    r_ps = [Res() for _ in range(4)]
    xt = [P.sb([128, 2048], F32, "xt%d" % i) for i in range(2)]
    r_xt = [Res() for _ in range(2)]
    ot = [P.sb([128, 2048], F32, "ot%d" % i) for i in range(2)]
    r_ot = [Res() for _ in range(2)]
    pi = 0
    for tt in range(ntok // 128):
        s = tt % 2
        P.dma(lambda e, tt=tt, s=s: e.dma_start(out=xt[s][:], in_=x_tile(tt)),
              writes=[r_xt[s]], queue="pool")
        for ct in range(4):
            b = pi % 4
            pi += 1
            for kc in range(16):
                P.op("pe", lambda e, kc=kc, tt=tt, ct=ct, b=b: e.matmul(
                    pss[b][:], lhsT=ysb[:, kc, tt * 128:(tt + 1) * 128],
                    rhs=wbf[:, kc, ct * 512:(ct + 1) * 512], start=(kc == 0), stop=(kc == 15)),
                    reads=[r_y[kc], r_w[kc]], writes=[r_ps[b]])
            P.op("dve", lambda e, s=s, ct=ct, b=b: e.tensor_tensor(
                out=ot[s][:, ct * 512:(ct + 1) * 512], in0=pss[b][:], in1=xt[s][:, ct * 512:(ct + 1) * 512],
                op=ALU.add), reads=[r_ps[b], r_xt[s]], writes=[r_ot[s]])
        P.dma(lambda e, tt=tt, s=s: e.dma_start(out=out_tile(tt), in_=ot[s][:]), reads=[r_ot[s]])


def build_outproj(ntok=1024):
    nc = bass.Bass("TRN2", target_bir_lowering=False)
    yT = nc.dram_tensor("yT", [2048, ntok], BF16, kind="ExternalInput").ap()
    x = nc.dram_tensor("x", [ntok, 2048], F32, kind="ExternalInput").ap()
    w = nc.dram_tensor("w", [2048, 2048], F32, kind="ExternalInput").ap()
    out = nc.dram_tensor("out", [ntok, 2048], F32, kind="ExternalOutput").ap()
    P = Prog(nc)
    emit_outproj(P, ntok, w, lambda tt: x[tt * 128:(tt + 1) * 128, :], lambda tt: out[tt * 128:(tt + 1) * 128, :], yT=yT)
    P.finish()
    return nc


def emit_proj(P, tm_blocks, fm_funcs, n_sg, io, S=SEQ):
    n_tm = sum(a * 128 + b for a, b in tm_blocks)
    NR = sum(a for a, b in tm_blocks)
    NV = sum(b for a, b in tm_blocks)
    NF = 128 * len(fm_funcs)
    ncols = n_tm + NF + n_sg
    x_tile = io["x_tile"]
    w, gx, gains, cosd, sind, identf, identb = (io[k] for k in ("w", "gx", "gains", "cos", "sin", "identf", "identb"))
    ropeT, vtm, fmT, sgT = io.get("ropeT"), io.get("vtm"), io.get("fmT"), io.get("sgT")
    idf = P.sb([128, 128], F32, "idf")
    idb = P.sb([128, 128], BF16, "idb")
    gxs = P.sb([128, 16], F32, "gxs")
    gns = P.sb([128, max(n_tm, 1)], F32, "gns")
    r_c = Res()
    P.dma(lambda e: e.dma_start(out=idf[:], in_=identf), writes=[r_c], queue="pool")
    P.dma(lambda e: e.dma_start(out=idb[:], in_=identb), writes=[r_c], queue="pool")
    P.dma(lambda e: e.dma_start(out=gxs[:], in_=gx), writes=[r_c], queue="pool")
    P.dma(lambda e: e.dma_start(out=gns[:], in_=gains), writes=[r_c], queue="pool")
    wbf = P.sb([128, 16, ncols], BF16, "wbf")
    r_w = [Res() for _ in range(16)]
    stg = [P.sb([128, ncols], F32, "stg%d" % i) for i in range(2)]
    r_stg = [Res() for _ in range(2)]
    for kc in range(16):
        s = kc % 2
        P.dma(lambda e, kc=kc, s=s: e.dma_start(out=stg[s][:], in_=w[kc * 128:(kc + 1) * 128, :]),
              writes=[r_stg[s]])
        P.op("dve", lambda e, kc=kc, s=s: e.tensor_scalar(
            out=wbf[:, kc, :], in0=stg[s][:], scalar1=gxs[:, kc:kc + 1], scalar2=None, op0=ALU.mult),
            reads=[r_stg[s], r_c], writes=[r_w[kc]])

    xt = [P.sb([128, 2048], F32, "xt%d" % i) for i in range(2)]
    r_xt = [Res() for _ in range(2)]
    junk = P.sb([128, 2048], BF16, "junk")
    r_junk = Res()
    st1 = [P.sb([128, 4], F32, "st1_%d" % i) for i in range(2)]
    r_st1 = [Res() for _ in range(2)]
    xn = [P.sb([128, 2048], BF16, "xn%d" % i) for i in range(2)]
    r_xn = [Res() for _ in range(2)]
    xnT = [P.sb([128, 16, 512], BF16, "xnT%d" % i) for i in range(2)]
    r_xnT = [Res() for _ in range(2)]
    pT = [P.ps([128, 1024], BF16, "pT%d" % i) for i in range(2)]
    r_pT = [Res() for _ in range(2)]
    pacc = [P.ps([128, 512], F32, "pacc%d" % i) for i in range(4)]
    r_pacc = [Res() for _ in range(4)]
    ptr = [P.ps([128, 512], F32, "ptr%d" % i) for i in range(2)]
    r_ptr = [Res() for _ in range(2)]
    cs = [P.sb([128, 512], F32, "cs%d" % i) for i in range(2)]
    sn = [P.sb([128, 512], F32, "sn%d" % i) for i in range(2)]
    r_cs = [Res() for _ in range(2)]
    sq = P.sb([128, 512], F32, "sq")
    r_sq = Res()
    hst = P.sb([128, 8], F32, "hst")
    r_hst = Res()
    qn = P.sb([128, 512], F32, "qn")
    r_qn = Res()
    t1 = P.sb([128, 512], F32, "t1")
    t2 = P.sb([128, 512], F32, "t2")
    r_t = Res()
    nblk = len(tm_blocks)
    qr = [P.sb([128, 4, 512], F32, "qr%d" % i) for i in range(max(nblk, 1))]
    r_qr = [[Res() for _ in range(4)] for _ in range(max(nblk, 1))]
    vsb = [P.sb([128, 512], BF16, "vsb%d" % i) for i in range(2)]
    r_vsb = [Res() for _ in range(2)]
    qTs = [P.sb([128, 512], BF16, "qTs%d" % i) for i in range(2)]
    r_qTs = [Res() for _ in range(2)]
    fms = [P.sb([128, 512], BF16, "fms%d" % i) for i in range(2)]
    r_fms = [Res() for _ in range(2)]
    sgs = P.sb([128, 512], F32, "sgs")
    r_sgs = Res()

    cnt = {"pacc": 0, "ptr": 0, "vsb": 0, "qTs": 0, "fms": 0, "pT": 0, "x": 0}
    NG = S // 512
    def prep(G):
        gb = G % 2
        for st in range(4):
            tok0 = G * 512 + st * 128
            xb = cnt["x"] % 2
            cnt["x"] += 1
            P.dma(lambda e, xb=xb, tok0=tok0: e.dma_start(out=xt[xb][:], in_=x_tile(tok0 // 128)),
                  writes=[r_xt[xb]], queue="pool")
            P.op("act", lambda e, xb=xb: e.activation(
                out=junk[:], in_=xt[xb][:], func=AF.Square, accum_out=st1[xb][:, 0:1]),
                reads=[r_xt[xb]], writes=[r_junk, r_st1[xb]])
            P.op("act", lambda e, xb=xb: e.activation(
                out=st1[xb][:, 1:2], in_=st1[xb][:, 0:1], func=AF.Sqrt, bias=EPS, scale=1.0 / 2048),
                reads=[r_st1[xb]], writes=[r_st1[xb]])
            P.op("dve", lambda e, xb=xb: e.reciprocal(out=st1[xb][:, 2:3], in_=st1[xb][:, 1:2]),
                 reads=[r_st1[xb]], writes=[r_st1[xb]])
            P.op("dve", lambda e, xb=xb: e.tensor_scalar(
                out=xn[xb][:], in0=xt[xb][:], scalar1=st1[xb][:, 2:3], scalar2=None, op0=ALU.mult),
                reads=[r_xt[xb], r_st1[xb]], writes=[r_xn[xb]])
            for half in range(2):
                pb = cnt["pT"] % 2
                cnt["pT"] += 1
                for j in range(8):
                    kc = half * 8 + j
                    P.op("pe", lambda e, xb=xb, kc=kc, j=j, pb=pb: e.transpose(
                        out=pT[pb][:, j * 128:(j + 1) * 128], in_=xn[xb][:, kc * 128:(kc + 1) * 128],
                        identity=idb[:]), reads=[r_xn[xb], r_c], writes=[r_pT[pb]])
                eng = "act" if half == 0 else "dve"
                if eng == "act":
                    P.op("act", lambda e, pb=pb, half=half, st=st, gb=gb: e.copy(
                        out=xnT[gb][:, half * 8:(half + 1) * 8, st * 128:(st + 1) * 128],
                        in_=pT[pb][:].rearrange("p (j t) -> p j t", t=128)),
                        reads=[r_pT[pb]], writes=[r_xnT[gb]])
                else:
                    P.op("dve", lambda e, pb=pb, half=half, st=st, gb=gb: e.tensor_copy(
                        out=xnT[gb][:, half * 8:(half + 1) * 8, st * 128:(st + 1) * 128],
                        in_=pT[pb][:].rearrange("p (j t) -> p j t", t=128)),
                        reads=[r_pT[pb]], writes=[r_xnT[gb]])
    def proj(G):
        gb = G % 2
        col0 = 0
        for bi, (nr, nv) in enumerate(tm_blocks):
            bw = nr * 128 + nv
            for st in range(4):
                tok0 = G * 512 + st * 128
                pb = cnt["pacc"] % 4
                cnt["pacc"] += 1
                for kc in range(16):
                    P.op("pe", lambda e, kc=kc, st=st, gb=gb, pb=pb, col0=col0, bw=bw: e.matmul(
                        pacc[pb][:, 0:bw], lhsT=xnT[gb][:, kc, st * 128:(st + 1) * 128],
                        rhs=wbf[:, kc, col0:col0 + bw], start=(kc == 0), stop=(kc == 15)),
                        reads=[r_xnT[gb], r_w[kc]], writes=[r_pacc[pb]])
                if nr > 0:
                    rw = nr * 128
                    cb = (G * 4 + st) % 2
                    P.dma(lambda e, cb=cb, tok0=tok0: e.dma_start(out=cs[cb][:], in_=cosd[tok0:tok0 + 128, :]),
                          writes=[r_cs[cb]], queue="pool")
                    P.dma(lambda e, cb=cb, tok0=tok0: e.dma_start(out=sn[cb][:], in_=sind[tok0:tok0 + 128, :]),
                          writes=[r_cs[cb]], queue="pool")
                    P.op("act", lambda e, pb=pb, rw=rw: e.activation(
                        out=sq[:, 0:rw], in_=pacc[pb][:, 0:rw], func=AF.Square),
                        reads=[r_pacc[pb]], writes=[r_sq])
                    P.op("dve", lambda e, rw=rw, nr=nr: e.tensor_reduce(
                        out=hst[:, 0:nr], in_=sq[:, 0:rw].rearrange("p (h d) -> p h d", d=128),
                        axis=AX.X, op=ALU.add), reads=[r_sq], writes=[r_hst])
                    P.op("act", lambda e, nr=nr: e.activation(
                        out=hst[:, 4:4 + nr], in_=hst[:, 0:nr], func=AF.Sqrt, bias=EPS, scale=1.0 / 128),
                        reads=[r_hst], writes=[r_hst])
                    P.op("dve", lambda e, nr=nr: e.reciprocal(out=hst[:, 0:nr], in_=hst[:, 4:4 + nr]),
                         reads=[r_hst], writes=[r_hst])
                    for h in range(nr):
                        P.op("dve", lambda e, h=h, pb=pb, col0=col0: e.scalar_tensor_tensor(
                            out=qn[:, h * 128:(h + 1) * 128], in0=pacc[pb][:, h * 128:(h + 1) * 128],
                            scalar=hst[:, h:h + 1], in1=gns[:, col0 + h * 128:col0 + (h + 1) * 128],
                            op0=ALU.mult, op1=ALU.mult),
                            reads=[r_pacc[pb], r_hst, r_c], writes=[r_qn])
                    P.op("pool", lambda e, rw=rw, cb=cb: e.tensor_tensor(
                        out=t1[:, 0:rw], in0=qn[:, 0:rw], in1=cs[cb][:, 0:rw], op=ALU.mult),
                        reads=[r_qn, r_cs[cb]], writes=[r_t])
                    for hf in range(2):
                        P.op("pool", lambda e, rw=rw, cb=cb, hf=hf: e.tensor_tensor(
                            out=t2[:, 0:rw].rearrange("p (h two d) -> p h two d", two=2, d=64)[:, :, hf, :],
                            in0=qn[:, 0:rw].rearrange("p (h two d) -> p h two d", two=2, d=64)[:, :, 1 - hf, :],
                            in1=sn[cb][:, 0:rw].rearrange("p (h two d) -> p h two d", two=2, d=64)[:, :, hf, :],
                            op=ALU.mult), reads=[r_qn, r_cs[cb], r_t], writes=[r_t])
                    P.op("pool", lambda e, rw=rw, bi=bi, st=st: e.tensor_tensor(
                        out=qr[bi][:, st, 0:rw], in0=t1[:, 0:rw], in1=t2[:, 0:rw], op=ALU.add),
                        reads=[r_t], writes=[r_qr[bi][st]])
                if nv > 0:
                    vb = cnt["vsb"] % 2
                    cnt["vsb"] += 1
                    voff = sum(b for a, b in tm_blocks[:bi])
                    P.op("act", lambda e, vb=vb, pb=pb, nr=nr, nv=nv: e.copy(
                        out=vsb[vb][:, 0:nv], in_=pacc[pb][:, nr * 128:nr * 128 + nv]),
                        reads=[r_pacc[pb]], writes=[r_vsb[vb]])
                    P.dma(lambda e, vb=vb, tok0=tok0, voff=voff, nv=nv: e.dma_start(
                        out=vtm[tok0:tok0 + 128, voff:voff + nv], in_=vsb[vb][:, 0:nv]),
                        reads=[r_vsb[vb]])
            hoff = sum(a for a, b in tm_blocks[:bi])
            for h in range(nr):
                tb = cnt["ptr"] % 2
                cnt["ptr"] += 1
                for st in range(4):
                    P.op("pe", lambda e, bi=bi, st=st, h=h, tb=tb: e.transpose(
                        out=ptr[tb][:, st * 128:(st + 1) * 128], in_=qr[bi][:, st, h * 128:(h + 1) * 128],
                        identity=idf[:]), reads=[r_qr[bi][st], r_c], writes=[r_ptr[tb]])
                qb = cnt["qTs"] % 2
                cnt["qTs"] += 1
                P.op("act", lambda e, qb=qb, tb=tb: e.copy(out=qTs[qb][:], in_=ptr[tb][:]),
                     reads=[r_ptr[tb]], writes=[r_qTs[qb]])
                P.dma(lambda e, qb=qb, hh=hoff + h, G=G: e.dma_start(
                    out=ropeT[hh, :, G * 512:(G + 1) * 512], in_=qTs[qb][:]), reads=[r_qTs[qb]])
            col0 += bw
        for fi, func in enumerate(fm_funcs):
            pb = cnt["pacc"] % 4
            cnt["pacc"] += 1
            c0 = n_tm + fi * 128
            for kc in range(16):
                P.op("pe", lambda e, kc=kc, gb=gb, pb=pb, c0=c0: e.matmul(
                    pacc[pb][:], lhsT=wbf[:, kc, c0:c0 + 128], rhs=xnT[gb][:, kc, :],
                    start=(kc == 0), stop=(kc == 15)),
                    reads=[r_xnT[gb], r_w[kc]], writes=[r_pacc[pb]])
            fb = cnt["fms"] % 2
            cnt["fms"] += 1
            af = AF.Silu if func == "silu" else AF.Copy
            P.op("act", lambda e, fb=fb, pb=pb, af=af: e.activation(out=fms[fb][:], in_=pacc[pb][:], func=af),
                 reads=[r_pacc[pb]], writes=[r_fms[fb]])
            P.dma(lambda e, fb=fb, fi=fi, G=G: e.dma_start(
                out=fmT[fi * 128:(fi + 1) * 128, G * 512:(G + 1) * 512], in_=fms[fb][:]), reads=[r_fms[fb]])
        if n_sg > 0:
            pb = cnt["pacc"] % 4
            cnt["pacc"] += 1
            c0 = n_tm + NF
            for kc in range(16):
                P.op("pe", lambda e, kc=kc, gb=gb, pb=pb, c0=c0: e.matmul(
                    pacc[pb][0:n_sg, :], lhsT=wbf[:, kc, c0:c0 + n_sg], rhs=xnT[gb][:, kc, :],
                    start=(kc == 0), stop=(kc == 15)),
                    reads=[r_xnT[gb], r_w[kc]], writes=[r_pacc[pb]])
            P.op("act", lambda e, pb=pb: e.activation(out=sgs[0:n_sg, :], in_=pacc[pb][0:n_sg, :], func=AF.Sigmoid),
                 reads=[r_pacc[pb]], writes=[r_sgs])
            P.dma(lambda e, G=G: e.dma_start(out=sgT[0:n_sg, G * 512:(G + 1) * 512], in_=sgs[0:n_sg, :]),
                  reads=[r_sgs])

    prep(0)
    for G in range(NG):
        if G + 1 < NG:
            prep(G + 1)
        proj(G)


def build_proj(tm_blocks, fm_funcs, n_sg, S=SEQ):
    nc = bass.Bass("TRN2", target_bir_lowering=False)
    n_tm = sum(a * 128 + b for a, b in tm_blocks)
    NR = sum(a for a, b in tm_blocks)
    NV = sum(b for a, b in tm_blocks)
    NF = 128 * len(fm_funcs)
    ncols = n_tm + NF + n_sg
    x = nc.dram_tensor("x", [S, 2048], F32, kind="ExternalInput").ap()
    io = dict(
        x_tile=lambda T: x[T * 128:(T + 1) * 128, :],
        w=nc.dram_tensor("w", [2048, ncols], F32, kind="ExternalInput").ap(),
        gx=nc.dram_tensor("gx", [128, 16], F32, kind="ExternalInput").ap(),
        gains=nc.dram_tensor("gains", [128, max(n_tm, 1)], F32, kind="ExternalInput").ap(),
        cos=nc.dram_tensor("cos", [S, 512], F32, kind="ExternalInput").ap(),
        sin=nc.dram_tensor("sin", [S, 512], F32, kind="ExternalInput").ap(),
        identf=nc.dram_tensor("identf", [128, 128], F32, kind="ExternalInput").ap(),
        identb=nc.dram_tensor("identb", [128, 128], BF16, kind="ExternalInput").ap(),
        ropeT=nc.dram_tensor("ropeT", [max(NR, 1), 128, S], BF16, kind="ExternalOutput").ap(),
        vtm=nc.dram_tensor("vtm", [S, max(NV, 1)], BF16, kind="ExternalOutput").ap(),
        fmT=nc.dram_tensor("fmT", [max(NF, 1), S], BF16, kind="ExternalOutput").ap(),
        sgT=nc.dram_tensor("sgT", [max(n_sg, 1), S], F32, kind="ExternalOutput").ap())
    P = Prog(nc)
    emit_proj(P, tm_blocks, fm_funcs, n_sg, io, S)
    P.finish()
    return nc


def rope_tables(S=SEQ):
    half = HD // 2
    inv_freq = np.exp(-math.log(10000.0) * np.arange(half, dtype=np.float32) / half).astype(np.float32)
    ang = np.arange(S, dtype=np.float32)[:, None] * inv_freq[None, :]
    c = np.cos(ang).astype(np.float32)
    s = np.sin(ang).astype(np.float32)
    cos128 = np.concatenate([c, c], axis=1)
    sin128 = np.concatenate([-s, s], axis=1)
    return np.tile(cos128, (1, 4)), np.tile(sin128, (1, 4))


class AttnCtx:
    def __init__(self, P, QW, p_dt=BF16, n_s=3):
        self.P = P
        self.QW = QW
        self.S = [P.ps([128, 512], F32, "S%d" % i) for i in range(n_s)]
        self.r_S = [Res() for _ in range(n_s)]
        self.pt = [P.sb([128, QW], p_dt, "pt%d" % i) for i in range(4)]
        self.r_pt = [Res() for _ in range(4)]
        self.o = [P.ps([128, 512], F32, "o%d" % i) for i in range(2)]
        self.r_o = [Res() for _ in range(2)]
        self.den = [P.ps([128, 512], F32, "den%d" % i) for i in range(2)]
        self.r_den = [Res() for _ in range(2)]
        self.si = 0
        self.pi = 0
        self.oi = 0
        self.ones = P.sb([128, 128], p_dt, "ones")
        self.r_ones = Res()
        P.op("pool", lambda e: e.memset(self.ones[:], 1.0), writes=[self.r_ones])
        self.pt32 = None

    def add_fp32(self):
        P = self.P
        self.pt32 = [P.sb([128, self.QW], F32, "pt32_%d" % i) for i in range(4)]
        self.r_pt32 = [Res() for _ in range(4)]
        self.ones32 = P.sb([128, 128], F32, "ones32")
        P.op("pool", lambda e: e.memset(self.ones32[:], 1.0), writes=[self.r_ones])
        self.pi32 = 0


def softmax_branch(P, A, qT_ap, r_q, ktiles, maskeng="pool", fp32=False):
    QW = A.QW
    ob = A.oi % 2
    A.oi += 1
    n = len(ktiles)
    used = []
    info = []

    def stage1(i):
        kt = ktiles[i]
        nk = kt["nk"]
        sb_ = A.si % len(A.S)
        A.si += 1
        if fp32:
            pb = A.pi32 % 4
            A.pi32 += 1
            pts, r_pts, ones = A.pt32, A.r_pt32, A.ones32
        else:
            pb = A.pi % 4
            A.pi += 1
            pts, r_pts, ones = A.pt, A.r_pt, A.ones
        used.append(pb)
        info.append((pb, pts, r_pts, ones))
        bias = kt.get("bias")
        P.op("pe", lambda e, kt=kt, sb_=sb_, nk=nk, bias=bias: e.matmul(
            A.S[sb_][0:nk, 0:QW], lhsT=kt["kT"], rhs=qT_ap, start=True, stop=(bias is None)),
            reads=[kt["r_k"], r_q], writes=[A.r_S[sb_]])
        if bias is not None:
            P.op("pe", lambda e, sb_=sb_, nk=nk, bias=bias: e.matmul(
                A.S[sb_][0:nk, 0:QW], lhsT=bias[0], rhs=bias[1], start=False, stop=True),
                reads=list(bias[2]), writes=[A.r_S[sb_]])
        P.op("act", lambda e, sb_=sb_, pb=pb, nk=nk, pts=pts: e.activation(
            out=pts[pb][0:nk, :], in_=A.S[sb_][0:nk, 0:QW], func=AF.Exp, scale=SCALE),
            reads=[A.r_S[sb_]], writes=[r_pts[pb]])
        mask = kt.get("mask")
        if mask is not None:
            P.op(maskeng, lambda e, pb=pb, nk=nk, mask=mask, pts=pts: e.tensor_tensor(
                out=pts[pb][0:nk, :], in0=pts[pb][0:nk, :], in1=mask[0], op=ALU.mult),
                reads=[r_pts[pb], mask[1]], writes=[r_pts[pb]])

    def stage2(i):
        kt = ktiles[i]
        nk = kt["nk"]
        pb, pts, r_pts, ones = info[i]
        P.op("pe", lambda e, kt=kt, pb=pb, nk=nk, i=i, pts=pts: e.matmul(
            A.o[ob][:, 0:QW], lhsT=kt["v"], rhs=pts[pb][0:nk, :], start=(i == 0), stop=(i == n - 1)),
            reads=[kt["r_v"], r_pts[pb]], writes=[A.r_o[ob]])
        P.op("pe", lambda e, pb=pb, nk=nk, i=i, pts=pts, ones=ones: e.matmul(
            A.den[ob][:, 0:QW], lhsT=ones[0:nk, :], rhs=pts[pb][0:nk, :], start=(i == 0), stop=(i == n - 1)),
            reads=[A.r_ones, r_pts[pb]], writes=[A.r_den[ob]])

    for step in range(n + 1):
        if step < n:
            stage1(step)
        if step >= 1:
            stage2(step - 1)
    return A.o[ob], A.r_o[ob], A.den[ob], A.r_den[ob], used, None


def emit_moba(P, io, S=SEQ):
    ropeT, vtm, gT, identf, eseld, cmaskd, ywrite = (io[k] for k in ("ropeT", "vtm", "gT", "identf", "esel", "cmask", "ywrite"))
    QW = 256
    A = AttnCtx(P, QW, n_s=2)
    idf = P.sb([128, 128], F32, "idf")
    esel = P.sb([16, 16 * 128], BF16, "esel")
    cmask = P.sb([128, 512], BF16, "cmask")
    r_c = Res()
    P.dma(lambda e: e.dma_start(out=idf[:], in_=identf), writes=[r_c], queue="pool")
    P.dma(lambda e: e.dma_start(out=esel[:], in_=eseld), writes=[r_c], queue="pool")
    P.dma(lambda e: e.dma_start(out=cmask[:], in_=cmaskd), writes=[r_c], queue="pool")
    qT = [P.sb([128, S], BF16, "qT%d" % i) for i in range(2)]
    kT = [P.sb([128, S], BF16, "kT%d" % i) for i in range(2)]
    vv = [P.sb([128, S // 128, 128], BF16, "vv%d" % i) for i in range(2)]
    r_q = [Res() for _ in range(2)]
    r_k = [Res() for _ in range(2)]
    r_v = [Res() for _ in range(2)]
    km32 = P.sb([128, 16], F32, "km32")
    kmb = P.sb([128, 16], BF16, "kmb")
    r_km = Res()
    gps = P.ps([128, 512], F32, "gps")
    r_gps = Res()
    tps = P.ps([128, 512], F32, "tps")
    r_tps = Res()
    gm = P.sb([128, 16], F32, "gm")
    top8 = P.sb([128, 8], F32, "top8")
    bia = P.sb([128, 16], F32, "bia")
    r_gm = Res()
    biasT = [P.sb([16, S], BF16, "biasT%d" % i) for i in range(2)]
    r_bT = [Res() for _ in range(2)]
    gt = [P.sb([128, QW], BF16, "gt%d" % i) for i in range(2)]
    r_gt = [Res() for _ in range(2)]
    rd = P.sb([128, QW], F32, "rd")
    r_rd = Res()
    yo = P.sb([128, QW], F32, "yo")
    r_yo = Res()
    yb = [P.sb([128, QW], BF16, "yb%d" % i) for i in range(2)]
    r_yb = [Res() for _ in range(2)]
    cnt = 0
    for h in range(4):
        hb = h % 2
        P.dma(lambda e, h=h, hb=hb: e.dma_start(out=qT[hb][:], in_=ropeT[h]), writes=[r_q[hb]])
        P.dma(lambda e, h=h, hb=hb: e.dma_start(out=kT[hb][:], in_=ropeT[4 + h]), writes=[r_k[hb]])
        P.dma(lambda e, h=h, hb=hb: e.dma_start(
            out=vv[hb][:], in_=vtm[:, h * 128:(h + 1) * 128].rearrange("(t p) d -> p t d", p=128)),
            writes=[r_v[hb]])
        P.op("dve", lambda e, hb=hb: e.tensor_reduce(
            out=km32[:, 0:S // 256], in_=kT[hb][:].rearrange("p (n k) -> p n k", k=256), axis=AX.X, op=ALU.add),
            reads=[r_k[hb]], writes=[r_km])
        P.op("dve", lambda e: e.tensor_scalar(out=kmb[:, 0:S // 256], in0=km32[:, 0:S // 256], scalar1=1.0 / 256,
                                              scalar2=None, op0=ALU.mult), reads=[r_km], writes=[r_km])
        for qg in range(S // 512):
            for j in range(4):
                qi = qg * 4 + j
                cur = qi // 2
                P.op("pe", lambda e, qi=qi, hb=hb: e.matmul(
                    gps[:, 0:S // 256], lhsT=qT[hb][:, qi * 128:(qi + 1) * 128], rhs=kmb[:, 0:S // 256],
                    start=True, stop=True),
                    reads=[r_q[hb], r_km], writes=[r_gps])
                P.op("dve", lambda e: e.memset(gm[:], -1e30), writes=[r_gm])
                if cur > 0:
                    P.op("dve", lambda e, cur=cur: e.tensor_copy(out=gm[:, 0:cur], in_=gps[:, 0:cur]),
                         reads=[r_gps], writes=[r_gm])
                P.op("dve", lambda e: e.max(out=top8[:], in_=gm[:]), reads=[r_gm], writes=[r_gm])
                P.op("dve", lambda e: e.tensor_scalar(
                    out=bia[:], in0=gm[:], scalar1=top8[:, 2:3], scalar2=-30000.0, op0=ALU.is_lt, op1=ALU.mult),
                    reads=[r_gm], writes=[r_gm])
                P.op("pe", lambda e, j=j: e.transpose(
                    out=tps[0:16, j * 128:(j + 1) * 128], in_=bia[:], identity=idf[:]),
                    reads=[r_gm, r_c], writes=[r_tps])
            P.op("act", lambda e, qg=qg, hb=hb: e.copy(out=biasT[hb][:, qg * 512:(qg + 1) * 512], in_=tps[0:16, :]),
                 reads=[r_tps], writes=[r_bT[hb]])
        for c in range(S // QW):
            ktiles = []
            for kt in range(2 * c + 2):
                d = dict(kT=kT[hb][:, kt * 128:(kt + 1) * 128], r_k=r_k[hb], v=vv[hb][:, kt, :], r_v=r_v[hb], nk=128)
                n = kt // 2
                if n < c:
                    d["bias"] = (esel[:, n * 128:(n + 1) * 128], biasT[hb][:, c * QW:(c + 1) * QW], [r_c, r_bT[hb]])
                else:
                    d["mask"] = (cmask[:, (kt - 2 * c) * 256:(kt - 2 * c + 1) * 256], r_c)
                ktiles.append(d)
            gb = cnt % 2
            cnt += 1
            P.dma(lambda e, gb=gb, h=h, c=c: e.dma_start(
                out=gt[gb][:], in_=gT[h * 128:(h + 1) * 128, c * QW:(c + 1) * QW]), writes=[r_gt[gb]], queue="pool")
            o_ps, r_o, d_ps, r_d, _, _ = softmax_branch(P, A, qT[hb][:, c * QW:(c + 1) * QW], r_q[hb], ktiles)
            P.op("dve", lambda e, d_ps=d_ps: e.reciprocal(out=rd[:], in_=d_ps[:, 0:QW]),
                 reads=[r_d], writes=[r_rd])
            P.op("dve", lambda e, o_ps=o_ps: e.tensor_tensor(out=yo[:], in0=o_ps[:, 0:QW], in1=rd[:], op=ALU.mult),
                 reads=[r_o, r_rd], writes=[r_yo])
            P.op("pool", lambda e, gb=gb: e.tensor_tensor(out=yb[gb][:], in0=yo[:], in1=gt[gb][:], op=ALU.mult),
                 reads=[r_yo, r_gt[gb]], writes=[r_yb[gb]])
            P.dma(lambda e, gb=gb, h=h, c=c: e.dma_start(out=ywrite(h, c * QW, QW), in_=yb[gb][:]), reads=[r_yb[gb]])


def moba_consts():
    esel = np.zeros((16, 16 * 128), np.float32)
    for n in range(16):
        esel[n, n * 128:(n + 1) * 128] = 1.0
    k = np.arange(128)[:, None]
    q = np.arange(256)[None, :]
    cm = np.concatenate([(k <= q), (k + 128 <= q)], axis=1).astype(np.float32)
    return {"identf": np.eye(128, dtype=np.float32), "esel": esel.astype(NPBF), "cmask": cm.astype(NPBF)}


def emit_sb(P, io, S=SEQ):
    QW = 512
    fmT, vtm, trid, dmaskd, ywrite = (io[k] for k in ("fmT", "vtm", "tri", "dmask", "ywrite"))
    tri = P.sb([128, 128], BF16, "tri")
    ones = P.sb([128, 128], BF16, "ones")
    dmask = P.sb([128, 4 * QW], BF16, "dmask")
    r_c = Res()
    P.dma(lambda e: e.dma_start(out=tri[:], in_=trid), writes=[r_c], queue="pool")
    P.dma(lambda e: e.dma_start(out=dmask[:], in_=dmaskd), writes=[r_c], queue="pool")
    P.op("pool", lambda e: e.memset(ones[:], 1.0), writes=[r_c])
    qT = [P.sb([128, S], BF16, "qT%d" % i) for i in range(2)]
    kT = [P.sb([128, S], BF16, "kT%d" % i) for i in range(2)]
    vv = [P.sb([128, S // 128, 128], BF16, "vv%d" % i) for i in range(2)]
    r_q = [Res() for _ in range(2)]
    r_k = [Res() for _ in range(2)]
    r_v = [Res() for _ in range(2)]
    zps = [P.ps([128, QW], F32, "z%d" % i) for i in range(2)]
    r_z = [Res() for _ in range(2)]
    ups = [P.ps([128, QW], F32, "u%d" % i) for i in range(2)]
    r_u = [Res() for _ in range(2)]
    wps = [P.ps([128, QW], F32, "w%d" % i) for i in range(2)]
    r_wp = [Res() for _ in range(2)]
    ops_ = [P.ps([128, QW], F32, "o%d" % i) for i in range(2)]
    r_o = [Res() for _ in range(2)]
    NBUF = 2
    ex = [P.sb([128, QW], F32, "ex%d" % i) for i in range(NBUF)]
    r_ex = [Res() for _ in range(NBUF)]
    sp = [P.sb([128, QW], F32, "sp%d" % i) for i in range(NBUF)]
    r_sp = [Res() for _ in range(NBUF)]
    hi = [P.sb([128, QW], BF16, "hi%d" % i) for i in range(NBUF)]
    lo = [P.sb([128, QW], BF16, "lo%d" % i) for i in range(NBUF)]
    r_hl = [Res() for _ in range(NBUF)]
    tt = [P.sb([128, QW], F32, "tt%d" % i) for i in range(NBUF)]
    r_tt = [Res() for _ in range(NBUF)]
    aa = [P.sb([128, QW], BF16, "aa%d" % i) for i in range(NBUF)]
    r_aa = [Res() for _ in range(NBUF)]
    C = [P.sb([128, QW], F32, "C%d" % i) for i in range(2)]
    r_C = [Res() for _ in range(2)]
    gt = [P.sb([128, QW], BF16, "gt%d" % i) for i in range(2)]
    r_gt = [Res() for _ in range(2)]
    yb = [P.sb([128, QW], BF16, "yb%d" % i) for i in range(2)]
    r_yb = [Res() for _ in range(2)]
    ti = 0
    qc = 0
    for h in range(4):
        hb = h % 2
        P.dma(lambda e, h=h, hb=hb: e.dma_start(out=qT[hb][:], in_=fmT[h * 128:(h + 1) * 128, :]), writes=[r_q[hb]])
        P.dma(lambda e, h=h, hb=hb: e.dma_start(out=kT[hb][:], in_=fmT[512 + h * 128:512 + (h + 1) * 128, :]),
              writes=[r_k[hb]])
        P.dma(lambda e, h=h, hb=hb: e.dma_start(
            out=vv[hb][:], in_=vtm[:, h * 128:(h + 1) * 128].rearrange("(t p) d -> p t d", p=128)),
            writes=[r_v[hb]])
        for c in range(S // QW):
            cb = qc % 2
            qc += 1
            P.dma(lambda e, cb=cb, h=h, c=c: e.dma_start(
                out=gt[cb][:], in_=fmT[1024 + h * 128:1024 + (h + 1) * 128, c * QW:(c + 1) * QW]),
                writes=[r_gt[cb]], queue="pool")
            P.op("pool", lambda e, cb=cb: e.memset(C[cb][:], 0.0), writes=[r_C[cb]])
            nkt = (c + 1) * (QW // 128)
            kts = list(range(nkt - 1, -1, -1))
            bufs = []

            def stage1(i, hb=hb, c=c):
                nonlocal ti
                kt = kts[i]
                b = ti % 2
                ti += 1
                bufs.append(b)
                dj = kt - c * (QW // 128)
                P.op("pe", lambda e, b=b, kt=kt: e.matmul(
                    zps[b][:], lhsT=kT[hb][:, kt * 128:(kt + 1) * 128], rhs=qT[hb][:, c * QW:(c + 1) * QW],
                    start=True, stop=True), reads=[r_k[hb], r_q[hb]], writes=[r_z[b]])
                P.op("act", lambda e, b=b: e.activation(out=ex[b][:], in_=zps[b][:], func=AF.Exp, scale=SCALE),
                     reads=[r_z[b]], writes=[r_ex[b]])
                P.op("act", lambda e, b=b: e.activation(out=sp[b][:], in_=ex[b][:], func=AF.Ln, bias=1.0),
                     reads=[r_ex[b]], writes=[r_sp[b]])
                if dj >= 0:
                    P.op("pool", lambda e, b=b, dj=dj: e.tensor_tensor(
                        out=sp[b][:], in0=sp[b][:], in1=dmask[:, dj * QW:(dj + 1) * QW], op=ALU.mult),
                        reads=[r_sp[b], r_c], writes=[r_sp[b]])
                P.op("pool", lambda e, b=b: e.tensor_copy(out=hi[b][:], in_=sp[b][:]),
                     reads=[r_sp[b]], writes=[r_hl[b]])
                P.op("pool", lambda e, b=b: e.tensor_tensor(out=lo[b][:], in0=sp[b][:], in1=hi[b][:], op=ALU.subtract),
                     reads=[r_sp[b], r_hl[b]], writes=[r_hl[b]])
                P.op("pe", lambda e, b=b: e.matmul(ups[b][:], lhsT=tri[:], rhs=hi[b][:], start=True, stop=False),
                     reads=[r_c, r_hl[b]], writes=[r_u[b]])
                P.op("pe", lambda e, b=b: e.matmul(ups[b][:], lhsT=tri[:], rhs=lo[b][:], start=False, stop=True),
                     reads=[r_c, r_hl[b]], writes=[r_u[b]])
                P.op("pe", lambda e, b=b: e.matmul(wps[b][:], lhsT=ones[:], rhs=hi[b][:], start=True, stop=False),
                     reads=[r_c, r_hl[b]], writes=[r_wp[b]])
                P.op("pe", lambda e, b=b: e.matmul(wps[b][:], lhsT=ones[:], rhs=lo[b][:], start=False, stop=True),
                     reads=[r_c, r_hl[b]], writes=[r_wp[b]])

            def stage2(i, hb=hb, c=c, cb=cb, nkt=nkt):
                kt = kts[i]
                b = bufs[i]
                dj = kt - c * (QW // 128)
                P.op("dve", lambda e, b=b: e.tensor_tensor(out=tt[b][:], in0=ups[b][:], in1=C[cb][:], op=ALU.add),
                     reads=[r_u[b], r_C[cb]], writes=[r_tt[b]])
                P.op("dve", lambda e, b=b: e.scalar_tensor_tensor(
                    out=tt[b][:], in0=zps[b][:], scalar=SCALE, in1=tt[b][:], op0=ALU.mult, op1=ALU.subtract),
                    reads=[r_z[b], r_tt[b]], writes=[r_tt[b]])
                P.op("act", lambda e, b=b: e.activation(out=aa[b][:], in_=tt[b][:], func=AF.Exp),
                     reads=[r_tt[b]], writes=[r_aa[b]])
                if dj >= 0:
                    P.op("pool", lambda e, b=b, dj=dj: e.tensor_tensor(
                        out=aa[b][:], in0=aa[b][:], in1=dmask[:, dj * QW:(dj + 1) * QW], op=ALU.mult),
                        reads=[r_aa[b], r_c], writes=[r_aa[b]])
                if kt > 0:
                    P.op("dve", lambda e, b=b: e.tensor_tensor(
                        out=C[cb][:], in0=wps[b][:], in1=C[cb][:], op=ALU.add),
                        reads=[r_wp[b], r_C[cb]], writes=[r_C[cb]])
                P.op("pe", lambda e, b=b, kt=kt, i=i: e.matmul(
                    ops_[cb][:], lhsT=vv[hb][:, kt, :], rhs=aa[b][:], start=(i == 0), stop=(i == nkt - 1)),
                    reads=[r_v[hb], r_aa[b]], writes=[r_o[cb]])

            for step in range(nkt + 1):
                if step < nkt:
                    stage1(step)
                if step >= 1:
                    stage2(step - 1)
            P.op("dve", lambda e, cb=cb: e.tensor_tensor(out=yb[cb][:], in0=ops_[cb][:], in1=gt[cb][:], op=ALU.mult),
                 reads=[r_o[cb], r_gt[cb]], writes=[r_yb[cb]])
            P.dma(lambda e, cb=cb, h=h, c=c: e.dma_start(out=ywrite(h, c * QW, QW), in_=yb[cb][:]), reads=[r_yb[cb]])


def sb_consts():
    kp = np.arange(128)[:, None]
    k = np.arange(128)[None, :]
    tri = (kp >= k).astype(np.float32)
    kk = np.arange(128)[:, None]
    q = np.arange(512)[None, :]
    dm = np.concatenate([(128 * j + kk < q) for j in range(4)], axis=1).astype(np.float32)
    return {"tri": tri.astype(NPBF), "dmask": dm.astype(NPBF)}


def emit_nsa(P, io, S=SEQ):
    QW = 256
    NCMP = (S - 32) // 16 + 1
    NSLC = S // 64
    NKT = S // 128
    ntl = [(0, min(128, NCMP))] + ([(1, NCMP - 128)] if NCMP > 128 else [])
    (ropeT, vtm, fmT, sgT, wkd, wvd, posd, kcgd, coscd, sincd, identf, ovld, maddd, cmpmd, e64d, selgd, cmaskd,
     wmaskd, ywrite) = (io[k] for k in ("ropeT", "vtm", "fmT", "sgT", "wk", "wv", "posT", "kcg", "cosc", "sinc", "identf",
                                        "ovl", "madd", "cmpm", "e64", "selg", "cmask", "wmask", "ywrite"))
    A = AttnCtx(P, QW, n_s=2)
    A.add_fp32()
    gbc = P.ps([128, 512], F32, "gbc")
    r_gbc = Res()
    misc = P.ps([128, 512], F32, "misc")
    r_misc = Res()
    r_c = Res()
    idf = P.sb([128, 128], F32, "idf")
    ovl = P.sb([128, 2, NSLC], F32, "ovl_s")
    e64 = P.sb([NSLC, NKT * 128], BF16, "e64_s")
    selg = P.sb([12, 12 * 128], F32, "selg_s")
    cmask = P.sb([128, 512], BF16, "cmask_s")
    wmask = P.sb([128, 512], BF16, "wmask_s")
    kcg = P.sb([128, 128], F32, "kcg_s")
    posT = P.sb([128, 32], F32, "posT_s")
    cosc = P.sb([128, 2, 128], F32, "cosc_s")
    sinc = P.sb([128, 2, 128], F32, "sinc_s")
    sg = [P.sb([12, QW], F32, "sg_s%d" % i) for i in range(2)]
    r_sg = [Res() for _ in range(2)]
    for dst, srcd in ((idf[:], identf), (ovl[:], ovld), (e64[:], e64d), (selg[:], selgd), (cmask[:], cmaskd),
                      (wmask[:], wmaskd), (kcg[:], kcgd), (posT[:], posd),
                      (cosc[:], coscd.rearrange("(t p) d -> p t d", p=128)),
                      (sinc[:], sincd.rearrange("(t p) d -> p t d", p=128))):
        P.dma(lambda e, dst=dst, srcd=srcd: e.dma_start(out=dst, in_=srcd), writes=[r_c], queue="pool")
    qT = P.sb([128, 4, S], BF16, "qT")
    ksT = P.sb([128, S], BF16, "ksT")
    kwT = P.sb([128, S], BF16, "kwT")
    vs = P.sb([128, NKT, 128], BF16, "vs")
    vw = P.sb([128, NKT, 128], BF16, "vw")
    r_in = Res()
    for h in range(4):
        P.dma(lambda e, h=h: e.dma_start(out=qT[:, h, :], in_=ropeT[h]), writes=[r_in])
    P.dma(lambda e: e.dma_start(out=ksT[:], in_=ropeT[4]), writes=[r_in])
    P.dma(lambda e: e.dma_start(out=kwT[:], in_=ropeT[5]), writes=[r_in])
    P.dma(lambda e: e.dma_start(out=vs[:], in_=vtm[:, 0:128].rearrange("(t p) d -> p t d", p=128)), writes=[r_in])
    P.dma(lambda e: e.dma_start(out=vw[:], in_=vtm[:, 128:256].rearrange("(t p) d -> p t d", p=128)), writes=[r_in])

    kcT = P.sb([128, S], BF16, "kcT")
    vcT = P.sb([128, S], BF16, "vcT")
    r_kv = Res()
    P.dma(lambda e: e.dma_start(out=kcT[:], in_=fmT[0:128, :]), writes=[r_kv])
    P.dma(lambda e: e.dma_start(out=vcT[:], in_=fmT[128:256, :]), writes=[r_kv])
    wst = [P.sb([128, 8, 128], F32, "wst%d" % i) for i in range(2)]
    r_wst = [Res() for _ in range(2)]
    wkb = P.sb([128, 32, 128], BF16, "wkb")
    wvb = P.sb([128, 32, 128], BF16, "wvb")
    r_wb = Res()
    wi = 0
    for (wd_, wb_) in ((wkd, wkb), (wvd, wvb)):
        for ch in range(4):
            s_ = wi % 2
            wi += 1
            P.dma(lambda e, s_=s_, wd_=wd_, ch=ch: e.dma_start(out=wst[s_][:], in_=wd_[:, ch * 8:(ch + 1) * 8, :]),
                  writes=[r_wst[s_]])
            P.op("dve", lambda e, s_=s_, wb_=wb_, ch=ch: e.tensor_copy(out=wb_[:, ch * 8:(ch + 1) * 8, :], in_=wst[s_][:]),
                 reads=[r_wst[s_]], writes=[r_wb])
    kcp = P.sb([128, 32, 256], BF16, "kcp")
    vcp = kcp
    r_cp = Res()

    def build_cp(src_):
        for l in range(32):
            eng = "dve" if l % 2 == 0 else "pool"
            P.op(eng, lambda e, l=l: e.tensor_scalar(
                out=kcp[:, l, 0:NCMP], in0=src_[:, l:l + 16 * (NCMP - 1) + 1:16], scalar1=posT[:, l:l + 1],
                scalar2=None, op0=ALU.add), reads=[r_kv, r_c], writes=[r_cp])
    kccT = P.sb([128, 256], BF16, "kccT")
    r_kcc = Res()
    vcc = P.sb([128, 2, 128], F32, "vcc")
    r_vcc = Res()
    csq = P.sb([128, 128], F32, "csq")
    cst = P.sb([128, 4], F32, "cst")
    cqn = P.sb([128, 128], F32, "cqn")
    ct1 = P.sb([128, 128], F32, "ct1")
    ct2 = P.sb([128, 128], F32, "ct2")
    ckr = P.sb([128, 128], F32, "ckr")
    r_cw = Res()
    build_cp(kcT)
    for nt, nk in ntl:
        for l in range(32):
            P.op("pe", lambda e, l=l, nt=nt, nk=nk: e.matmul(
                misc[0:nk, 0:128], lhsT=kcp[:, l, nt * 128:nt * 128 + nk], rhs=wkb[:, l, :],
                start=(l == 0), stop=(l == 31)), reads=[r_cp, r_wb], writes=[r_misc])
        P.op("act", lambda e, nk=nk: e.activation(out=csq[0:nk, :], in_=misc[0:nk, 0:128], func=AF.Square,
                                                  accum_out=cst[0:nk, 0:1]), reads=[r_misc], writes=[r_cw])
        P.op("act", lambda e, nk=nk: e.activation(out=cst[0:nk, 1:2], in_=cst[0:nk, 0:1], func=AF.Sqrt, bias=EPS,
                                                  scale=1.0 / 128), reads=[r_cw], writes=[r_cw])
        P.op("dve", lambda e, nk=nk: e.reciprocal(out=cst[0:nk, 2:3], in_=cst[0:nk, 1:2]), reads=[r_cw], writes=[r_cw])
        P.op("dve", lambda e, nk=nk: e.scalar_tensor_tensor(
            out=cqn[0:nk, :], in0=misc[0:nk, 0:128], scalar=cst[0:nk, 2:3], in1=kcg[0:nk, :],
            op0=ALU.mult, op1=ALU.mult), reads=[r_misc, r_cw, r_c], writes=[r_cw])
        P.op("dve", lambda e, nk=nk, nt=nt: e.tensor_tensor(out=ct1[0:nk, :], in0=cqn[0:nk, :], in1=cosc[0:nk, nt, :],
                                                            op=ALU.mult), reads=[r_cw, r_c], writes=[r_cw])
        for hf in range(2):
            P.op("dve", lambda e, nk=nk, nt=nt, hf=hf: e.tensor_tensor(
                out=ct2[0:nk, hf * 64:(hf + 1) * 64], in0=cqn[0:nk, (1 - hf) * 64:(2 - hf) * 64],
                in1=sinc[0:nk, nt, hf * 64:(hf + 1) * 64], op=ALU.mult), reads=[r_cw, r_c], writes=[r_cw])
        P.op("dve", lambda e, nk=nk: e.tensor_tensor(out=ckr[0:nk, :], in0=ct1[0:nk, :], in1=ct2[0:nk, :], op=ALU.add),
             reads=[r_cw], writes=[r_cw])
        P.op("pe", lambda e, nk=nk: e.transpose(out=misc[:, 128:128 + nk], in_=ckr[0:nk, :], identity=idf[0:nk, 0:nk]),
             reads=[r_cw, r_c], writes=[r_misc])
        P.op("act", lambda e, nk=nk, nt=nt: e.copy(out=kccT[:, nt * 128:nt * 128 + nk], in_=misc[:, 128:128 + nk]),
             reads=[r_misc], writes=[r_kcc])
    build_cp(vcT)
    for nt, nk in ntl:
        for l in range(32):
            P.op("pe", lambda e, l=l, nt=nt, nk=nk: e.matmul(
                misc[0:nk, 256:384], lhsT=vcp[:, l, nt * 128:nt * 128 + nk], rhs=wvb[:, l, :],
                start=(l == 0), stop=(l == 31)), reads=[r_cp, r_wb], writes=[r_misc])
        P.op("act", lambda e, nk=nk, nt=nt: e.copy(out=vcc[0:nk, nt, :], in_=misc[0:nk, 256:384]),
             reads=[r_misc], writes=[r_vcc])


    cmT = [P.sb([128, 2, QW], BF16, "cmT%d" % i) for i in range(2)]
    r_cm = [Res() for _ in range(2)]
    madd = [P.sb([128, 2, NSLC], F32, "madd%d" % i) for i in range(2)]
    r_madd = [Res() for _ in range(2)]
    gt = [P.sb([128, 4, QW], BF16, "gt%d" % i) for i in range(2)]
    r_gt = [Res() for _ in range(2)]
    psumT = P.sb([128, 2, QW], F32, "psumT")
    r_psT = Res()
    pn = P.sb([128, QW], F32, "pn")
    r_pn = Res()
    rd = P.sb([128, QW], F32, "rd")
    r_rd = Res()
    tb = P.sb([128, QW], F32, "tb")
    r_tb = Res()
    yacc = [P.sb([128, QW], F32, "yacc%d" % i) for i in range(4)]
    r_ya = [Res() for _ in range(4)]
    impm = P.sb([128, NSLC], F32, "impm")
    imp2 = P.sb([128, NSLC], F32, "imp2")
    t8a = P.sb([128, 8], F32, "t8a")
    t8b = P.sb([128, 8], F32, "t8b")
    bia = P.sb([128, NSLC], F32, "bia")
    r_sel = Res()
    biasT = P.sb([NSLC, QW], BF16, "biasT")
    r_bT = Res()
    yb = [P.sb([128, QW], BF16, "yb%d" % i) for i in range(2)]
    r_yb = [Res() for _ in range(2)]
    ybi = 0

    def finish_branch(hl, br, o_ps, r_o, d_ps, r_d, c, first, clamp=False):
        if clamp:
            P.op("dve", lambda e: e.tensor_scalar(out=rd[:], in0=d_ps[:, 0:QW], scalar1=1e-30, scalar2=None,
                                                  op0=ALU.max), reads=[r_d], writes=[r_rd])
            P.op("dve", lambda e: e.reciprocal(out=rd[:], in_=rd[:]), reads=[r_rd], writes=[r_rd])
        else:
            P.op("dve", lambda e: e.reciprocal(out=rd[:], in_=d_ps[:, 0:QW]), reads=[r_d], writes=[r_rd])
        P.op("dve", lambda e: e.tensor_tensor(out=tb[:], in0=o_ps[:, 0:QW], in1=rd[:], op=ALU.mult),
             reads=[r_o, r_rd], writes=[r_tb])
        row = br * 4 + hl
        P.op("pe", lambda e: e.matmul(gbc[:, 0:QW], lhsT=selg[:, row * 128:(row + 1) * 128],
                                      rhs=sg[c % 2][:], start=True, stop=True),
             reads=[r_c, r_sg[c % 2]], writes=[r_gbc])
        if first:
            P.op("dve", lambda e: e.tensor_tensor(out=yacc[hl][:], in0=gbc[:, 0:QW], in1=tb[:], op=ALU.mult),
                 reads=[r_gbc, r_tb], writes=[r_ya[hl]])
        else:
            P.op("dve", lambda e: e.tensor_tensor(out=tb[:], in0=gbc[:, 0:QW], in1=tb[:], op=ALU.mult),
                 reads=[r_gbc, r_tb], writes=[r_tb])
            P.op("pool", lambda e: e.tensor_tensor(out=yacc[hl][:], in0=yacc[hl][:], in1=tb[:], op=ALU.add),
                 reads=[r_tb, r_ya[hl]], writes=[r_ya[hl]])

    for c in range(S // QW):
        cb = c % 2
        P.dma(lambda e, cb=cb, c=c: e.dma_start(
            out=cmT[cb][:], in_=cmpmd[:, c * QW:(c + 1) * QW].rearrange("(t p) q -> p t q", p=128)),
            writes=[r_cm[cb]], queue="pool")
        P.dma(lambda e, cb=cb, c=c: e.dma_start(out=sg[cb][:], in_=sgT[:, c * QW:(c + 1) * QW]),
              writes=[r_sg[cb]], queue="pool")
        P.dma(lambda e, cb=cb, c=c: e.dma_start(
            out=madd[cb][:], in_=maddd[c * QW:(c + 1) * QW, :].rearrange("(t p) j -> p t j", p=128)),
            writes=[r_madd[cb]], queue="pool")
        P.dma(lambda e, cb=cb, c=c: e.dma_start(
            out=gt[cb][:], in_=fmT[256:768, c * QW:(c + 1) * QW].rearrange("(h p) q -> p h q", p=128)),
            writes=[r_gt[cb]], queue="pool")
        for hl in range(4):
            ktiles = []
            for nt, nk in ntl:
                ktiles.append(dict(kT=kccT[:, nt * 128:nt * 128 + nk], r_k=r_kcc, v=vcc[0:nk, nt, :], r_v=r_vcc, nk=nk,
                                   mask=(cmT[cb][0:nk, nt, :], r_cm[cb])))
            o_ps, r_o, d_ps, r_d, used, _ = softmax_branch(P, A, qT[:, hl, c * QW:(c + 1) * QW], r_in, ktiles, fp32=True)
            finish_branch(hl, 0, o_ps, r_o, d_ps, r_d, c, first=True, clamp=True)
            for nt, nk in ntl:
                pb = used[nt]
                if hl == 0:
                    P.op("pool", lambda e, nt=nt, nk=nk, pb=pb: e.tensor_tensor(
                        out=psumT[0:nk, nt, :], in0=A.pt32[pb][0:nk, :], in1=rd[0:nk, :], op=ALU.mult),
                        reads=[A.r_pt32[pb], r_rd], writes=[r_psT])
                else:
                    P.op("pool", lambda e, nk=nk, pb=pb: e.tensor_tensor(
                        out=pn[0:nk, :], in0=A.pt32[pb][0:nk, :], in1=rd[0:nk, :], op=ALU.mult),
                        reads=[A.r_pt32[pb], r_rd], writes=[r_pn])
                    P.op("pool", lambda e, nt=nt, nk=nk: e.tensor_tensor(
                        out=psumT[0:nk, nt, :], in0=psumT[0:nk, nt, :], in1=pn[0:nk, :], op=ALU.add),
                        reads=[r_pn, r_psT], writes=[r_psT])
        for qs in range(QW // 128):
            for nt, nk in ntl:
                P.op("pe", lambda e, qs=qs, nt=nt, nk=nk: e.matmul(
                    misc[:, 0:NSLC], lhsT=psumT[0:nk, nt, qs * 128:(qs + 1) * 128], rhs=ovl[0:nk, nt, :],
                    start=(nt == 0), stop=(nt == len(ntl) - 1)), reads=[r_psT, r_c], writes=[r_misc])
            P.op("dve", lambda e, qs=qs, cb=cb: e.tensor_tensor(out=impm[:], in0=misc[:, 0:NSLC], in1=madd[cb][:, qs, :],
                                                                op=ALU.add), reads=[r_misc, r_madd[cb]], writes=[r_sel])
            if NSLC > 16:
                P.op("dve", lambda e: e.max(out=t8a[:], in_=impm[:]), reads=[r_sel], writes=[r_sel])
                P.op("dve", lambda e: e.match_replace(out=imp2[:], in_to_replace=t8a[:], in_values=impm[:],
                                                      imm_value=-3.0e38), reads=[r_sel], writes=[r_sel])
                P.op("dve", lambda e: e.max(out=t8b[:], in_=imp2[:]), reads=[r_sel], writes=[r_sel])
                P.op("dve", lambda e: e.tensor_scalar(out=bia[:], in0=impm[:], scalar1=t8b[:, 7:8], scalar2=-30000.0,
                                                      op0=ALU.is_lt, op1=ALU.mult), reads=[r_sel], writes=[r_sel])
            else:
                P.op("dve", lambda e: e.memset(bia[:], 0.0), writes=[r_sel])
            P.op("pe", lambda e, qs=qs: e.transpose(out=misc[0:NSLC, 128 + qs * 128:256 + qs * 128], in_=bia[:],
                                                    identity=idf[:]), reads=[r_sel, r_c], writes=[r_misc])
        P.op("act", lambda e: e.copy(out=biasT[:], in_=misc[0:NSLC, 128:128 + QW]), reads=[r_misc], writes=[r_bT])
        for hl in range(4):
            ktiles = []
            for kt in range(2 * c + 2):
                d = dict(kT=ksT[:, kt * 128:(kt + 1) * 128], r_k=r_in, v=vs[:, kt, :], r_v=r_in, nk=128,
                         bias=(e64[:, kt * 128:(kt + 1) * 128], biasT[:], [r_c, r_bT]))
                if kt >= 2 * c:
                    d["mask"] = (cmask[:, (kt - 2 * c) * 256:(kt - 2 * c + 1) * 256], r_c)
                ktiles.append(d)
            o_ps, r_o, d_ps, r_d, _, _ = softmax_branch(P, A, qT[:, hl, c * QW:(c + 1) * QW], r_in, ktiles)
            finish_branch(hl, 1, o_ps, r_o, d_ps, r_d, c, first=False)
            ktiles = []
            for r in range(6):
                kt = 2 * c - 4 + r
                if kt < 0:
                    continue
                d = dict(kT=kwT[:, kt * 128:(kt + 1) * 128], r_k=r_in, v=vw[:, kt, :], r_v=r_in, nk=128)
                if r in (0, 1):
                    d["mask"] = (wmask[:, r * 256:(r + 1) * 256], r_c)
                elif r in (4, 5):
                    d["mask"] = (cmask[:, (r - 4) * 256:(r - 3) * 256], r_c)
                ktiles.append(d)
            o_ps, r_o, d_ps, r_d, _, _ = softmax_branch(P, A, qT[:, hl, c * QW:(c + 1) * QW], r_in, ktiles)
            finish_branch(hl, 2, o_ps, r_o, d_ps, r_d, c, first=False)
            ob = ybi % 2
            ybi += 1
            P.op("pool", lambda e, hl=hl, ob=ob, cb=cb: e.tensor_tensor(
                out=yb[ob][:], in0=yacc[hl][:], in1=gt[cb][:, hl, :], op=ALU.mult),
                reads=[r_ya[hl], r_gt[cb]], writes=[r_yb[ob]])
            P.dma(lambda e, hl=hl, ob=ob, c=c: e.dma_start(out=ywrite(hl, c * QW, QW), in_=yb[ob][:]), reads=[r_yb[ob]])


def nsa_consts(S=SEQ):
    NCMP = (S - 32) // 16 + 1
    NSLC = S // 64
    NKT = S // 128
    half = 64
    inv_freq = np.exp(-math.log(10000.0) * np.arange(half, dtype=np.float32) / half).astype(np.float32)
    pos_c = (np.arange(256) * 16 + 31).astype(np.float32)
    ang = pos_c[:, None] * inv_freq[None, :]
    c = np.cos(ang).astype(np.float32)
    s = np.sin(ang).astype(np.float32)
    cosc = np.concatenate([c, c], 1)
    sinc = np.concatenate([-s, s], 1)
    n = np.arange(256)
    j = np.arange(NSLC)
    ov = ((n[:, None] * 16 < j[None, :] * 64 + 64) & (n[:, None] * 16 + 32 > j[None, :] * 64)).astype(np.float32)
    ov[NCMP:] = 0
    ovl = np.ascontiguousarray(ov.reshape(2, 128, NSLC).transpose(1, 0, 2))
    t = np.arange(S)
    cur = t // 64
    forced = (j[None, :] == 0) | (j[None, :] == cur[:, None]) | (j[None, :] == cur[:, None] - 1)
    madd = np.where(forced, 1e30, 0.0).astype(np.float32)
    madd = np.where(j[None, :] <= cur[:, None], madd, -1e30).astype(np.float32)
    cmpm = ((n[:, None] * 16 + 31) <= t[None, :]).astype(np.float32)
    cmpm[NCMP:] = 0
    e64 = np.zeros((NSLC, NKT * 128), np.float32)
    for kt in range(NKT):
        e64[2 * kt, kt * 128:kt * 128 + 64] = 1
        e64[2 * kt + 1, kt * 128 + 64:kt * 128 + 128] = 1
    selg = np.zeros((12, 12 * 128), np.float32)
    for r in range(12):
        selg[r, r * 128:(r + 1) * 128] = 1
    k = np.arange(128)[:, None]
    q = np.arange(256)[None, :]
    cm = np.concatenate([(k <= q), (k + 128 <= q)], axis=1).astype(np.float32)
    wm = np.concatenate([((q - k + 512 - 128 * r >= 0) & (q - k + 512 - 128 * r < 512)) for r in (0, 1)], axis=1)
    return {"cosc": cosc, "sinc": sinc, "identf": np.eye(128, dtype=np.float32), "ovl": ovl, "madd": madd,
            "cmpm": cmpm.astype(NPBF), "e64": e64.astype(NPBF), "selg": selg, "cmask": cm.astype(NPBF),
            "wmask": wm.astype(np.float32).astype(NPBF)}


LAYER_KEYS = [
    dict(norm="l0_norm", w_in="l0_w_in", q_norm="l0_q_norm", k_norm="l0_k_norm", w_out="l0_w_out"),
    dict(norm="l1_norm", w_in="l1_w_in", w_out="l1_w_out"),
    dict(norm="l2_norm", w_in="l2_w_in", q_norm="l2_q_norm", kc_norm="l2_kc_norm", ks_norm="l2_ks_norm",
         kw_norm="l2_kw_norm", cmp_wk="l2_cmp_wk", cmp_wv="l2_cmp_wv", cmp_pos="l2_cmp_pos", w_out="l2_w_out"),
    dict(norm="l3_norm", w_in="l3_w_in", q_norm="l3_q_norm", k_norm="l3_k_norm", w_out="l3_w_out"),
]
CFGS = [
    ([(4, 0), (4, 0), (0, 512)], ["silu"] * 4, 0),
    ([(0, 512)], ["copy"] * 8 + ["silu"] * 4, 0),
    ([(4, 0), (2, 256)], ["copy", "copy"] + ["silu"] * 4, 12),
]
RG = [[0, 1, 2, 3], [4, 5, 6, 7]]
N_LAYERS = 4


def _cfg_dims(cfg):
    tm, fm, nsg = cfg
    n_tm = sum(a * 128 + b for a, b in tm)
    NR = sum(a for a, b in tm)
    NV = sum(b for a, b in tm)
    NF = 128 * len(fm)
    return n_tm, NR, NV, NF, n_tm + NF + nsg


def build_fused(n_layers=N_LAYERS):
    nc = bass.Bass("TRN2", target_bir_lowering=False)
    S = SEQ

    def din(name, shape, dt):
        return nc.dram_tensor(name, list(shape), dt, kind="ExternalInput").ap()

    def dint(name, shape, dt):
        return nc.dram_tensor(name, list(shape), dt).ap()

    xfull = din("xfull", [S, 2048], F32)
    xq0 = din("xq0", [1024, 2048], F32)
    sel = din("sel", [128, 4], F32)
    out = nc.dram_tensor("out", [1024, 2048], F32, kind="ExternalOutput").ap()
    cst = dict(cos=din("cos", [S, 512], F32), sin=din("sin", [S, 512], F32),
               identf=din("identf", [128, 128], F32), identb=din("identb", [128, 128], BF16),
               esel=din("esel", [16, 2048], BF16), cmask=din("cmask", [128, 512], BF16),
               tri=din("tri", [128, 128], BF16), dmask=din("dmask", [128, 2048], BF16),
               cosc=din("cosc", [256, 128], F32), sinc=din("sinc", [256, 128], F32),
               ovl=din("ovl", [128, 2, 64], F32), madd=din("madd", [S, 64], F32), cmpm=din("cmpm", [256, S], BF16),
               e64=din("e64", [64, S], BF16), selg=din("selg", [12, 12 * 128], F32), wmask=din("wmask", [128, 512], BF16))
    P = Prog(nc)
    xg = None
    xqc_prev = None
    for li in range(n_layers):
        kind = li % 3
        cfg = CFGS[kind]
        n_tm, NR, NV, NF, ncols = _cfg_dims(cfg)
        pre = "L%d_" % li
        w = din(pre + "w", [2048, ncols], F32)
        gx = din(pre + "gx", [128, 16], F32)
        gains = din(pre + "gains", [128, n_tm], F32)
        w_out = din(pre + "wout", [2048, 2048], F32)
        ropeT = dint(pre + "ropeT", [max(NR, 1), 128, S], BF16)
        vtm = dint(pre + "vtm", [S, NV], BF16)
        fmT = dint(pre + "fmT", [NF, S], BF16)
        sgT = dint(pre + "sgT", [12, S], F32)
        yTc = [dint(pre + "yTc%d" % j, [512, 1024], BF16) for j in range(4)]
        yG = [dint(pre + "yG%d" % j, [2048, 1024], BF16) for j in range(4)]
        if li == 0:
            x_tile = lambda T: xfull[T * 128:(T + 1) * 128, :]
        else:
            x_tile = lambda T, xg=xg: xg[T % 8][(T // 8) * 128:(T // 8 + 1) * 128, :]
        io = dict(x_tile=x_tile, w=w, gx=gx, gains=gains, cos=cst["cos"], sin=cst["sin"], identf=cst["identf"],
                  identb=cst["identb"], ropeT=ropeT, vtm=vtm, fmT=fmT, sgT=sgT)
        emit_proj(P, cfg[0], cfg[1], cfg[2], io, S)
        P.end_phase()
        ywrite = lambda h, c0, n, yTc=yTc: yTc[c0 // 1024][h * 128:(h + 1) * 128, (c0 % 1024):(c0 % 1024) + n]
        if kind == 0:
            emit_moba(P, dict(ropeT=ropeT, vtm=vtm, gT=fmT, identf=cst["identf"], esel=cst["esel"],
                              cmask=cst["cmask"], ywrite=ywrite), S)
        elif kind == 1:
            emit_sb(P, dict(fmT=fmT, vtm=vtm, tri=cst["tri"], dmask=cst["dmask"], ywrite=ywrite), S)
        else:
            io = dict(ropeT=ropeT, vtm=vtm, fmT=fmT, sgT=sgT, wk=din(pre + "wk", [128, 32, 128], F32),
                      wv=din(pre + "wv", [128, 32, 128], F32), posT=din(pre + "posT", [128, 32], F32),
                      kcg=din(pre + "kcg", [128, 128], F32), ywrite=ywrite)
            for k in ("cosc", "sinc", "identf", "ovl", "madd", "cmpm", "e64", "selg", "cmask", "wmask"):
                io[k] = cst[k]
            emit_nsa(P, io, S)
        P.end_phase()
        for j in range(4):
            P.cc(lambda e, j=j, yTc=yTc, yG=yG: e.collective_compute(
                "AllGather", ALU.bypass, replica_groups=RG, ins=[yTc[j].opt()], outs=[yG[j].opt()]))
        P.end_phase()
        if li == 0:
            x_tile_c = lambda tt: xq0[tt * 128:(tt + 1) * 128, :]
        else:
            x_tile_c = lambda tt, xqc_prev=xqc_prev: xqc_prev[tt]
        if li < n_layers - 1:
            xqc = [dint(pre + "xqc%d" % i, [128, 2048], F32) for i in range(8)]
            out_tile = lambda tt, xqc=xqc: xqc[tt]
        else:
            xqc = None
            out_tile = lambda tt: out[tt * 128:(tt + 1) * 128, :]
        emit_outproj(P, 1024, w_out, x_tile_c, out_tile, yG=yG, sel=sel)
        P.end_phase()
        if li < n_layers - 1:
            xg = [dint(pre + "xg%d" % i, [512, 2048], F32) for i in range(8)]
            for i in range(8):
                P.cc(lambda e, i=i, xqc=xqc, xg=xg: e.collective_compute(
                    "AllGather", ALU.bypass, replica_groups=RG, ins=[xqc[i].opt()], outs=[xg[i].opt()]))
            P.end_phase()
            xqc_prev = xqc
    P.finish()
    return nc


_PROGS = {}


def _rep(v):
    return np.ascontiguousarray(np.tile(np.asarray(v, np.float32)[None, :], (128, 1)))


def _layer_inputs(li, LK, core):
    kind = li % 3
    b, hg = divmod(core, 4)
    w_in = LK["w_in"]
    W = 2048
    sl = slice(hg * 512, (hg + 1) * 512)
    pre = "L%d_" % li
    m = {}
    if kind == 0:
        cols = [w_in[:, 0:W][:, sl], w_in[:, W:2 * W][:, sl], w_in[:, 2 * W:3 * W][:, sl], w_in[:, 3 * W:4 * W][:, sl]]
        gains = np.concatenate([np.tile(LK["q_norm"], 4), np.tile(LK["k_norm"], 4), np.ones(512, np.float32)])
    elif kind == 1:
        cols = [w_in[:, 2 * W:3 * W][:, sl], w_in[:, 0:W][:, sl], w_in[:, W:2 * W][:, sl], w_in[:, 3 * W:4 * W][:, sl]]
        gains = np.ones(512, np.float32)
    else:
        g = hg
        h1 = slice(g * 128, (g + 1) * 128)
        bg = [7168 + br * 16 + 4 * g + hl for br in range(3) for hl in range(4)]
        cols = [w_in[:, 0:2048][:, sl], w_in[:, 3072:3584][:, h1], w_in[:, 4096:4608][:, h1],
                w_in[:, 3584:4096][:, h1], w_in[:, 4608:5120][:, h1],
                w_in[:, 2048:2560][:, h1], w_in[:, 2560:3072][:, h1], w_in[:, 5120:7168][:, sl], w_in[:, bg]]
        gains = np.concatenate([np.tile(LK["q_norm"], 4), LK["ks_norm"], LK["kw_norm"], np.ones(256, np.float32)])
        m[pre + "wk"] = np.ascontiguousarray(LK["cmp_wk"].transpose(1, 0, 2))
        m[pre + "wv"] = np.ascontiguousarray(LK["cmp_wv"].transpose(1, 0, 2))
        m[pre + "posT"] = np.ascontiguousarray(LK["cmp_pos"].T)
        m[pre + "kcg"] = _rep(LK["kc_norm"])
    m[pre + "w"] = np.ascontiguousarray(np.concatenate(cols, axis=1))
    m[pre + "gx"] = np.ascontiguousarray(LK["norm"].reshape(16, 128).T)
    m[pre + "gains"] = _rep(gains)
    m[pre + "wout"] = LK["w_out"]
    return m


def kernel(**inputs):
    x = np.asarray(inputs["x"], np.float32)
    cos, sin = rope_tables(SEQ)
    identf = np.eye(128, dtype=np.float32)
    cst = {"cos": cos, "sin": sin, "identf": identf, "identb": identf.astype(NPBF)}
    cst.update(moba_consts())
    cst.update(sb_consts())
    cst.update(nsa_consts())
    LKs = [{k: np.asarray(inputs[v], np.float32) for k, v in LAYER_KEYS[li].items()} for li in range(N_LAYERS)]
    maps = []
    for core in range(NCORES):
        b, sq = divmod(core, 4)
        selv = np.zeros((128, 4), np.float32)
        selv[:, sq] = 1.0
        m = {"xfull": np.ascontiguousarray(x[b]), "xq0": np.ascontiguousarray(x[b, sq * 1024:(sq + 1) * 1024]),
             "sel": selv}
        m.update(cst)
        for li in range(N_LAYERS):
            m.update(_layer_inputs(li, LKs[li], core))
        maps.append(m)
    if "fused" not in _PROGS:
        _PROGS["fused"] = build_fused()
    res = run_bass_kernel_spmd(_PROGS["fused"], maps, core_ids=list(range(NCORES)))
    r = res.results
    out = np.stack([np.concatenate([r[b * 4 + sq]["out"] for sq in range(4)], axis=0) for b in range(BATCH)], axis=0)
    return out.astype(np.float32)
```

```python
import math
from contextlib import ExitStack

import numpy as np
import ml_dtypes

import concourse.bass as bass
import concourse.mybir as mybir
from concourse.bass_utils import run_bass_kernel_spmd

F32 = mybir.dt.float32
BF16 = mybir.dt.bfloat16
AF = mybir.ActivationFunctionType
ALU = mybir.AluOpType
AX = mybir.AxisListType
NPBF = ml_dtypes.bfloat16

D_MODEL = 2048
BATCH = 2
SEQ = 4096
HD = 128
NH = 16
EPS = 1e-6
SCALE = HD ** -0.5
NCORES = 8


class Res:
    __slots__ = ("w", "r")

    def __init__(self):
        self.w = None
        self.r = {}


class Prog:
    COMPUTE = ("pe", "act", "dve", "pool")
    NDMA_SEMS = 8

    def __init__(self, nc):
        self.nc = nc
        self.es = ExitStack()
        self.sems = {}
        self.cnt = {}
        self.q = {e: [] for e in ("pe", "act", "dve", "pool", "sp")}
        self.known = {e: {} for e in self.q}
        for e in self.COMPUTE:
            self.sems[e] = nc.alloc_semaphore(name="s_" + e)
            self.cnt[e] = 0
        self.dma_rr = {"sp": 0, "pool": 0}
        for qn in ("sp", "pool"):
            for i in range(self.NDMA_SEMS):
                k = "d_%s%d" % (qn, i)
                self.sems[k] = nc.alloc_semaphore(name=k)
                self.cnt[k] = 0
        self.n_sb = 0
        self.n_ps = 0
        self.phase = 0
        self.sems["cc"] = nc.alloc_semaphore(name="s_cc")
        self.cnt["cc"] = 0

    def sb(self, shape, dt, name=None):
        self.n_sb += 1
        return self.es.enter_context(self.nc.sbuf_tensor("sb%d_" % self.phase + (name or ("t%d" % self.n_sb)), list(shape), dt))

    def ps(self, shape, dt, name=None):
        self.n_ps += 1
        return self.es.enter_context(self.nc.psum_tensor("ps%d_" % self.phase + (name or ("t%d" % self.n_ps)), list(shape), dt))

    def _deps(self, eng, reads, writes):
        deps = {}

        def add(k, v):
            if v > deps.get(k, 0):
                deps[k] = v

        for r in reads:
            if r.w is not None:
                add(*r.w)
        for w in writes:
            if w.w is not None:
                add(*w.w)
            for k, v in w.r.items():
                add(k, v)
        waits = []
        kn = self.known[eng]
        for k, v in deps.items():
            if eng == "pe" and k == "pe":
                continue
            if kn.get(k, 0) >= v:
                continue
            kn[k] = v
            waits.append((k, v))
        return waits

    def op(self, eng, fn, reads=(), writes=()):
        waits = self._deps(eng, reads, writes)
        self.cnt[eng] += 1
        n = self.cnt[eng]
        for r in reads:
            r.r[eng] = n
        for w in writes:
            w.w = (eng, n)
            w.r = {}
        self.q[eng].append((waits, fn, eng, 1))

    def dma(self, fn, reads=(), writes=(), queue="sp"):
        i = self.dma_rr[queue]
        self.dma_rr[queue] = (i + 1) % self.NDMA_SEMS
        k = "d_%s%d" % (queue, i)
        waits = self._deps(queue, reads, writes)
        kn = self.known[queue]
        prev = self.cnt[k]
        if prev > 0 and kn.get(k, 0) < prev:
            kn[k] = prev
            waits.append((k, prev))
        self.cnt[k] += 16
        n = self.cnt[k]
        for r in reads:
            r.r[k] = n
        for w in writes:
            w.w = (k, n)
            w.r = {}
        self.q[queue].append((waits, fn, k, 16))

    def cc(self, fn, reads=(), writes=()):
        waits = self._deps("pool", reads, writes)
        kn = self.known["pool"]
        prev = self.cnt["cc"]
        if prev > 0 and kn.get("cc", 0) < prev:
            kn["cc"] = prev
            waits.append(("cc", prev))
        self.cnt["cc"] += 1
        n = self.cnt["cc"]
        for r in reads:
            r.r["cc"] = n
        for w in writes:
            w.w = ("cc", n)
            w.r = {}
        self.q["pool"].append((waits, fn, "cc", 1))

    def barrier(self):
        allw = [(k, v) for k, v in self.cnt.items() if v > 0]
        for eng in self.q:
            kn = self.known[eng]
            waits = []
            for k, v in allw:
                if eng == "pe" and k == "pe":
                    continue
                if kn.get(k, 0) < v:
                    kn[k] = v
                    waits.append((k, v))
            if waits:
                self.q[eng].append((waits, None, None, 0))

    def flush(self):
        nc = self.nc
        sems = self.sems
        q = self.q

        def run(eng, lst):
            for waits, fn, k, inc in lst:
                for (wk, wv) in waits:
                    eng.wait_ge(sems[wk], wv)
                if fn is not None:
                    ins = fn(eng)
                    ins.then_inc(sems[k], inc)

        with nc.Block() as block:
            @block.tensor
            def _(e):
                run(e, q["pe"])

            @block.scalar
            def _(e):
                run(e, q["act"])

            @block.vector
            def _(e):
                run(e, q["dve"])

            @block.gpsimd
            def _(e):
                run(e, q["pool"])

            @block.sync
            def _(e):
                run(e, q["sp"])
        self.q = {e: [] for e in q}

    def end_phase(self):
        self.barrier()
        self.flush()
        self.es.close()
        self.es = ExitStack()
        self.phase += 1

    def finish(self):
        self.end_phase()
        nc = self.nc
        nc.all_engine_barrier()
        nc.clear_and_free_semaphores(list(self.sems.values()))
        nc.all_engine_barrier()


def emit_outproj(P, ntok, w, x_tile, out_tile, yT=None, yG=None, sel=None, pre_hook=None, r_yG=None):
    wbf = P.sb([128, 16, 2048], BF16, "wbf")
    r_w = [Res() for _ in range(16)]
    stg = [P.sb([128, 2048], F32, "stg%d" % i) for i in range(2)]
    r_stg = [Res() for _ in range(2)]
    ysb = P.sb([128, 16, ntok], BF16, "ysb")
    if yG is None:
        r_y = [Res()] * 16
        P.dma(lambda e: e.dma_start(out=ysb[:], in_=yT.rearrange("(c p) t -> p c t", p=128)),
              writes=[r_y[0]], queue="pool")
    else:
        r_y = [Res() for _ in range(16)]
        sels = P.sb([128, 4], F32, "sels")
        r_sel = Res()
        P.dma(lambda e: e.dma_start(out=sels[:], in_=sel), writes=[r_sel], queue="pool")
        if pre_hook is not None:
            pre_hook()
        r_yGl = [r_yG] if r_yG is not None else []
        cand = [P.sb([128, 4, ntok], BF16, "cand%d" % i) for i in range(2)]
        r_cand = [Res() for _ in range(2)]
        for kc in range(16):
            cb = kc % 2
            for j in range(4):
                P.dma(lambda e, kc=kc, cb=cb, j=j: e.dma_start(out=cand[cb][:, j, :], in_=yG[j][kc * 128:(kc + 1) * 128, :]),
                      reads=r_yGl, writes=[r_cand[cb]], queue="pool")
            eng = "dve"
            P.op(eng, lambda e, kc=kc, cb=cb: e.tensor_scalar(
                out=ysb[:, kc, :], in0=cand[cb][:, 0, :], scalar1=sels[:, 0:1], scalar2=None, op0=ALU.mult),
                reads=[r_cand[cb], r_sel], writes=[r_y[kc]])
            for j in range(1, 4):
                P.op(eng, lambda e, kc=kc, cb=cb, j=j: e.scalar_tensor_tensor(
                    out=ysb[:, kc, :], in0=cand[cb][:, j, :], scalar=sels[:, j:j + 1], in1=ysb[:, kc, :],
                    op0=ALU.mult, op1=ALU.add), reads=[r_cand[cb], r_sel, r_y[kc]], writes=[r_y[kc]])
    for kc in range(16):
        s = kc % 2
        P.dma(lambda e, kc=kc, s=s: e.dma_start(out=stg[s][:], in_=w[kc * 128:(kc + 1) * 128, :]),
              writes=[r_stg[s]])
        eng = "dve" if kc % 2 == 0 else "act"
        if eng == "dve":
            P.op("dve", lambda e, kc=kc, s=s: e.tensor_copy(out=wbf[:, kc, :], in_=stg[s][:]),
                 reads=[r_stg[s]], writes=[r_w[kc]])
        else:
            P.op("act", lambda e, kc=kc, s=s: e.copy(out=wbf[:, kc, :], in_=stg[s][:]),
                 reads=[r_stg[s]], writes=[r_w[kc]])
    pss = [P.ps([128, 512], F32, "pso%d" % i) for i in range(4)]
    r_ps = [Res() for _ in range(4)]
    xt = [P.sb([128, 2048], F32, "xt%d" % i) for i in range(2)]
    r_xt = [Res() for _ in range(2)]
    ot = [P.sb([128, 2048], F32, "ot%d" % i) for i in range(2)]
    r_ot = [Res() for _ in range(2)]
    pi = 0
    for tt in range(ntok // 128):
        s = tt % 2
        P.dma(lambda e, tt=tt, s=s: e.dma_start(out=xt[s][:], in_=x_tile(tt)),
              writes=[r_xt[s]], queue="pool")
        for ct in range(4):
            b = pi % 4
            pi += 1
            for kc in range(16):
                P.op("pe", lambda e, kc=kc, tt=tt, ct=ct, b=b: e.matmul(
                    pss[b][:], lhsT=ysb[:, kc, tt * 128:(tt + 1) * 128],
                    rhs=wbf[:, kc, ct * 512:(ct + 1) * 512], start=(kc == 0), stop=(kc == 15)),
                    reads=[r_y[kc], r_w[kc]], writes=[r_ps[b]])
            P.op("dve", lambda e, s=s, ct=ct, b=b: e.tensor_tensor(
                out=ot[s][:, ct * 512:(ct + 1) * 512], in0=pss[b][:], in1=xt[s][:, ct * 512:(ct + 1) * 512],
                op=ALU.add), reads=[r_ps[b], r_xt[s]], writes=[r_ot[s]])
        P.dma(lambda e, tt=tt, s=s: e.dma_start(out=out_tile(tt), in_=ot[s][:]), reads=[r_ot[s]])


def build_outproj(ntok=1024):
    nc = bass.Bass("TRN2", target_bir_lowering=False)
    yT = nc.dram_tensor("yT", [2048, ntok], BF16, kind="ExternalInput").ap()
    x = nc.dram_tensor("x", [ntok, 2048], F32, kind="ExternalInput").ap()
    w = nc.dram_tensor("w", [2048, 2048], F32, kind="ExternalInput").ap()
    out = nc.dram_tensor("out", [ntok, 2048], F32, kind="ExternalOutput").ap()
    P = Prog(nc)
    emit_outproj(P, ntok, w, lambda tt: x[tt * 128:(tt + 1) * 128, :], lambda tt: out[tt * 128:(tt + 1) * 128, :], yT=yT)
    P.finish()
    return nc


def emit_proj(P, tm_blocks, fm_funcs, n_sg, io, S=SEQ):
    n_tm = sum(a * 128 + b for a, b in tm_blocks)
    NR = sum(a for a, b in tm_blocks)
    NV = sum(b for a, b in tm_blocks)
    NF = 128 * len(fm_funcs)
    ncols = n_tm + NF + n_sg
    x_tile = io["x_tile"]
    w, gx, gains, cosd, sind, identf, identb = (io[k] for k in ("w", "gx", "gains", "cos", "sin", "identf", "identb"))
    ropeT, vtm, fmT, sgT = io.get("ropeT"), io.get("vtm"), io.get("fmT"), io.get("sgT")
    idf = P.sb([128, 128], F32, "idf")
    idb = P.sb([128, 128], BF16, "idb")
    gxs = P.sb([128, 16], F32, "gxs")
    gns = P.sb([128, max(n_tm, 1)], F32, "gns")
    r_c = Res()
    P.dma(lambda e: e.dma_start(out=idf[:], in_=identf), writes=[r_c], queue="pool")
    P.dma(lambda e: e.dma_start(out=idb[:], in_=identb), writes=[r_c], queue="pool")
    P.dma(lambda e: e.dma_start(out=gxs[:], in_=gx), writes=[r_c], queue="pool")
    P.dma(lambda e: e.dma_start(out=gns[:], in_=gains), writes=[r_c], queue="pool")
    if io.get("pre_hook") is not None:
        io["pre_hook"]()
    r_xl = [io["r_x"]] if io.get("r_x") is not None else []
    wbf = P.sb([128, 16, ncols], BF16, "wbf")
    r_w = [Res() for _ in range(16)]
    stg = [P.sb([128, ncols], F32, "stg%d" % i) for i in range(2)]
    r_stg = [Res() for _ in range(2)]
    for kc in range(16):
        s = kc % 2
        P.dma(lambda e, kc=kc, s=s: e.dma_start(out=stg[s][:], in_=w[kc * 128:(kc + 1) * 128, :]),
              writes=[r_stg[s]])
        P.op("dve", lambda e, kc=kc, s=s: e.tensor_scalar(
            out=wbf[:, kc, :], in0=stg[s][:], scalar1=gxs[:, kc:kc + 1], scalar2=None, op0=ALU.mult),
            reads=[r_stg[s], r_c], writes=[r_w[kc]])

    xt = [P.sb([128, 2048], F32, "xt%d" % i) for i in range(2)]
    r_xt = [Res() for _ in range(2)]
    junk = P.sb([128, 2048], BF16, "junk")
    r_junk = Res()
    st1 = [P.sb([128, 4], F32, "st1_%d" % i) for i in range(2)]
    r_st1 = [Res() for _ in range(2)]
    xn = [P.sb([128, 2048], BF16, "xn%d" % i) for i in range(2)]
    r_xn = [Res() for _ in range(2)]
    xnT = [P.sb([128, 16, 512], BF16, "xnT%d" % i) for i in range(2)]
    r_xnT = [Res() for _ in range(2)]
    pT = [P.ps([128, 1024], BF16, "pT%d" % i) for i in range(2)]
    r_pT = [Res() for _ in range(2)]
    pacc = [P.ps([128, 512], F32, "pacc%d" % i) for i in range(4)]
    r_pacc = [Res() for _ in range(4)]
    ptr = [P.ps([128, 512], F32, "ptr%d" % i) for i in range(2)]
    r_ptr = [Res() for _ in range(2)]
    cs = [P.sb([128, 512], F32, "cs%d" % i) for i in range(2)]
    sn = [P.sb([128, 512], F32, "sn%d" % i) for i in range(2)]
    r_cs = [Res() for _ in range(2)]
    sq = P.sb([128, 512], F32, "sq")
    r_sq = Res()
    hst = P.sb([128, 8], F32, "hst")
    r_hst = Res()
    qn = P.sb([128, 512], F32, "qn")
    r_qn = Res()
    t1 = P.sb([128, 512], F32, "t1")
    t2 = P.sb([128, 512], F32, "t2")
    r_t = Res()
    nblk = len(tm_blocks)
    qr = [P.sb([128, 4, 512], F32, "qr%d" % i) for i in range(max(nblk, 1))]
    r_qr = [[Res() for _ in range(4)] for _ in range(max(nblk, 1))]
    vsb = [P.sb([128, 512], BF16, "vsb%d" % i) for i in range(2)]
    r_vsb = [Res() for _ in range(2)]
    qTs = [P.sb([128, 512], BF16, "qTs%d" % i) for i in range(2)]
    r_qTs = [Res() for _ in range(2)]
    fms = [P.sb([128, 512], BF16, "fms%d" % i) for i in range(2)]
    r_fms = [Res() for _ in range(2)]
    sgs = P.sb([128, 512], F32, "sgs")
    r_sgs = Res()

    cnt = {"pacc": 0, "ptr": 0, "vsb": 0, "qTs": 0, "fms": 0, "pT": 0, "x": 0}
    NG = S // 512
    def prep(G):
        gb = G % 2
        for st in range(4):
            tok0 = G * 512 + st * 128
            xb = cnt["x"] % 2
            cnt["x"] += 1
            P.dma(lambda e, xb=xb, tok0=tok0: e.dma_start(out=xt[xb][:], in_=x_tile(tok0 // 128)),
                  reads=r_xl, writes=[r_xt[xb]], queue="pool")
            P.op("act", lambda e, xb=xb: e.activation(
                out=junk[:], in_=xt[xb][:], func=AF.Square, accum_out=st1[xb][:, 0:1]),
                reads=[r_xt[xb]], writes=[r_junk, r_st1[xb]])
            P.op("act", lambda e, xb=xb: e.activation(
                out=st1[xb][:, 1:2], in_=st1[xb][:, 0:1], func=AF.Sqrt, bias=EPS, scale=1.0 / 2048),
                reads=[r_st1[xb]], writes=[r_st1[xb]])
            P.op("dve", lambda e, xb=xb: e.reciprocal(out=st1[xb][:, 2:3], in_=st1[xb][:, 1:2]),
                 reads=[r_st1[xb]], writes=[r_st1[xb]])
            P.op("dve", lambda e, xb=xb: e.tensor_scalar(
                out=xn[xb][:], in0=xt[xb][:], scalar1=st1[xb][:, 2:3], scalar2=None, op0=ALU.mult),
                reads=[r_xt[xb], r_st1[xb]], writes=[r_xn[xb]])
            for half in range(2):
                pb = cnt["pT"] % 2
                cnt["pT"] += 1
                for j in range(8):
                    kc = half * 8 + j
                    P.op("pe", lambda e, xb=xb, kc=kc, j=j, pb=pb: e.transpose(
                        out=pT[pb][:, j * 128:(j + 1) * 128], in_=xn[xb][:, kc * 128:(kc + 1) * 128],
                        identity=idb[:]), reads=[r_xn[xb], r_c], writes=[r_pT[pb]])
                eng = "act" if half == 0 else "dve"
                if eng == "act":
                    P.op("act", lambda e, pb=pb, half=half, st=st, gb=gb: e.copy(
                        out=xnT[gb][:, half * 8:(half + 1) * 8, st * 128:(st + 1) * 128],
                        in_=pT[pb][:].rearrange("p (j t) -> p j t", t=128)),
                        reads=[r_pT[pb]], writes=[r_xnT[gb]])
                else:
                    P.op("dve", lambda e, pb=pb, half=half, st=st, gb=gb: e.tensor_copy(
                        out=xnT[gb][:, half * 8:(half + 1) * 8, st * 128:(st + 1) * 128],
                        in_=pT[pb][:].rearrange("p (j t) -> p j t", t=128)),
                        reads=[r_pT[pb]], writes=[r_xnT[gb]])
    def proj(G):
        gb = G % 2
        col0 = 0
        for bi, (nr, nv) in enumerate(tm_blocks):
            bw = nr * 128 + nv
            for st in range(4):
                tok0 = G * 512 + st * 128
                pb = cnt["pacc"] % 4
                cnt["pacc"] += 1
                for kc in range(16):
                    P.op("pe", lambda e, kc=kc, st=st, gb=gb, pb=pb, col0=col0, bw=bw: e.matmul(
                        pacc[pb][:, 0:bw], lhsT=xnT[gb][:, kc, st * 128:(st + 1) * 128],
                        rhs=wbf[:, kc, col0:col0 + bw], start=(kc == 0), stop=(kc == 15)),
                        reads=[r_xnT[gb], r_w[kc]], writes=[r_pacc[pb]])
                if nr > 0:
                    rw = nr * 128
                    cb = (G * 4 + st) % 2
                    P.dma(lambda e, cb=cb, tok0=tok0: e.dma_start(out=cs[cb][:], in_=cosd[tok0:tok0 + 128, :]),
                          writes=[r_cs[cb]], queue="pool")
                    P.dma(lambda e, cb=cb, tok0=tok0: e.dma_start(out=sn[cb][:], in_=sind[tok0:tok0 + 128, :]),
                          writes=[r_cs[cb]], queue="pool")
                    P.op("act", lambda e, pb=pb, rw=rw: e.activation(
                        out=sq[:, 0:rw], in_=pacc[pb][:, 0:rw], func=AF.Square),
                        reads=[r_pacc[pb]], writes=[r_sq])
                    P.op("dve", lambda e, rw=rw, nr=nr: e.tensor_reduce(
                        out=hst[:, 0:nr], in_=sq[:, 0:rw].rearrange("p (h d) -> p h d", d=128),
                        axis=AX.X, op=ALU.add), reads=[r_sq], writes=[r_hst])
                    P.op("act", lambda e, nr=nr: e.activation(
                        out=hst[:, 4:4 + nr], in_=hst[:, 0:nr], func=AF.Sqrt, bias=EPS, scale=1.0 / 128),
                        reads=[r_hst], writes=[r_hst])
                    P.op("dve", lambda e, nr=nr: e.reciprocal(out=hst[:, 0:nr], in_=hst[:, 4:4 + nr]),
                         reads=[r_hst], writes=[r_hst])
                    for h in range(nr):
                        P.op("dve", lambda e, h=h, pb=pb, col0=col0: e.scalar_tensor_tensor(
                            out=qn[:, h * 128:(h + 1) * 128], in0=pacc[pb][:, h * 128:(h + 1) * 128],
                            scalar=hst[:, h:h + 1], in1=gns[:, col0 + h * 128:col0 + (h + 1) * 128],
                            op0=ALU.mult, op1=ALU.mult),
                            reads=[r_pacc[pb], r_hst, r_c], writes=[r_qn])
                    P.op("pool", lambda e, rw=rw, cb=cb: e.tensor_tensor(
                        out=t1[:, 0:rw], in0=qn[:, 0:rw], in1=cs[cb][:, 0:rw], op=ALU.mult),
                        reads=[r_qn, r_cs[cb]], writes=[r_t])
                    for hf in range(2):
                        P.op("pool", lambda e, rw=rw, cb=cb, hf=hf: e.tensor_tensor(
                            out=t2[:, 0:rw].rearrange("p (h two d) -> p h two d", two=2, d=64)[:, :, hf, :],
                            in0=qn[:, 0:rw].rearrange("p (h two d) -> p h two d", two=2, d=64)[:, :, 1 - hf, :],
                            in1=sn[cb][:, 0:rw].rearrange("p (h two d) -> p h two d", two=2, d=64)[:, :, hf, :],
                            op=ALU.mult), reads=[r_qn, r_cs[cb], r_t], writes=[r_t])
                    P.op("pool", lambda e, rw=rw, bi=bi, st=st: e.tensor_tensor(
                        out=qr[bi][:, st, 0:rw], in0=t1[:, 0:rw], in1=t2[:, 0:rw], op=ALU.add),
                        reads=[r_t], writes=[r_qr[bi][st]])
                if nv > 0:
                    vb = cnt["vsb"] % 2
                    cnt["vsb"] += 1
                    voff = sum(b for a, b in tm_blocks[:bi])
                    P.op("act", lambda e, vb=vb, pb=pb, nr=nr, nv=nv: e.copy(
                        out=vsb[vb][:, 0:nv], in_=pacc[pb][:, nr * 128:nr * 128 + nv]),
                        reads=[r_pacc[pb]], writes=[r_vsb[vb]])
                    P.dma(lambda e, vb=vb, tok0=tok0, voff=voff, nv=nv: e.dma_start(
                        out=vtm[tok0:tok0 + 128, voff:voff + nv], in_=vsb[vb][:, 0:nv]),
                        reads=[r_vsb[vb]])
            hoff = sum(a for a, b in tm_blocks[:bi])
            for h in range(nr):
                tb = cnt["ptr"] % 2
                cnt["ptr"] += 1
                for st in range(4):
                    P.op("pe", lambda e, bi=bi, st=st, h=h, tb=tb: e.transpose(
                        out=ptr[tb][:, st * 128:(st + 1) * 128], in_=qr[bi][:, st, h * 128:(h + 1) * 128],
                        identity=idf[:]), reads=[r_qr[bi][st], r_c], writes=[r_ptr[tb]])
                qb = cnt["qTs"] % 2
                cnt["qTs"] += 1
                P.op("act", lambda e, qb=qb, tb=tb: e.copy(out=qTs[qb][:], in_=ptr[tb][:]),
                     reads=[r_ptr[tb]], writes=[r_qTs[qb]])
                P.dma(lambda e, qb=qb, hh=hoff + h, G=G: e.dma_start(
                    out=ropeT[hh, :, G * 512:(G + 1) * 512], in_=qTs[qb][:]), reads=[r_qTs[qb]])
            col0 += bw
        for fi, func in enumerate(fm_funcs):
            pb = cnt["pacc"] % 4
            cnt["pacc"] += 1
            c0 = n_tm + fi * 128
            for kc in range(16):
                P.op("pe", lambda e, kc=kc, gb=gb, pb=pb, c0=c0: e.matmul(
                    pacc[pb][:], lhsT=wbf[:, kc, c0:c0 + 128], rhs=xnT[gb][:, kc, :],
                    start=(kc == 0), stop=(kc == 15)),
                    reads=[r_xnT[gb], r_w[kc]], writes=[r_pacc[pb]])
            fb = cnt["fms"] % 2
            cnt["fms"] += 1
            af = AF.Silu if func == "silu" else AF.Copy
            P.op("act", lambda e, fb=fb, pb=pb, af=af: e.activation(out=fms[fb][:], in_=pacc[pb][:], func=af),
                 reads=[r_pacc[pb]], writes=[r_fms[fb]])
            P.dma(lambda e, fb=fb, fi=fi, G=G: e.dma_start(
                out=fmT[fi * 128:(fi + 1) * 128, G * 512:(G + 1) * 512], in_=fms[fb][:]), reads=[r_fms[fb]])
        if n_sg > 0:
            pb = cnt["pacc"] % 4
            cnt["pacc"] += 1
            c0 = n_tm + NF
            for kc in range(16):
                P.op("pe", lambda e, kc=kc, gb=gb, pb=pb, c0=c0: e.matmul(
                    pacc[pb][0:n_sg, :], lhsT=wbf[:, kc, c0:c0 + n_sg], rhs=xnT[gb][:, kc, :],
                    start=(kc == 0), stop=(kc == 15)),
                    reads=[r_xnT[gb], r_w[kc]], writes=[r_pacc[pb]])
            P.op("act", lambda e, pb=pb: e.activation(out=sgs[0:n_sg, :], in_=pacc[pb][0:n_sg, :], func=AF.Sigmoid),
                 reads=[r_pacc[pb]], writes=[r_sgs])
            P.dma(lambda e, G=G: e.dma_start(out=sgT[0:n_sg, G * 512:(G + 1) * 512], in_=sgs[0:n_sg, :]),
                  reads=[r_sgs])

    prep(0)
    for G in range(NG):
        if G + 1 < NG:
            prep(G + 1)
        proj(G)


def build_proj(tm_blocks, fm_funcs, n_sg, S=SEQ):
    nc = bass.Bass("TRN2", target_bir_lowering=False)
    n_tm = sum(a * 128 + b for a, b in tm_blocks)
    NR = sum(a for a, b in tm_blocks)
    NV = sum(b for a, b in tm_blocks)
    NF = 128 * len(fm_funcs)
    ncols = n_tm + NF + n_sg
    x = nc.dram_tensor("x", [S, 2048], F32, kind="ExternalInput").ap()
    io = dict(
        x_tile=lambda T: x[T * 128:(T + 1) * 128, :],
        w=nc.dram_tensor("w", [2048, ncols], F32, kind="ExternalInput").ap(),
        gx=nc.dram_tensor("gx", [128, 16], F32, kind="ExternalInput").ap(),
        gains=nc.dram_tensor("gains", [128, max(n_tm, 1)], F32, kind="ExternalInput").ap(),
        cos=nc.dram_tensor("cos", [S, 512], F32, kind="ExternalInput").ap(),
        sin=nc.dram_tensor("sin", [S, 512], F32, kind="ExternalInput").ap(),
        identf=nc.dram_tensor("identf", [128, 128], F32, kind="ExternalInput").ap(),
        identb=nc.dram_tensor("identb", [128, 128], BF16, kind="ExternalInput").ap(),
        ropeT=nc.dram_tensor("ropeT", [max(NR, 1), 128, S], BF16, kind="ExternalOutput").ap(),
        vtm=nc.dram_tensor("vtm", [S, max(NV, 1)], BF16, kind="ExternalOutput").ap(),
        fmT=nc.dram_tensor("fmT", [max(NF, 1), S], BF16, kind="ExternalOutput").ap(),
        sgT=nc.dram_tensor("sgT", [max(n_sg, 1), S], F32, kind="ExternalOutput").ap())
    P = Prog(nc)
    emit_proj(P, tm_blocks, fm_funcs, n_sg, io, S)
    P.finish()
    return nc


def rope_tables(S=SEQ):
    half = HD // 2
    inv_freq = np.exp(-math.log(10000.0) * np.arange(half, dtype=np.float32) / half).astype(np.float32)
    ang = np.arange(S, dtype=np.float32)[:, None] * inv_freq[None, :]
    c = np.cos(ang).astype(np.float32)
    s = np.sin(ang).astype(np.float32)
    cos128 = np.concatenate([c, c], axis=1)
    sin128 = np.concatenate([-s, s], axis=1)
    return np.tile(cos128, (1, 4)), np.tile(sin128, (1, 4))


class AttnCtx:
    def __init__(self, P, QW, p_dt=BF16, n_s=3):
        self.P = P
        self.QW = QW
        self.S = [P.ps([128, 512], F32, "S%d" % i) for i in range(n_s)]
        self.r_S = [Res() for _ in range(n_s)]
        self.pt = [P.sb([128, QW], p_dt, "pt%d" % i) for i in range(4)]
        self.r_pt = [Res() for _ in range(4)]
        self.o = [P.ps([128, 512], F32, "o%d" % i) for i in range(2)]
        self.r_o = [Res() for _ in range(2)]
        self.den = [P.ps([128, 512], F32, "den%d" % i) for i in range(2)]
        self.r_den = [Res() for _ in range(2)]
        self.si = 0
        self.pi = 0
        self.oi = 0
        self.ones = P.sb([128, 128], p_dt, "ones")
        self.r_ones = Res()
        P.op("pool", lambda e: e.memset(self.ones[:], 1.0), writes=[self.r_ones])
        self.pt32 = None

    def add_fp32(self):
        P = self.P
        self.pt32 = [P.sb([128, self.QW], F32, "pt32_%d" % i) for i in range(4)]
        self.r_pt32 = [Res() for _ in range(4)]
        self.ones32 = P.sb([128, 128], F32, "ones32")
        P.op("pool", lambda e: e.memset(self.ones32[:], 1.0), writes=[self.r_ones])
        self.pi32 = 0


def softmax_branch(P, A, qT_ap, r_q, ktiles, maskeng="pool", fp32=False):
    QW = A.QW
    ob = A.oi % 2
    A.oi += 1
    n = len(ktiles)
    used = []
    info = []

    def stage1(i):
        kt = ktiles[i]
        nk = kt["nk"]
        sb_ = A.si % len(A.S)
        A.si += 1
        if fp32:
            pb = A.pi32 % 4
            A.pi32 += 1
            pts, r_pts, ones = A.pt32, A.r_pt32, A.ones32
        else:
            pb = A.pi % 4
            A.pi += 1
            pts, r_pts, ones = A.pt, A.r_pt, A.ones
        used.append(pb)
        info.append((pb, pts, r_pts, ones))
        bias = kt.get("bias")
        P.op("pe", lambda e, kt=kt, sb_=sb_, nk=nk, bias=bias: e.matmul(
            A.S[sb_][0:nk, 0:QW], lhsT=kt["kT"], rhs=qT_ap, start=True, stop=(bias is None)),
            reads=[kt["r_k"], r_q], writes=[A.r_S[sb_]])
        if bias is not None:
            P.op("pe", lambda e, sb_=sb_, nk=nk, bias=bias: e.matmul(
                A.S[sb_][0:nk, 0:QW], lhsT=bias[0], rhs=bias[1], start=False, stop=True),
                reads=list(bias[2]), writes=[A.r_S[sb_]])
        P.op("act", lambda e, sb_=sb_, pb=pb, nk=nk, pts=pts: e.activation(
            out=pts[pb][0:nk, :], in_=A.S[sb_][0:nk, 0:QW], func=AF.Exp, scale=SCALE),
            reads=[A.r_S[sb_]], writes=[r_pts[pb]])
        mask = kt.get("mask")
        if mask is not None:
            P.op(maskeng, lambda e, pb=pb, nk=nk, mask=mask, pts=pts: e.tensor_tensor(
                out=pts[pb][0:nk, :], in0=pts[pb][0:nk, :], in1=mask[0], op=ALU.mult),
                reads=[r_pts[pb], mask[1]], writes=[r_pts[pb]])

    def stage2(i):
        kt = ktiles[i]
        nk = kt["nk"]
        pb, pts, r_pts, ones = info[i]
        P.op("pe", lambda e, kt=kt, pb=pb, nk=nk, i=i, pts=pts: e.matmul(
            A.o[ob][:, 0:QW], lhsT=kt["v"], rhs=pts[pb][0:nk, :], start=(i == 0), stop=(i == n - 1)),
            reads=[kt["r_v"], r_pts[pb]], writes=[A.r_o[ob]])
        P.op("pe", lambda e, pb=pb, nk=nk, i=i, pts=pts, ones=ones: e.matmul(
            A.den[ob][:, 0:QW], lhsT=ones[0:nk, :], rhs=pts[pb][0:nk, :], start=(i == 0), stop=(i == n - 1)),
            reads=[A.r_ones, r_pts[pb]], writes=[A.r_den[ob]])

    for step in range(n + 1):
        if step < n:
            stage1(step)
        if step >= 1:
            stage2(step - 1)
    return A.o[ob], A.r_o[ob], A.den[ob], A.r_den[ob], used, None


def emit_moba(P, io, S=SEQ):
    ropeT, vtm, gT, identf, eseld, cmaskd, ywrite = (io[k] for k in ("ropeT", "vtm", "gT", "identf", "esel", "cmask", "ywrite"))
    QW = 256
    A = AttnCtx(P, QW, n_s=2)
    idf = P.sb([128, 128], F32, "idf")
    esel = P.sb([16, 16 * 128], BF16, "esel")
    cmask = P.sb([128, 512], BF16, "cmask")
    r_c = Res()
    P.dma(lambda e: e.dma_start(out=idf[:], in_=identf), writes=[r_c], queue="pool")
    P.dma(lambda e: e.dma_start(out=esel[:], in_=eseld), writes=[r_c], queue="pool")
    P.dma(lambda e: e.dma_start(out=cmask[:], in_=cmaskd), writes=[r_c], queue="pool")
    qT = [P.sb([128, S], BF16, "qT%d" % i) for i in range(2)]
    kT = [P.sb([128, S], BF16, "kT%d" % i) for i in range(2)]
    vv = [P.sb([128, S // 128, 128], BF16, "vv%d" % i) for i in range(2)]
    r_q = [Res() for _ in range(2)]
    r_k = [Res() for _ in range(2)]
    r_v = [Res() for _ in range(2)]
    km32 = P.sb([128, 16], F32, "km32")
    kmb = P.sb([128, 16], BF16, "kmb")
    r_km = Res()
    gps = P.ps([128, 512], F32, "gps")
    r_gps = Res()
    tps = P.ps([128, 512], F32, "tps")
    r_tps = Res()
    gm = P.sb([128, 16], F32, "gm")
    top8 = P.sb([128, 8], F32, "top8")
    bia = P.sb([128, 16], F32, "bia")
    r_gm = Res()
    biasT = [P.sb([16, S], BF16, "biasT%d" % i) for i in range(2)]
    r_bT = [Res() for _ in range(2)]
    gt = [P.sb([128, QW], BF16, "gt%d" % i) for i in range(2)]
    r_gt = [Res() for _ in range(2)]
    rd = P.sb([128, QW], F32, "rd")
    r_rd = Res()
    yo = P.sb([128, QW], F32, "yo")
    r_yo = Res()
    yb = [P.sb([128, QW], BF16, "yb%d" % i) for i in range(2)]
    r_yb = [Res() for _ in range(2)]
    cnt = 0
    for h in range(4):
        hb = h % 2
        P.dma(lambda e, h=h, hb=hb: e.dma_start(out=qT[hb][:], in_=ropeT[h]), writes=[r_q[hb]])
        P.dma(lambda e, h=h, hb=hb: e.dma_start(out=kT[hb][:], in_=ropeT[4 + h]), writes=[r_k[hb]])
        P.dma(lambda e, h=h, hb=hb: e.dma_start(
            out=vv[hb][:], in_=vtm[:, h * 128:(h + 1) * 128].rearrange("(t p) d -> p t d", p=128)),
            writes=[r_v[hb]])
        P.op("dve", lambda e, hb=hb: e.tensor_reduce(
            out=km32[:, 0:S // 256], in_=kT[hb][:].rearrange("p (n k) -> p n k", k=256), axis=AX.X, op=ALU.add),
            reads=[r_k[hb]], writes=[r_km])
        P.op("dve", lambda e: e.tensor_scalar(out=kmb[:, 0:S // 256], in0=km32[:, 0:S // 256], scalar1=1.0 / 256,
                                              scalar2=None, op0=ALU.mult), reads=[r_km], writes=[r_km])
        for qg in range(S // 512):
            for j in range(4):
                qi = qg * 4 + j
                cur = qi // 2
                P.op("pe", lambda e, qi=qi, hb=hb: e.matmul(
                    gps[:, 0:S // 256], lhsT=qT[hb][:, qi * 128:(qi + 1) * 128], rhs=kmb[:, 0:S // 256],
                    start=True, stop=True),
                    reads=[r_q[hb], r_km], writes=[r_gps])
                P.op("dve", lambda e: e.memset(gm[:], -1e30), writes=[r_gm])
                if cur > 0:
                    P.op("dve", lambda e, cur=cur: e.tensor_copy(out=gm[:, 0:cur], in_=gps[:, 0:cur]),
                         reads=[r_gps], writes=[r_gm])
                P.op("dve", lambda e: e.max(out=top8[:], in_=gm[:]), reads=[r_gm], writes=[r_gm])
                P.op("dve", lambda e: e.tensor_scalar(
                    out=bia[:], in0=gm[:], scalar1=top8[:, 2:3], scalar2=-30000.0, op0=ALU.is_lt, op1=ALU.mult),
                    reads=[r_gm], writes=[r_gm])
                P.op("pe", lambda e, j=j: e.transpose(
                    out=tps[0:16, j * 128:(j + 1) * 128], in_=bia[:], identity=idf[:]),
                    reads=[r_gm, r_c], writes=[r_tps])
            P.op("act", lambda e, qg=qg, hb=hb: e.copy(out=biasT[hb][:, qg * 512:(qg + 1) * 512], in_=tps[0:16, :]),
                 reads=[r_tps], writes=[r_bT[hb]])
        for c in range(S // QW):
            ktiles = []
            for kt in range(2 * c + 2):
                d = dict(kT=kT[hb][:, kt * 128:(kt + 1) * 128], r_k=r_k[hb], v=vv[hb][:, kt, :], r_v=r_v[hb], nk=128)
                n = kt // 2
                if n < c:
                    d["bias"] = (esel[:, n * 128:(n + 1) * 128], biasT[hb][:, c * QW:(c + 1) * QW], [r_c, r_bT[hb]])
                else:
                    d["mask"] = (cmask[:, (kt - 2 * c) * 256:(kt - 2 * c + 1) * 256], r_c)
                ktiles.append(d)
            gb = cnt % 2
            cnt += 1
            P.dma(lambda e, gb=gb, h=h, c=c: e.dma_start(
                out=gt[gb][:], in_=gT[h * 128:(h + 1) * 128, c * QW:(c + 1) * QW]), writes=[r_gt[gb]], queue="pool")
            o_ps, r_o, d_ps, r_d, _, _ = softmax_branch(P, A, qT[hb][:, c * QW:(c + 1) * QW], r_q[hb], ktiles)
            P.op("dve", lambda e, d_ps=d_ps: e.reciprocal(out=rd[:], in_=d_ps[:, 0:QW]),
                 reads=[r_d], writes=[r_rd])
            P.op("dve", lambda e, o_ps=o_ps: e.tensor_tensor(out=yo[:], in0=o_ps[:, 0:QW], in1=rd[:], op=ALU.mult),
                 reads=[r_o, r_rd], writes=[r_yo])
            P.op("pool", lambda e, gb=gb: e.tensor_tensor(out=yb[gb][:], in0=yo[:], in1=gt[gb][:], op=ALU.mult),
                 reads=[r_yo, r_gt[gb]], writes=[r_yb[gb]])
            P.dma(lambda e, gb=gb, h=h, c=c: e.dma_start(out=ywrite(h, c * QW, QW), in_=yb[gb][:]), reads=[r_yb[gb]])


def moba_consts():
    esel = np.zeros((16, 16 * 128), np.float32)
    for n in range(16):
        esel[n, n * 128:(n + 1) * 128] = 1.0
    k = np.arange(128)[:, None]
    q = np.arange(256)[None, :]
    cm = np.concatenate([(k <= q), (k + 128 <= q)], axis=1).astype(np.float32)
    return {"identf": np.eye(128, dtype=np.float32), "esel": esel.astype(NPBF), "cmask": cm.astype(NPBF)}


def emit_sb(P, io, S=SEQ):
    QW = 512
    fmT, vtm, trid, dmaskd, ywrite = (io[k] for k in ("fmT", "vtm", "tri", "dmask", "ywrite"))
    tri = P.sb([128, 128], BF16, "tri")
    ones = P.sb([128, 128], BF16, "ones")
    dmask = P.sb([128, 4 * QW], BF16, "dmask")
    r_c = Res()
    P.dma(lambda e: e.dma_start(out=tri[:], in_=trid), writes=[r_c], queue="pool")
    P.dma(lambda e: e.dma_start(out=dmask[:], in_=dmaskd), writes=[r_c], queue="pool")
    P.op("pool", lambda e: e.memset(ones[:], 1.0), writes=[r_c])
    qT = [P.sb([128, S], BF16, "qT%d" % i) for i in range(2)]
    kT = [P.sb([128, S], BF16, "kT%d" % i) for i in range(2)]
    vv = [P.sb([128, S // 128, 128], BF16, "vv%d" % i) for i in range(2)]
    r_q = [Res() for _ in range(2)]
    r_k = [Res() for _ in range(2)]
    r_v = [Res() for _ in range(2)]
    zps = [P.ps([128, QW], F32, "z%d" % i) for i in range(3)]
    r_z = [Res() for _ in range(3)]
    ups = [P.ps([128, QW], F32, "u%d" % i) for i in range(2)]
    r_u = [Res() for _ in range(2)]
    wps = [P.ps([128, QW], F32, "w%d" % i) for i in range(2)]
    r_wp = [Res() for _ in range(2)]
    ops_ = [P.ps([128, QW], F32, "o0")] * 2
    r_o = [Res()] * 2
    NBUF = 3
    ex = [P.sb([128, QW], F32, "ex%d" % i) for i in range(NBUF)]
    r_ex = [Res() for _ in range(NBUF)]
    sp = [P.sb([128, QW], F32, "sp%d" % i) for i in range(NBUF)]
    r_sp = [Res() for _ in range(NBUF)]
    hi = [P.sb([128, QW], BF16, "hi%d" % i) for i in range(NBUF)]
    lo = [P.sb([128, QW], BF16, "lo%d" % i) for i in range(NBUF)]
    r_hl = [Res() for _ in range(NBUF)]
    tt = [P.sb([128, QW], F32, "tt%d" % i) for i in range(2)]
    r_tt = [Res() for _ in range(2)]
    aa = [P.sb([128, QW], BF16, "aa%d" % i) for i in range(2)]
    r_aa = [Res() for _ in range(2)]
    C = [P.sb([128, QW], F32, "C%d" % i) for i in range(2)]
    r_C = [Res() for _ in range(2)]
    gt = [P.sb([128, QW], BF16, "gt%d" % i) for i in range(2)]
    r_gt = [Res() for _ in range(2)]
    yb = [P.sb([128, QW], BF16, "yb%d" % i) for i in range(2)]
    r_yb = [Res() for _ in range(2)]
    ti = 0
    qc = 0
    for h in range(4):
        hb = h % 2
        P.dma(lambda e, h=h, hb=hb: e.dma_start(out=qT[hb][:], in_=fmT[h * 128:(h + 1) * 128, :]), writes=[r_q[hb]])
        P.dma(lambda e, h=h, hb=hb: e.dma_start(out=kT[hb][:], in_=fmT[512 + h * 128:512 + (h + 1) * 128, :]),
              writes=[r_k[hb]])
        P.dma(lambda e, h=h, hb=hb: e.dma_start(
            out=vv[hb][:], in_=vtm[:, h * 128:(h + 1) * 128].rearrange("(t p) d -> p t d", p=128)),
            writes=[r_v[hb]])
        for c in range(S // QW):
            cb = qc % 2
            qc += 1
            P.dma(lambda e, cb=cb, h=h, c=c: e.dma_start(
                out=gt[cb][:], in_=fmT[1024 + h * 128:1024 + (h + 1) * 128, c * QW:(c + 1) * QW]),
                writes=[r_gt[cb]], queue="pool")
            P.op("pool", lambda e, cb=cb: e.memset(C[cb][:], 0.0), writes=[r_C[cb]])
            nkt = (c + 1) * (QW // 128)
            kts = list(range(nkt - 1, -1, -1))
            bufs = []

            def stage1a(i, hb=hb, c=c):
                nonlocal ti
                kt = kts[i]
                b = ti % 3
                ti += 1
                bufs.append(b)
                dj = kt - c * (QW // 128)
                P.op("pe", lambda e, b=b, kt=kt: e.matmul(
                    zps[b][:], lhsT=kT[hb][:, kt * 128:(kt + 1) * 128], rhs=qT[hb][:, c * QW:(c + 1) * QW],
                    start=True, stop=True), reads=[r_k[hb], r_q[hb]], writes=[r_z[b]])
                P.op("act", lambda e, b=b: e.activation(out=ex[b][:], in_=zps[b][:], func=AF.Exp, scale=SCALE),
                     reads=[r_z[b]], writes=[r_ex[b]])
                P.op("act", lambda e, b=b: e.activation(out=sp[b][:], in_=ex[b][:], func=AF.Ln, bias=1.0),
                     reads=[r_ex[b]], writes=[r_sp[b]])
                if dj >= 0:
                    P.op("pool", lambda e, b=b, dj=dj: e.tensor_tensor(
                        out=sp[b][:], in0=sp[b][:], in1=dmask[:, dj * QW:(dj + 1) * QW], op=ALU.mult),
                        reads=[r_sp[b], r_c], writes=[r_sp[b]])

            def stage1b(i):
                b = bufs[i]
                u = i % 2
                P.op("act", lambda e, b=b: e.copy(out=hi[b][:], in_=sp[b][:]),
                     reads=[r_sp[b]], writes=[r_hl[b]])
                P.op("pool", lambda e, b=b: e.tensor_tensor(out=lo[b][:], in0=sp[b][:], in1=hi[b][:], op=ALU.subtract),
                     reads=[r_sp[b], r_hl[b]], writes=[r_hl[b]])
                P.op("pe", lambda e, b=b, u=u: e.matmul(ups[u][:], lhsT=tri[:], rhs=hi[b][:], start=True, stop=False),
                     reads=[r_c, r_hl[b]], writes=[r_u[u]])
                P.op("pe", lambda e, b=b, u=u: e.matmul(ups[u][:], lhsT=tri[:], rhs=lo[b][:], start=False, stop=True),
                     reads=[r_c, r_hl[b]], writes=[r_u[u]])
                P.op("pe", lambda e, b=b, u=u: e.matmul(wps[u][:], lhsT=ones[:], rhs=hi[b][:], start=True, stop=False),
                     reads=[r_c, r_hl[b]], writes=[r_wp[u]])
                P.op("pe", lambda e, b=b, u=u: e.matmul(wps[u][:], lhsT=ones[:], rhs=lo[b][:], start=False, stop=True),
                     reads=[r_c, r_hl[b]], writes=[r_wp[u]])

            def stage2(i, hb=hb, c=c, cb=cb, nkt=nkt):
                kt = kts[i]
                b = bufs[i]
                u = i % 2
                dj = kt - c * (QW // 128)
                P.op("dve", lambda e, u=u: e.tensor_tensor(out=tt[u][:], in0=ups[u][:], in1=C[cb][:], op=ALU.add),
                     reads=[r_u[u], r_C[cb]], writes=[r_tt[u]])
                P.op("dve", lambda e, b=b, u=u: e.scalar_tensor_tensor(
                    out=tt[u][:], in0=zps[b][:], scalar=SCALE, in1=tt[u][:], op0=ALU.mult, op1=ALU.subtract),
                    reads=[r_z[b], r_tt[u]], writes=[r_tt[u]])
                P.op("act", lambda e, u=u: e.activation(out=aa[u][:], in_=tt[u][:], func=AF.Exp),
                     reads=[r_tt[u]], writes=[r_aa[u]])
                if dj >= 0:
                    P.op("pool", lambda e, u=u, dj=dj: e.tensor_tensor(
                        out=aa[u][:], in0=aa[u][:], in1=dmask[:, dj * QW:(dj + 1) * QW], op=ALU.mult),
                        reads=[r_aa[u], r_c], writes=[r_aa[u]])
                if kt > 0:
                    P.op("dve", lambda e, u=u: e.tensor_tensor(
                        out=C[cb][:], in0=wps[u][:], in1=C[cb][:], op=ALU.add),
                        reads=[r_wp[u], r_C[cb]], writes=[r_C[cb]])
                P.op("pe", lambda e, u=u, kt=kt, i=i: e.matmul(
                    ops_[cb][:], lhsT=vv[hb][:, kt, :], rhs=aa[u][:], start=(i == 0), stop=(i == nkt - 1)),
                    reads=[r_v[hb], r_aa[u]], writes=[r_o[cb]])

            for step in range(nkt + 2):
                if step < nkt:
                    stage1a(step)
                if 1 <= step <= nkt:
                    stage1b(step - 1)
                if step >= 2:
                    stage2(step - 2)
            P.op("dve", lambda e, cb=cb: e.tensor_tensor(out=yb[cb][:], in0=ops_[cb][:], in1=gt[cb][:], op=ALU.mult),
                 reads=[r_o[cb], r_gt[cb]], writes=[r_yb[cb]])
            P.dma(lambda e, cb=cb, h=h, c=c: e.dma_start(out=ywrite(h, c * QW, QW), in_=yb[cb][:]), reads=[r_yb[cb]])


def sb_consts():
    kp = np.arange(128)[:, None]
    k = np.arange(128)[None, :]
    tri = (kp >= k).astype(np.float32)
    kk = np.arange(128)[:, None]
    q = np.arange(512)[None, :]
    dm = np.concatenate([(128 * j + kk < q) for j in range(4)], axis=1).astype(np.float32)
    return {"tri": tri.astype(NPBF), "dmask": dm.astype(NPBF)}


def emit_nsa(P, io, S=SEQ):
    QW = 256
    NCMP = (S - 32) // 16 + 1
    NSLC = S // 64
    NKT = S // 128
    ntl = [(0, min(128, NCMP))] + ([(1, NCMP - 128)] if NCMP > 128 else [])
    (ropeT, vtm, fmT, sgT, wkd, wvd, posd, kcgd, coscd, sincd, identf, ovld, maddd, cmpmd, e64d, selgd, cmaskd,
     wmaskd, ywrite) = (io[k] for k in ("ropeT", "vtm", "fmT", "sgT", "wk", "wv", "posT", "kcg", "cosc", "sinc", "identf",
                                        "ovl", "madd", "cmpm", "e64", "selg", "cmask", "wmask", "ywrite"))
    A = AttnCtx(P, QW, n_s=2)
    A.add_fp32()
    gbc = P.ps([128, 512], F32, "gbc")
    r_gbc = Res()
    misc = P.ps([128, 512], F32, "misc")
    r_misc = Res()
    r_c = Res()
    idf = P.sb([128, 128], F32, "idf")
    ovl = P.sb([128, 2, NSLC], F32, "ovl_s")
    e64 = P.sb([NSLC, NKT * 128], BF16, "e64_s")
    selg = P.sb([12, 12 * 128], F32, "selg_s")
    cmask = P.sb([128, 512], BF16, "cmask_s")
    wmask = P.sb([128, 512], BF16, "wmask_s")
    kcg = P.sb([128, 128], F32, "kcg_s")
    posT = P.sb([128, 32], F32, "posT_s")
    cosc = P.sb([128, 2, 128], F32, "cosc_s")
    sinc = P.sb([128, 2, 128], F32, "sinc_s")
    sg = [P.sb([12, QW], F32, "sg_s%d" % i) for i in range(2)]
    r_sg = [Res() for _ in range(2)]
    for dst, srcd in ((idf[:], identf), (ovl[:], ovld), (e64[:], e64d), (selg[:], selgd), (cmask[:], cmaskd),
                      (wmask[:], wmaskd), (kcg[:], kcgd), (posT[:], posd),
                      (cosc[:], coscd.rearrange("(t p) d -> p t d", p=128)),
                      (sinc[:], sincd.rearrange("(t p) d -> p t d", p=128))):
        P.dma(lambda e, dst=dst, srcd=srcd: e.dma_start(out=dst, in_=srcd), writes=[r_c], queue="pool")
    qT = P.sb([128, 4, S], BF16, "qT")
    ksT = P.sb([128, S], BF16, "ksT")
    kwT = P.sb([128, S], BF16, "kwT")
    vs = P.sb([128, NKT, 128], BF16, "vs")
    vw = P.sb([128, NKT, 128], BF16, "vw")
    r_in = Res()
    for h in range(4):
        P.dma(lambda e, h=h: e.dma_start(out=qT[:, h, :], in_=ropeT[h]), writes=[r_in])
    P.dma(lambda e: e.dma_start(out=ksT[:], in_=ropeT[4]), writes=[r_in])
    P.dma(lambda e: e.dma_start(out=kwT[:], in_=ropeT[5]), writes=[r_in])
    P.dma(lambda e: e.dma_start(out=vs[:], in_=vtm[:, 0:128].rearrange("(t p) d -> p t d", p=128)), writes=[r_in])
    P.dma(lambda e: e.dma_start(out=vw[:], in_=vtm[:, 128:256].rearrange("(t p) d -> p t d", p=128)), writes=[r_in])

    kcT = P.sb([128, S], BF16, "kcT")
    vcT = P.sb([128, S], BF16, "vcT")
    r_kv = Res()
    P.dma(lambda e: e.dma_start(out=kcT[:], in_=fmT[0:128, :]), writes=[r_kv])
    P.dma(lambda e: e.dma_start(out=vcT[:], in_=fmT[128:256, :]), writes=[r_kv])
    wst = [P.sb([128, 8, 128], F32, "wst%d" % i) for i in range(2)]
    r_wst = [Res() for _ in range(2)]
    wkb = P.sb([128, 32, 128], BF16, "wkb")
    wvb = P.sb([128, 32, 128], BF16, "wvb")
    r_wb = Res()
    wi = 0
    for (wd_, wb_) in ((wkd, wkb), (wvd, wvb)):
        for ch in range(4):
            s_ = wi % 2
            wi += 1
            P.dma(lambda e, s_=s_, wd_=wd_, ch=ch: e.dma_start(out=wst[s_][:], in_=wd_[:, ch * 8:(ch + 1) * 8, :]),
                  writes=[r_wst[s_]])
            P.op("dve", lambda e, s_=s_, wb_=wb_, ch=ch: e.tensor_copy(out=wb_[:, ch * 8:(ch + 1) * 8, :], in_=wst[s_][:]),
                 reads=[r_wst[s_]], writes=[r_wb])
    kcp = P.sb([128, 32, 256], BF16, "kcp")
    vcp = kcp
    r_cp = Res()

    def build_cp(src_):
        for l in range(32):
            eng = "dve" if l % 2 == 0 else "pool"
            P.op(eng, lambda e, l=l: e.tensor_scalar(
                out=kcp[:, l, 0:NCMP], in0=src_[:, l:l + 16 * (NCMP - 1) + 1:16], scalar1=posT[:, l:l + 1],
                scalar2=None, op0=ALU.add), reads=[r_kv, r_c], writes=[r_cp])
    kccT = P.sb([128, 256], BF16, "kccT")
    r_kcc = Res()
    vcc = P.sb([128, 2, 128], F32, "vcc")
    r_vcc = Res()
    csq = P.sb([128, 128], F32, "csq")
    cst = P.sb([128, 4], F32, "cst")
    cqn = P.sb([128, 128], F32, "cqn")
    ct1 = P.sb([128, 128], F32, "ct1")
    ct2 = P.sb([128, 128], F32, "ct2")
    ckr = P.sb([128, 128], F32, "ckr")
    r_cw = Res()
    build_cp(kcT)
    for nt, nk in ntl:
        for l in range(32):
            P.op("pe", lambda e, l=l, nt=nt, nk=nk: e.matmul(
                misc[0:nk, 0:128], lhsT=kcp[:, l, nt * 128:nt * 128 + nk], rhs=wkb[:, l, :],
                start=(l == 0), stop=(l == 31)), reads=[r_cp, r_wb], writes=[r_misc])
        P.op("act", lambda e, nk=nk: e.activation(out=csq[0:nk, :], in_=misc[0:nk, 0:128], func=AF.Square,
                                                  accum_out=cst[0:nk, 0:1]), reads=[r_misc], writes=[r_cw])
        P.op("act", lambda e, nk=nk: e.activation(out=cst[0:nk, 1:2], in_=cst[0:nk, 0:1], func=AF.Sqrt, bias=EPS,
                                                  scale=1.0 / 128), reads=[r_cw], writes=[r_cw])
        P.op("dve", lambda e, nk=nk: e.reciprocal(out=cst[0:nk, 2:3], in_=cst[0:nk, 1:2]), reads=[r_cw], writes=[r_cw])
        P.op("dve", lambda e, nk=nk: e.scalar_tensor_tensor(
            out=cqn[0:nk, :], in0=misc[0:nk, 0:128], scalar=cst[0:nk, 2:3], in1=kcg[0:nk, :],
            op0=ALU.mult, op1=ALU.mult), reads=[r_misc, r_cw, r_c], writes=[r_cw])
        P.op("dve", lambda e, nk=nk, nt=nt: e.tensor_tensor(out=ct1[0:nk, :], in0=cqn[0:nk, :], in1=cosc[0:nk, nt, :],
                                                            op=ALU.mult), reads=[r_cw, r_c], writes=[r_cw])
        for hf in range(2):
            P.op("dve", lambda e, nk=nk, nt=nt, hf=hf: e.tensor_tensor(
                out=ct2[0:nk, hf * 64:(hf + 1) * 64], in0=cqn[0:nk, (1 - hf) * 64:(2 - hf) * 64],
                in1=sinc[0:nk, nt, hf * 64:(hf + 1) * 64], op=ALU.mult), reads=[r_cw, r_c], writes=[r_cw])
        P.op("dve", lambda e, nk=nk: e.tensor_tensor(out=ckr[0:nk, :], in0=ct1[0:nk, :], in1=ct2[0:nk, :], op=ALU.add),
             reads=[r_cw], writes=[r_cw])
        P.op("pe", lambda e, nk=nk: e.transpose(out=misc[:, 128:128 + nk], in_=ckr[0:nk, :], identity=idf[0:nk, 0:nk]),
             reads=[r_cw, r_c], writes=[r_misc])
        P.op("act", lambda e, nk=nk, nt=nt: e.copy(out=kccT[:, nt * 128:nt * 128 + nk], in_=misc[:, 128:128 + nk]),
             reads=[r_misc], writes=[r_kcc])
    build_cp(vcT)
    for nt, nk in ntl:
        for l in range(32):
            P.op("pe", lambda e, l=l, nt=nt, nk=nk: e.matmul(
                misc[0:nk, 256:384], lhsT=vcp[:, l, nt * 128:nt * 128 + nk], rhs=wvb[:, l, :],
                start=(l == 0), stop=(l == 31)), reads=[r_cp, r_wb], writes=[r_misc])
        P.op("act", lambda e, nk=nk, nt=nt: e.copy(out=vcc[0:nk, nt, :], in_=misc[0:nk, 256:384]),
             reads=[r_misc], writes=[r_vcc])


    cmT = [P.sb([128, 2, QW], BF16, "cmT%d" % i) for i in range(2)]
    r_cm = [Res() for _ in range(2)]
    madd = [P.sb([128, 2, NSLC], F32, "madd%d" % i) for i in range(2)]
    r_madd = [Res() for _ in range(2)]
    gt = [P.sb([128, 4, QW], BF16, "gt%d" % i) for i in range(2)]
    r_gt = [Res() for _ in range(2)]
    psumT = P.sb([128, 2, QW], F32, "psumT")
    r_psT = Res()
    pn = P.sb([128, QW], F32, "pn")
    r_pn = Res()
    rd = P.sb([128, QW], F32, "rd")
    r_rd = Res()
    tb = P.sb([128, QW], F32, "tb")
    r_tb = Res()
    yacc = [P.sb([128, QW], F32, "yacc%d" % i) for i in range(4)]
    r_ya = [Res() for _ in range(4)]
    impm = P.sb([128, NSLC], F32, "impm")
    imp2 = P.sb([128, NSLC], F32, "imp2")
    t8a = P.sb([128, 8], F32, "t8a")
    t8b = P.sb([128, 8], F32, "t8b")
    bia = P.sb([128, NSLC], F32, "bia")
    r_sel = Res()
    biasT = P.sb([NSLC, QW], BF16, "biasT")
    r_bT = Res()
    yb = [P.sb([128, QW], BF16, "yb%d" % i) for i in range(2)]
    r_yb = [Res() for _ in range(2)]
    ybi = 0

    def finish_branch(hl, br, o_ps, r_o, d_ps, r_d, c, first, clamp=False):
        if clamp:
            P.op("dve", lambda e: e.tensor_scalar(out=rd[:], in0=d_ps[:, 0:QW], scalar1=1e-30, scalar2=None,
                                                  op0=ALU.max), reads=[r_d], writes=[r_rd])
            P.op("dve", lambda e: e.reciprocal(out=rd[:], in_=rd[:]), reads=[r_rd], writes=[r_rd])
        else:
            P.op("dve", lambda e: e.reciprocal(out=rd[:], in_=d_ps[:, 0:QW]), reads=[r_d], writes=[r_rd])
        P.op("dve", lambda e: e.tensor_tensor(out=tb[:], in0=o_ps[:, 0:QW], in1=rd[:], op=ALU.mult),
             reads=[r_o, r_rd], writes=[r_tb])
        row = br * 4 + hl
        P.op("pe", lambda e: e.matmul(gbc[:, 0:QW], lhsT=selg[:, row * 128:(row + 1) * 128],
                                      rhs=sg[c % 2][:], start=True, stop=True),
             reads=[r_c, r_sg[c % 2]], writes=[r_gbc])
        if first:
            P.op("dve", lambda e: e.tensor_tensor(out=yacc[hl][:], in0=gbc[:, 0:QW], in1=tb[:], op=ALU.mult),
                 reads=[r_gbc, r_tb], writes=[r_ya[hl]])
        else:
            P.op("dve", lambda e: e.tensor_tensor(out=tb[:], in0=gbc[:, 0:QW], in1=tb[:], op=ALU.mult),
                 reads=[r_gbc, r_tb], writes=[r_tb])
            P.op("pool", lambda e: e.tensor_tensor(out=yacc[hl][:], in0=yacc[hl][:], in1=tb[:], op=ALU.add),
                 reads=[r_tb, r_ya[hl]], writes=[r_ya[hl]])

    for c in range(S // QW):
        cb = c % 2
        P.dma(lambda e, cb=cb, c=c: e.dma_start(
            out=cmT[cb][:], in_=cmpmd[:, c * QW:(c + 1) * QW].rearrange("(t p) q -> p t q", p=128)),
            writes=[r_cm[cb]], queue="pool")
        P.dma(lambda e, cb=cb, c=c: e.dma_start(out=sg[cb][:], in_=sgT[:, c * QW:(c + 1) * QW]),
              writes=[r_sg[cb]], queue="pool")
        P.dma(lambda e, cb=cb, c=c: e.dma_start(
            out=madd[cb][:], in_=maddd[c * QW:(c + 1) * QW, :].rearrange("(t p) j -> p t j", p=128)),
            writes=[r_madd[cb]], queue="pool")
        P.dma(lambda e, cb=cb, c=c: e.dma_start(
            out=gt[cb][:], in_=fmT[256:768, c * QW:(c + 1) * QW].rearrange("(h p) q -> p h q", p=128)),
            writes=[r_gt[cb]], queue="pool")
        for hl in range(4):
            ktiles = []
            for nt, nk in ntl:
                ktiles.append(dict(kT=kccT[:, nt * 128:nt * 128 + nk], r_k=r_kcc, v=vcc[0:nk, nt, :], r_v=r_vcc, nk=nk,
                                   mask=(cmT[cb][0:nk, nt, :], r_cm[cb])))
            o_ps, r_o, d_ps, r_d, used, _ = softmax_branch(P, A, qT[:, hl, c * QW:(c + 1) * QW], r_in, ktiles, fp32=True)
            finish_branch(hl, 0, o_ps, r_o, d_ps, r_d, c, first=True, clamp=True)
            for nt, nk in ntl:
                pb = used[nt]
                if hl == 0:
                    P.op("pool", lambda e, nt=nt, nk=nk, pb=pb: e.tensor_tensor(
                        out=psumT[0:nk, nt, :], in0=A.pt32[pb][0:nk, :], in1=rd[0:nk, :], op=ALU.mult),
                        reads=[A.r_pt32[pb], r_rd], writes=[r_psT])
                else:
                    P.op("pool", lambda e, nk=nk, pb=pb: e.tensor_tensor(
                        out=pn[0:nk, :], in0=A.pt32[pb][0:nk, :], in1=rd[0:nk, :], op=ALU.mult),
                        reads=[A.r_pt32[pb], r_rd], writes=[r_pn])
                    P.op("pool", lambda e, nt=nt, nk=nk: e.tensor_tensor(
                        out=psumT[0:nk, nt, :], in0=psumT[0:nk, nt, :], in1=pn[0:nk, :], op=ALU.add),
                        reads=[r_pn, r_psT], writes=[r_psT])
        for qs in range(QW // 128):
            for nt, nk in ntl:
                P.op("pe", lambda e, qs=qs, nt=nt, nk=nk: e.matmul(
                    misc[:, 0:NSLC], lhsT=psumT[0:nk, nt, qs * 128:(qs + 1) * 128], rhs=ovl[0:nk, nt, :],
                    start=(nt == 0), stop=(nt == len(ntl) - 1)), reads=[r_psT, r_c], writes=[r_misc])
            P.op("dve", lambda e, qs=qs, cb=cb: e.tensor_tensor(out=impm[:], in0=misc[:, 0:NSLC], in1=madd[cb][:, qs, :],
                                                                op=ALU.add), reads=[r_misc, r_madd[cb]], writes=[r_sel])
            if NSLC > 16:
                P.op("dve", lambda e: e.max(out=t8a[:], in_=impm[:]), reads=[r_sel], writes=[r_sel])
                P.op("dve", lambda e: e.match_replace(out=imp2[:], in_to_replace=t8a[:], in_values=impm[:],
                                                      imm_value=-3.0e38), reads=[r_sel], writes=[r_sel])
                P.op("dve", lambda e: e.max(out=t8b[:], in_=imp2[:]), reads=[r_sel], writes=[r_sel])
                P.op("dve", lambda e: e.tensor_scalar(out=bia[:], in0=impm[:], scalar1=t8b[:, 7:8], scalar2=-30000.0,
                                                      op0=ALU.is_lt, op1=ALU.mult), reads=[r_sel], writes=[r_sel])
            else:
                P.op("dve", lambda e: e.memset(bia[:], 0.0), writes=[r_sel])
            P.op("pe", lambda e, qs=qs: e.transpose(out=misc[0:NSLC, 128 + qs * 128:256 + qs * 128], in_=bia[:],
                                                    identity=idf[:]), reads=[r_sel, r_c], writes=[r_misc])
        P.op("act", lambda e: e.copy(out=biasT[:], in_=misc[0:NSLC, 128:128 + QW]), reads=[r_misc], writes=[r_bT])
        for hl in range(4):
            ktiles = []
            for kt in range(2 * c + 2):
                d = dict(kT=ksT[:, kt * 128:(kt + 1) * 128], r_k=r_in, v=vs[:, kt, :], r_v=r_in, nk=128,
                         bias=(e64[:, kt * 128:(kt + 1) * 128], biasT[:], [r_c, r_bT]))
                if kt >= 2 * c:
                    d["mask"] = (cmask[:, (kt - 2 * c) * 256:(kt - 2 * c + 1) * 256], r_c)
                ktiles.append(d)
            o_ps, r_o, d_ps, r_d, _, _ = softmax_branch(P, A, qT[:, hl, c * QW:(c + 1) * QW], r_in, ktiles)
            finish_branch(hl, 1, o_ps, r_o, d_ps, r_d, c, first=False)
            ktiles = []
            for r in range(6):
                kt = 2 * c - 4 + r
                if kt < 0:
                    continue
                d = dict(kT=kwT[:, kt * 128:(kt + 1) * 128], r_k=r_in, v=vw[:, kt, :], r_v=r_in, nk=128)
                if r in (0, 1):
                    d["mask"] = (wmask[:, r * 256:(r + 1) * 256], r_c)
                elif r in (4, 5):
                    d["mask"] = (cmask[:, (r - 4) * 256:(r - 3) * 256], r_c)
                ktiles.append(d)
            o_ps, r_o, d_ps, r_d, _, _ = softmax_branch(P, A, qT[:, hl, c * QW:(c + 1) * QW], r_in, ktiles)
            finish_branch(hl, 2, o_ps, r_o, d_ps, r_d, c, first=False)
            ob = ybi % 2
            ybi += 1
            P.op("pool", lambda e, hl=hl, ob=ob, cb=cb: e.tensor_tensor(
                out=yb[ob][:], in0=yacc[hl][:], in1=gt[cb][:, hl, :], op=ALU.mult),
                reads=[r_ya[hl], r_gt[cb]], writes=[r_yb[ob]])
            P.dma(lambda e, hl=hl, ob=ob, c=c: e.dma_start(out=ywrite(hl, c * QW, QW), in_=yb[ob][:]), reads=[r_yb[ob]])


def nsa_consts(S=SEQ):
    NCMP = (S - 32) // 16 + 1
    NSLC = S // 64
    NKT = S // 128
    half = 64
    inv_freq = np.exp(-math.log(10000.0) * np.arange(half, dtype=np.float32) / half).astype(np.float32)
    pos_c = (np.arange(256) * 16 + 31).astype(np.float32)
    ang = pos_c[:, None] * inv_freq[None, :]
    c = np.cos(ang).astype(np.float32)
    s = np.sin(ang).astype(np.float32)
    cosc = np.concatenate([c, c], 1)
    sinc = np.concatenate([-s, s], 1)
    n = np.arange(256)
    j = np.arange(NSLC)
    ov = ((n[:, None] * 16 < j[None, :] * 64 + 64) & (n[:, None] * 16 + 32 > j[None, :] * 64)).astype(np.float32)
    ov[NCMP:] = 0
    ovl = np.ascontiguousarray(ov.reshape(2, 128, NSLC).transpose(1, 0, 2))
    t = np.arange(S)
    cur = t // 64
    forced = (j[None, :] == 0) | (j[None, :] == cur[:, None]) | (j[None, :] == cur[:, None] - 1)
    madd = np.where(forced, 1e30, 0.0).astype(np.float32)
    madd = np.where(j[None, :] <= cur[:, None], madd, -1e30).astype(np.float32)
    cmpm = ((n[:, None] * 16 + 31) <= t[None, :]).astype(np.float32)
    cmpm[NCMP:] = 0
    e64 = np.zeros((NSLC, NKT * 128), np.float32)
    for kt in range(NKT):
        e64[2 * kt, kt * 128:kt * 128 + 64] = 1
        e64[2 * kt + 1, kt * 128 + 64:kt * 128 + 128] = 1
    selg = np.zeros((12, 12 * 128), np.float32)
    for r in range(12):
        selg[r, r * 128:(r + 1) * 128] = 1
    k = np.arange(128)[:, None]
    q = np.arange(256)[None, :]
    cm = np.concatenate([(k <= q), (k + 128 <= q)], axis=1).astype(np.float32)
    wm = np.concatenate([((q - k + 512 - 128 * r >= 0) & (q - k + 512 - 128 * r < 512)) for r in (0, 1)], axis=1)
    return {"cosc": cosc, "sinc": sinc, "identf": np.eye(128, dtype=np.float32), "ovl": ovl, "madd": madd,
            "cmpm": cmpm.astype(NPBF), "e64": e64.astype(NPBF), "selg": selg, "cmask": cm.astype(NPBF),
            "wmask": wm.astype(np.float32).astype(NPBF)}


LAYER_KEYS = [
    dict(norm="l0_norm", w_in="l0_w_in", q_norm="l0_q_norm", k_norm="l0_k_norm", w_out="l0_w_out"),
    dict(norm="l1_norm", w_in="l1_w_in", w_out="l1_w_out"),
    dict(norm="l2_norm", w_in="l2_w_in", q_norm="l2_q_norm", kc_norm="l2_kc_norm", ks_norm="l2_ks_norm",
         kw_norm="l2_kw_norm", cmp_wk="l2_cmp_wk", cmp_wv="l2_cmp_wv", cmp_pos="l2_cmp_pos", w_out="l2_w_out"),
    dict(norm="l3_norm", w_in="l3_w_in", q_norm="l3_q_norm", k_norm="l3_k_norm", w_out="l3_w_out"),
]
CFGS = [
    ([(4, 0), (4, 0), (0, 512)], ["silu"] * 4, 0),
    ([(0, 512)], ["copy"] * 8 + ["silu"] * 4, 0),
    ([(4, 0), (2, 256)], ["copy", "copy"] + ["silu"] * 4, 12),
]
RG = [[0, 1, 2, 3], [4, 5, 6, 7]]
N_LAYERS = 4


def _cfg_dims(cfg):
    tm, fm, nsg = cfg
    n_tm = sum(a * 128 + b for a, b in tm)
    NR = sum(a for a, b in tm)
    NV = sum(b for a, b in tm)
    NF = 128 * len(fm)
    return n_tm, NR, NV, NF, n_tm + NF + nsg


def build_fused(n_layers=N_LAYERS):
    nc = bass.Bass("TRN2", target_bir_lowering=False)
    S = SEQ

    def din(name, shape, dt):
        return nc.dram_tensor(name, list(shape), dt, kind="ExternalInput").ap()

    def dint(name, shape, dt):
        return nc.dram_tensor(name, list(shape), dt).ap()

    xfull = din("xfull", [S, 2048], F32)
    xq0 = din("xq0", [1024, 2048], F32)
    sel = din("sel", [128, 4], F32)
    out = nc.dram_tensor("out", [1024, 2048], F32, kind="ExternalOutput").ap()
    cst = dict(cos=din("cos", [S, 512], F32), sin=din("sin", [S, 512], F32),
               identf=din("identf", [128, 128], F32), identb=din("identb", [128, 128], BF16),
               esel=din("esel", [16, 2048], BF16), cmask=din("cmask", [128, 512], BF16),
               tri=din("tri", [128, 128], BF16), dmask=din("dmask", [128, 2048], BF16),
               cosc=din("cosc", [256, 128], F32), sinc=din("sinc", [256, 128], F32),
               ovl=din("ovl", [128, 2, 64], F32), madd=din("madd", [S, 64], F32), cmpm=din("cmpm", [256, S], BF16),
               e64=din("e64", [64, S], BF16), selg=din("selg", [12, 12 * 128], F32), wmask=din("wmask", [128, 512], BF16))
    P = Prog(nc)
    xg = None
    xqc_prev = None
    next_hook, next_rx = None, None
    for li in range(n_layers):
        kind = li % 3
        cfg = CFGS[kind]
        n_tm, NR, NV, NF, ncols = _cfg_dims(cfg)
        pre = "L%d_" % li
        w = din(pre + "w", [2048, ncols], F32)
        gx = din(pre + "gx", [128, 16], F32)
        gains = din(pre + "gains", [128, n_tm], F32)
        w_out = din(pre + "wout", [2048, 2048], F32)
        ropeT = dint(pre + "ropeT", [max(NR, 1), 128, S], BF16)
        vtm = dint(pre + "vtm", [S, NV], BF16)
        fmT = dint(pre + "fmT", [NF, S], BF16)
        sgT = dint(pre + "sgT", [12, S], F32)
        yTc = [dint(pre + "yTc%d" % j, [512, 1024], BF16) for j in range(4)]
        yG = [dint(pre + "yG%d" % j, [2048, 1024], BF16) for j in range(4)]
        if li == 0:
            x_tile = lambda T: xfull[T * 128:(T + 1) * 128, :]
        else:
            x_tile = lambda T, xg=xg: xg[T % 8][(T // 8) * 128:(T // 8 + 1) * 128, :]
        io = dict(x_tile=x_tile, w=w, gx=gx, gains=gains, cos=cst["cos"], sin=cst["sin"], identf=cst["identf"],
                  identb=cst["identb"], ropeT=ropeT, vtm=vtm, fmT=fmT, sgT=sgT, pre_hook=next_hook, r_x=next_rx)
        emit_proj(P, cfg[0], cfg[1], cfg[2], io, S)
        P.end_phase()
        ywrite = lambda h, c0, n, yTc=yTc: yTc[c0 // 1024][h * 128:(h + 1) * 128, (c0 % 1024):(c0 % 1024) + n]
        if kind == 0:
            emit_moba(P, dict(ropeT=ropeT, vtm=vtm, gT=fmT, identf=cst["identf"], esel=cst["esel"],
                              cmask=cst["cmask"], ywrite=ywrite), S)
        elif kind == 1:
            emit_sb(P, dict(fmT=fmT, vtm=vtm, tri=cst["tri"], dmask=cst["dmask"], ywrite=ywrite), S)
        else:
            io = dict(ropeT=ropeT, vtm=vtm, fmT=fmT, sgT=sgT, wk=din(pre + "wk", [128, 32, 128], F32),
                      wv=din(pre + "wv", [128, 32, 128], F32), posT=din(pre + "posT", [128, 32], F32),
                      kcg=din(pre + "kcg", [128, 128], F32), ywrite=ywrite)
            for k in ("cosc", "sinc", "identf", "ovl", "madd", "cmpm", "e64", "selg", "cmask", "wmask"):
                io[k] = cst[k]
            emit_nsa(P, io, S)
        P.end_phase()
        r_yG = Res()

        def hook1(yTc=yTc, yG=yG, r_yG=r_yG):
            for j in range(4):
                P.cc(lambda e, j=j: e.collective_compute(
                    "AllGather", ALU.bypass, replica_groups=RG, ins=[yTc[j].opt()], outs=[yG[j].opt()]),
                    writes=[r_yG])
        if li == 0:
            x_tile_c = lambda tt: xq0[tt * 128:(tt + 1) * 128, :]
        else:
            x_tile_c = lambda tt, xqc_prev=xqc_prev: xqc_prev[tt]
        if li < n_layers - 1:
            xqc = [dint(pre + "xqc%d" % i, [128, 2048], F32) for i in range(8)]
            out_tile = lambda tt, xqc=xqc: xqc[tt]
        else:
            xqc = None
            out_tile = lambda tt: out[tt * 128:(tt + 1) * 128, :]
        emit_outproj(P, 1024, w_out, x_tile_c, out_tile, yG=yG, sel=sel, pre_hook=hook1, r_yG=r_yG)
        P.end_phase()
        if li < n_layers - 1:
            xg = [dint(pre + "xg%d" % i, [512, 2048], F32) for i in range(8)]
            r_xg = Res()

            def hook2(xqc=xqc, xg=xg, r_xg=r_xg):
                for i in range(8):
                    P.cc(lambda e, i=i: e.collective_compute(
                        "AllGather", ALU.bypass, replica_groups=RG, ins=[xqc[i].opt()], outs=[xg[i].opt()]),
                        writes=[r_xg])
            next_hook, next_rx = hook2, r_xg
            xqc_prev = xqc
    P.finish()
    return nc


_PROGS = {}


def _rep(v):
    return np.ascontiguousarray(np.tile(np.asarray(v, np.float32)[None, :], (128, 1)))


def _layer_inputs(li, LK, core):
    kind = li % 3
    b, hg = divmod(core, 4)
    w_in = LK["w_in"]
    W = 2048
    sl = slice(hg * 512, (hg + 1) * 512)
    pre = "L%d_" % li
    m = {}
    if kind == 0:
        cols = [w_in[:, 0:W][:, sl], w_in[:, W:2 * W][:, sl], w_in[:, 2 * W:3 * W][:, sl], w_in[:, 3 * W:4 * W][:, sl]]
        gains = np.concatenate([np.tile(LK["q_norm"], 4), np.tile(LK["k_norm"], 4), np.ones(512, np.float32)])
    elif kind == 1:
        cols = [w_in[:, 2 * W:3 * W][:, sl], w_in[:, 0:W][:, sl], w_in[:, W:2 * W][:, sl], w_in[:, 3 * W:4 * W][:, sl]]
        gains = np.ones(512, np.float32)
    else:
        g = hg
        h1 = slice(g * 128, (g + 1) * 128)
        bg = [7168 + br * 16 + 4 * g + hl for br in range(3) for hl in range(4)]
        cols = [w_in[:, 0:2048][:, sl], w_in[:, 3072:3584][:, h1], w_in[:, 4096:4608][:, h1],
                w_in[:, 3584:4096][:, h1], w_in[:, 4608:5120][:, h1],
                w_in[:, 2048:2560][:, h1], w_in[:, 2560:3072][:, h1], w_in[:, 5120:7168][:, sl], w_in[:, bg]]
        gains = np.concatenate([np.tile(LK["q_norm"], 4), LK["ks_norm"], LK["kw_norm"], np.ones(256, np.float32)])
        m[pre + "wk"] = np.ascontiguousarray(LK["cmp_wk"].transpose(1, 0, 2))
        m[pre + "wv"] = np.ascontiguousarray(LK["cmp_wv"].transpose(1, 0, 2))
        m[pre + "posT"] = np.ascontiguousarray(LK["cmp_pos"].T)
        m[pre + "kcg"] = _rep(LK["kc_norm"])
    m[pre + "w"] = np.ascontiguousarray(np.concatenate(cols, axis=1))
    m[pre + "gx"] = np.ascontiguousarray(LK["norm"].reshape(16, 128).T)
    m[pre + "gains"] = _rep(gains)
    m[pre + "wout"] = LK["w_out"]
    return m


def kernel(**inputs):
    x = np.asarray(inputs["x"], np.float32)
    cos, sin = rope_tables(SEQ)
    identf = np.eye(128, dtype=np.float32)
    cst = {"cos": cos, "sin": sin, "identf": identf, "identb": identf.astype(NPBF)}
    cst.update(moba_consts())
    cst.update(sb_consts())
    cst.update(nsa_consts())
    LKs = [{k: np.asarray(inputs[v], np.float32) for k, v in LAYER_KEYS[li].items()} for li in range(N_LAYERS)]
    maps = []
    for core in range(NCORES):
        b, sq = divmod(core, 4)
        selv = np.zeros((128, 4), np.float32)
        selv[:, sq] = 1.0
        m = {"xfull": np.ascontiguousarray(x[b]), "xq0": np.ascontiguousarray(x[b, sq * 1024:(sq + 1) * 1024]),
             "sel": selv}
        m.update(cst)
        for li in range(N_LAYERS):
            m.update(_layer_inputs(li, LKs[li], core))
        maps.append(m)
    if "fused" not in _PROGS:
        _PROGS["fused"] = build_fused()
    res = run_bass_kernel_spmd(_PROGS["fused"], maps, core_ids=list(range(NCORES)))
    r = res.results
    out = np.stack([np.concatenate([r[b * 4 + sq]["out"] for sq in range(4)], axis=0) for b in range(BATCH)], axis=0)
    return out.astype(np.float32)
```

```python
import math
from contextlib import ExitStack

import numpy as np
import ml_dtypes

import concourse.bass as bass
import concourse.mybir as mybir
from concourse.bass_utils import run_bass_kernel_spmd

F32 = mybir.dt.float32
BF16 = mybir.dt.bfloat16
AF = mybir.ActivationFunctionType
ALU = mybir.AluOpType
AX = mybir.AxisListType
NPBF = ml_dtypes.bfloat16

D_MODEL = 2048
BATCH = 2
SEQ = 4096
HD = 128
NH = 16
EPS = 1e-6
SCALE = HD ** -0.5
NCORES = 8


class Res:
    __slots__ = ("w", "r")

    def __init__(self):
        self.w = None
        self.r = {}


class Prog:
    COMPUTE = ("pe", "act", "dve", "pool")
    NDMA_SEMS = 8

    def __init__(self, nc):
        self.nc = nc
        self.es = ExitStack()
        self.sems = {}
        self.cnt = {}
        self.q = {e: [] for e in ("pe", "act", "dve", "pool", "sp")}
        self.known = {e: {} for e in self.q}
        for e in self.COMPUTE:
            self.sems[e] = nc.alloc_semaphore(name="s_" + e)
            self.cnt[e] = 0
        self.dma_rr = {"sp": 0, "pool": 0}
        for qn in ("sp", "pool"):
            for i in range(self.NDMA_SEMS):
                k = "d_%s%d" % (qn, i)
                self.sems[k] = nc.alloc_semaphore(name=k)
                self.cnt[k] = 0
        self.n_sb = 0
        self.n_ps = 0
        self.phase = 0
        self.sems["cc"] = nc.alloc_semaphore(name="s_cc")
        self.cnt["cc"] = 0

    def sb(self, shape, dt, name=None):
        self.n_sb += 1
        return self.es.enter_context(self.nc.sbuf_tensor("sb%d_" % self.phase + (name or ("t%d" % self.n_sb)), list(shape), dt))

    def ps(self, shape, dt, name=None):
        self.n_ps += 1
        return self.es.enter_context(self.nc.psum_tensor("ps%d_" % self.phase + (name or ("t%d" % self.n_ps)), list(shape), dt))

    def _deps(self, eng, reads, writes):
        deps = {}

        def add(k, v):
            if v > deps.get(k, 0):
                deps[k] = v

        for r in reads:
            if r.w is not None:
                add(*r.w)
        for w in writes:
            if w.w is not None:
                add(*w.w)
            for k, v in w.r.items():
                add(k, v)
        waits = []
        kn = self.known[eng]
        for k, v in deps.items():
            if eng == "pe" and k == "pe":
                continue
            if kn.get(k, 0) >= v:
                continue
            kn[k] = v
            waits.append((k, v))
        return waits

    def op(self, eng, fn, reads=(), writes=()):
        waits = self._deps(eng, reads, writes)
        self.cnt[eng] += 1
        n = self.cnt[eng]
        for r in reads:
            r.r[eng] = n
        for w in writes:
            w.w = (eng, n)
            w.r = {}
        self.q[eng].append((waits, fn, eng, 1))

    def dma(self, fn, reads=(), writes=(), queue="sp"):
        i = self.dma_rr[queue]
        self.dma_rr[queue] = (i + 1) % self.NDMA_SEMS
        k = "d_%s%d" % (queue, i)
        waits = self._deps(queue, reads, writes)
        kn = self.known[queue]
        prev = self.cnt[k]
        if prev > 0 and kn.get(k, 0) < prev:
            kn[k] = prev
            waits.append((k, prev))
        self.cnt[k] += 16
        n = self.cnt[k]
        for r in reads:
            r.r[k] = n
        for w in writes:
            w.w = (k, n)
            w.r = {}
        self.q[queue].append((waits, fn, k, 16))

    def cc(self, fn, reads=(), writes=()):
        waits = self._deps("pool", reads, writes)
        kn = self.known["pool"]
        prev = self.cnt["cc"]
        if prev > 0 and kn.get("cc", 0) < prev:
            kn["cc"] = prev
            waits.append(("cc", prev))
        self.cnt["cc"] += 1
        n = self.cnt["cc"]
        for r in reads:
            r.r["cc"] = n
        for w in writes:
            w.w = ("cc", n)
            w.r = {}
        self.q["pool"].append((waits, fn, "cc", 1))

    def barrier(self):
        allw = [(k, v) for k, v in self.cnt.items() if v > 0]
        for eng in self.q:
            kn = self.known[eng]
            waits = []
            for k, v in allw:
                if eng == "pe" and k == "pe":
                    continue
                if kn.get(k, 0) < v:
                    kn[k] = v
                    waits.append((k, v))
            if waits:
                self.q[eng].append((waits, None, None, 0))

    def flush(self):
        nc = self.nc
        sems = self.sems
        q = self.q

        def run(eng, lst):
            for waits, fn, k, inc in lst:
                for (wk, wv) in waits:
                    eng.wait_ge(sems[wk], wv)
                if fn is not None:
                    ins = fn(eng)
                    ins.then_inc(sems[k], inc)

        with nc.Block() as block:
            @block.tensor
            def _(e):
                run(e, q["pe"])

            @block.scalar
            def _(e):
                run(e, q["act"])

            @block.vector
            def _(e):
                run(e, q["dve"])

            @block.gpsimd
            def _(e):
                run(e, q["pool"])

            @block.sync
            def _(e):
                run(e, q["sp"])
        self.q = {e: [] for e in q}

    def end_phase(self):
        self.barrier()
        self.flush()
        self.es.close()
        self.es = ExitStack()
        self.phase += 1

    def finish(self):
        self.end_phase()
        nc = self.nc
        nc.all_engine_barrier()
        nc.clear_and_free_semaphores(list(self.sems.values()))
        nc.all_engine_barrier()


def emit_outproj(P, ntok, w, x_tile, out_tile, yT=None, yG=None, sel=None, pre_hook=None, r_yG=None):
    wbf = P.sb([128, 16, 2048], BF16, "wbf")
    r_w = [Res() for _ in range(16)]
    stg = [P.sb([128, 2048], F32, "stg%d" % i) for i in range(2)]
    r_stg = [Res() for _ in range(2)]
    ysb = P.sb([128, 16, ntok], BF16, "ysb")
    if yG is None:
        r_y = [Res()] * 16
        P.dma(lambda e: e.dma_start(out=ysb[:], in_=yT.rearrange("(c p) t -> p c t", p=128)),
              writes=[r_y[0]], queue="pool")
    else:
        r_y = [Res() for _ in range(16)]
        sels = P.sb([128, 4], F32, "sels")
        r_sel = Res()
        P.dma(lambda e: e.dma_start(out=sels[:], in_=sel), writes=[r_sel], queue="pool")
        if pre_hook is not None:
            pre_hook()
        r_yGl = [r_yG] if r_yG is not None else []
        cand = [P.sb([128, 4, ntok], BF16, "cand%d" % i) for i in range(2)]
        r_cand = [Res() for _ in range(2)]
        for kc in range(16):
            cb = kc % 2
            for j in range(4):
                P.dma(lambda e, kc=kc, cb=cb, j=j: e.dma_start(out=cand[cb][:, j, :], in_=yG[j][kc * 128:(kc + 1) * 128, :]),
                      reads=r_yGl, writes=[r_cand[cb]], queue="pool")
            eng = "dve"
            P.op(eng, lambda e, kc=kc, cb=cb: e.tensor_scalar(
                out=ysb[:, kc, :], in0=cand[cb][:, 0, :], scalar1=sels[:, 0:1], scalar2=None, op0=ALU.mult),
                reads=[r_cand[cb], r_sel], writes=[r_y[kc]])
            for j in range(1, 4):
                P.op(eng, lambda e, kc=kc, cb=cb, j=j: e.scalar_tensor_tensor(
                    out=ysb[:, kc, :], in0=cand[cb][:, j, :], scalar=sels[:, j:j + 1], in1=ysb[:, kc, :],
                    op0=ALU.mult, op1=ALU.add), reads=[r_cand[cb], r_sel, r_y[kc]], writes=[r_y[kc]])
    for kc in range(16):
        s = kc % 2
        P.dma(lambda e, kc=kc, s=s: e.dma_start(out=stg[s][:], in_=w[kc * 128:(kc + 1) * 128, :]),
              writes=[r_stg[s]])
        eng = "dve" if kc % 2 == 0 else "act"
        if eng == "dve":
            P.op("dve", lambda e, kc=kc, s=s: e.tensor_copy(out=wbf[:, kc, :], in_=stg[s][:]),
                 reads=[r_stg[s]], writes=[r_w[kc]])
        else:
            P.op("act", lambda e, kc=kc, s=s: e.copy(out=wbf[:, kc, :], in_=stg[s][:]),
                 reads=[r_stg[s]], writes=[r_w[kc]])
    pss = [P.ps([128, 512], F32, "pso%d" % i) for i in range(4)]
    r_ps = [Res() for _ in range(4)]
    xt = [P.sb([128, 2048], F32, "xt%d" % i) for i in range(2)]
    r_xt = [Res() for _ in range(2)]
    ot = [P.sb([128, 2048], F32, "ot%d" % i) for i in range(2)]
    r_ot = [Res() for _ in range(2)]
    pi = 0
    for tt in range(ntok // 128):
        s = tt % 2
        P.dma(lambda e, tt=tt, s=s: e.dma_start(out=xt[s][:], in_=x_tile(tt)),
              writes=[r_xt[s]], queue="pool")
        for ct in range(4):
            b = pi % 4
            pi += 1
            for kc in range(16):
                P.op("pe", lambda e, kc=kc, tt=tt, ct=ct, b=b: e.matmul(
                    pss[b][:], lhsT=ysb[:, kc, tt * 128:(tt + 1) * 128],
                    rhs=wbf[:, kc, ct * 512:(ct + 1) * 512], start=(kc == 0), stop=(kc == 15)),
                    reads=[r_y[kc], r_w[kc]], writes=[r_ps[b]])
            P.op("dve", lambda e, s=s, ct=ct, b=b: e.tensor_tensor(
                out=ot[s][:, ct * 512:(ct + 1) * 512], in0=pss[b][:], in1=xt[s][:, ct * 512:(ct + 1) * 512],
                op=ALU.add), reads=[r_ps[b], r_xt[s]], writes=[r_ot[s]])
        P.dma(lambda e, tt=tt, s=s: e.dma_start(out=out_tile(tt), in_=ot[s][:]), reads=[r_ot[s]])


def build_outproj(ntok=1024):
    nc = bass.Bass("TRN2", target_bir_lowering=False)
    yT = nc.dram_tensor("yT", [2048, ntok], BF16, kind="ExternalInput").ap()
    x = nc.dram_tensor("x", [ntok, 2048], F32, kind="ExternalInput").ap()
    w = nc.dram_tensor("w", [2048, 2048], F32, kind="ExternalInput").ap()
    out = nc.dram_tensor("out", [ntok, 2048], F32, kind="ExternalOutput").ap()
    P = Prog(nc)
    emit_outproj(P, ntok, w, lambda tt: x[tt * 128:(tt + 1) * 128, :], lambda tt: out[tt * 128:(tt + 1) * 128, :], yT=yT)
    P.finish()
    return nc


def emit_proj(P, tm_blocks, fm_funcs, n_sg, io, S=SEQ):
    n_tm = sum(a * 128 + b for a, b in tm_blocks)
    NR = sum(a for a, b in tm_blocks)
    NV = sum(b for a, b in tm_blocks)
    NF = 128 * len(fm_funcs)
    ncols = n_tm + NF + n_sg
    x_tile = io["x_tile"]
    w, gx, gains, cosd, sind, identf, identb = (io[k] for k in ("w", "gx", "gains", "cos", "sin", "identf", "identb"))
    ropeT, vtm, fmT, sgT = io.get("ropeT"), io.get("vtm"), io.get("fmT"), io.get("sgT")
    idf = P.sb([128, 128], F32, "idf")
    idb = P.sb([128, 128], BF16, "idb")
    gxs = P.sb([128, 16], F32, "gxs")
    gns = P.sb([128, max(n_tm, 1)], F32, "gns")
    r_c = Res()
    P.dma(lambda e: e.dma_start(out=idf[:], in_=identf), writes=[r_c], queue="pool")
    P.dma(lambda e: e.dma_start(out=idb[:], in_=identb), writes=[r_c], queue="pool")
    P.dma(lambda e: e.dma_start(out=gxs[:], in_=gx), writes=[r_c], queue="pool")
    P.dma(lambda e: e.dma_start(out=gns[:], in_=gains), writes=[r_c], queue="pool")
    if io.get("pre_hook") is not None:
        io["pre_hook"]()
    r_xl = [io["r_x"]] if io.get("r_x") is not None else []
    wbf = P.sb([128, 16, ncols], BF16, "wbf")
    r_w = [Res() for _ in range(16)]
    stg = [P.sb([128, ncols], F32, "stg%d" % i) for i in range(2)]
    r_stg = [Res() for _ in range(2)]
    for kc in range(16):
        s = kc % 2
        P.dma(lambda e, kc=kc, s=s: e.dma_start(out=stg[s][:], in_=w[kc * 128:(kc + 1) * 128, :]),
              writes=[r_stg[s]])
        P.op("dve", lambda e, kc=kc, s=s: e.tensor_scalar(
            out=wbf[:, kc, :], in0=stg[s][:], scalar1=gxs[:, kc:kc + 1], scalar2=None, op0=ALU.mult),
            reads=[r_stg[s], r_c], writes=[r_w[kc]])

    xt = [P.sb([128, 2048], F32, "xt%d" % i) for i in range(2)]
    r_xt = [Res() for _ in range(2)]
    junk = P.sb([128, 2048], BF16, "junk")
    r_junk = Res()
    st1 = [P.sb([128, 4], F32, "st1_%d" % i) for i in range(2)]
    r_st1 = [Res() for _ in range(2)]
    xn = [P.sb([128, 2048], BF16, "xn%d" % i) for i in range(2)]
    r_xn = [Res() for _ in range(2)]
    xnT = [P.sb([128, 16, 512], BF16, "xnT%d" % i) for i in range(2)]
    r_xnT = [Res() for _ in range(2)]
    pT = [P.ps([128, 1024], BF16, "pT%d" % i) for i in range(2)]
    r_pT = [Res() for _ in range(2)]
    pacc = [P.ps([128, 512], F32, "pacc%d" % i) for i in range(4)]
    r_pacc = [Res() for _ in range(4)]
    ptr = [P.ps([128, 512], F32, "ptr%d" % i) for i in range(2)]
    r_ptr = [Res() for _ in range(2)]
    cs = [P.sb([128, 512], F32, "cs%d" % i) for i in range(2)]
    sn = [P.sb([128, 512], F32, "sn%d" % i) for i in range(2)]
    r_cs = [Res() for _ in range(2)]
    sq = P.sb([128, 512], F32, "sq")
    r_sq = Res()
    hst = P.sb([128, 8], F32, "hst")
    r_hst = Res()
    qn = P.sb([128, 512], F32, "qn")
    r_qn = Res()
    t1 = P.sb([128, 512], F32, "t1")
    t2 = P.sb([128, 512], F32, "t2")
    r_t = Res()
    nblk = len(tm_blocks)
    qr = [P.sb([128, 4, 512], F32, "qr%d" % i) for i in range(max(nblk, 1))]
    r_qr = [[Res() for _ in range(4)] for _ in range(max(nblk, 1))]
    vsb = [P.sb([128, 512], BF16, "vsb%d" % i) for i in range(2)]
    r_vsb = [Res() for _ in range(2)]
    qTs = [P.sb([128, 512], BF16, "qTs%d" % i) for i in range(2)]
    r_qTs = [Res() for _ in range(2)]
    fms = [P.sb([128, 512], BF16, "fms%d" % i) for i in range(2)]
    r_fms = [Res() for _ in range(2)]
    sgs = P.sb([128, 512], F32, "sgs")
    r_sgs = Res()

    cnt = {"pacc": 0, "ptr": 0, "vsb": 0, "qTs": 0, "fms": 0, "pT": 0, "x": 0}
    NG = S // 512
    def prep(G):
        gb = G % 2
        for st in range(4):
            tok0 = G * 512 + st * 128
            xb = cnt["x"] % 2
            cnt["x"] += 1
            P.dma(lambda e, xb=xb, tok0=tok0: e.dma_start(out=xt[xb][:], in_=x_tile(tok0 // 128)),
                  reads=r_xl, writes=[r_xt[xb]], queue="pool")
            P.op("act", lambda e, xb=xb: e.activation(
                out=junk[:], in_=xt[xb][:], func=AF.Square, accum_out=st1[xb][:, 0:1]),
                reads=[r_xt[xb]], writes=[r_junk, r_st1[xb]])
            P.op("act", lambda e, xb=xb: e.activation(
                out=st1[xb][:, 1:2], in_=st1[xb][:, 0:1], func=AF.Sqrt, bias=EPS, scale=1.0 / 2048),
                reads=[r_st1[xb]], writes=[r_st1[xb]])
            P.op("dve", lambda e, xb=xb: e.reciprocal(out=st1[xb][:, 2:3], in_=st1[xb][:, 1:2]),
                 reads=[r_st1[xb]], writes=[r_st1[xb]])
            P.op("dve", lambda e, xb=xb: e.tensor_scalar(
                out=xn[xb][:], in0=xt[xb][:], scalar1=st1[xb][:, 2:3], scalar2=None, op0=ALU.mult),
                reads=[r_xt[xb], r_st1[xb]], writes=[r_xn[xb]])
            for half in range(2):
                pb = cnt["pT"] % 2
                cnt["pT"] += 1
                for j in range(8):
                    kc = half * 8 + j
                    P.op("pe", lambda e, xb=xb, kc=kc, j=j, pb=pb: e.transpose(
                        out=pT[pb][:, j * 128:(j + 1) * 128], in_=xn[xb][:, kc * 128:(kc + 1) * 128],
                        identity=idb[:]), reads=[r_xn[xb], r_c], writes=[r_pT[pb]])
                eng = "act" if half == 0 else "dve"
                if eng == "act":
                    P.op("act", lambda e, pb=pb, half=half, st=st, gb=gb: e.copy(
                        out=xnT[gb][:, half * 8:(half + 1) * 8, st * 128:(st + 1) * 128],
                        in_=pT[pb][:].rearrange("p (j t) -> p j t", t=128)),
                        reads=[r_pT[pb]], writes=[r_xnT[gb]])
                else:
                    P.op("dve", lambda e, pb=pb, half=half, st=st, gb=gb: e.tensor_copy(
                        out=xnT[gb][:, half * 8:(half + 1) * 8, st * 128:(st + 1) * 128],
                        in_=pT[pb][:].rearrange("p (j t) -> p j t", t=128)),
                        reads=[r_pT[pb]], writes=[r_xnT[gb]])
    def proj(G):
        gb = G % 2
        col0 = 0
        for bi, (nr, nv) in enumerate(tm_blocks):
            bw = nr * 128 + nv
            for st in range(4):
                tok0 = G * 512 + st * 128
                pb = cnt["pacc"] % 4
                cnt["pacc"] += 1
                for kc in range(16):
                    P.op("pe", lambda e, kc=kc, st=st, gb=gb, pb=pb, col0=col0, bw=bw: e.matmul(
                        pacc[pb][:, 0:bw], lhsT=xnT[gb][:, kc, st * 128:(st + 1) * 128],
                        rhs=wbf[:, kc, col0:col0 + bw], start=(kc == 0), stop=(kc == 15)),
                        reads=[r_xnT[gb], r_w[kc]], writes=[r_pacc[pb]])
                if nr > 0:
                    rw = nr * 128
                    cb = (G * 4 + st) % 2
                    P.dma(lambda e, cb=cb, tok0=tok0: e.dma_start(out=cs[cb][:], in_=cosd[tok0:tok0 + 128, :]),
                          writes=[r_cs[cb]], queue="pool")
                    P.dma(lambda e, cb=cb, tok0=tok0: e.dma_start(out=sn[cb][:], in_=sind[tok0:tok0 + 128, :]),
                          writes=[r_cs[cb]], queue="pool")
                    P.op("act", lambda e, pb=pb, rw=rw: e.activation(
                        out=sq[:, 0:rw], in_=pacc[pb][:, 0:rw], func=AF.Square),
                        reads=[r_pacc[pb]], writes=[r_sq])
                    P.op("dve", lambda e, rw=rw, nr=nr: e.tensor_reduce(
                        out=hst[:, 0:nr], in_=sq[:, 0:rw].rearrange("p (h d) -> p h d", d=128),
                        axis=AX.X, op=ALU.add), reads=[r_sq], writes=[r_hst])
                    P.op("act", lambda e, nr=nr: e.activation(
                        out=hst[:, 4:4 + nr], in_=hst[:, 0:nr], func=AF.Sqrt, bias=EPS, scale=1.0 / 128),
                        reads=[r_hst], writes=[r_hst])
                    P.op("dve", lambda e, nr=nr: e.reciprocal(out=hst[:, 0:nr], in_=hst[:, 4:4 + nr]),
                         reads=[r_hst], writes=[r_hst])
                    for h in range(nr):
                        P.op("dve", lambda e, h=h, pb=pb, col0=col0: e.scalar_tensor_tensor(
                            out=qn[:, h * 128:(h + 1) * 128], in0=pacc[pb][:, h * 128:(h + 1) * 128],
                            scalar=hst[:, h:h + 1], in1=gns[:, col0 + h * 128:col0 + (h + 1) * 128],
                            op0=ALU.mult, op1=ALU.mult),
                            reads=[r_pacc[pb], r_hst, r_c], writes=[r_qn])
                    P.op("pool", lambda e, rw=rw, cb=cb: e.tensor_tensor(
                        out=t1[:, 0:rw], in0=qn[:, 0:rw], in1=cs[cb][:, 0:rw], op=ALU.mult),
                        reads=[r_qn, r_cs[cb]], writes=[r_t])
                    for hf in range(2):
                        P.op("pool", lambda e, rw=rw, cb=cb, hf=hf: e.tensor_tensor(
                            out=t2[:, 0:rw].rearrange("p (h two d) -> p h two d", two=2, d=64)[:, :, hf, :],
                            in0=qn[:, 0:rw].rearrange("p (h two d) -> p h two d", two=2, d=64)[:, :, 1 - hf, :],
                            in1=sn[cb][:, 0:rw].rearrange("p (h two d) -> p h two d", two=2, d=64)[:, :, hf, :],
                            op=ALU.mult), reads=[r_qn, r_cs[cb], r_t], writes=[r_t])
                    P.op("pool", lambda e, rw=rw, bi=bi, st=st: e.tensor_tensor(
                        out=qr[bi][:, st, 0:rw], in0=t1[:, 0:rw], in1=t2[:, 0:rw], op=ALU.add),
                        reads=[r_t], writes=[r_qr[bi][st]])
                if nv > 0:
                    vb = cnt["vsb"] % 2
                    cnt["vsb"] += 1
                    voff = sum(b for a, b in tm_blocks[:bi])
                    P.op("act", lambda e, vb=vb, pb=pb, nr=nr, nv=nv: e.copy(
                        out=vsb[vb][:, 0:nv], in_=pacc[pb][:, nr * 128:nr * 128 + nv]),
                        reads=[r_pacc[pb]], writes=[r_vsb[vb]])
                    P.dma(lambda e, vb=vb, tok0=tok0, voff=voff, nv=nv: e.dma_start(
                        out=vtm[tok0:tok0 + 128, voff:voff + nv], in_=vsb[vb][:, 0:nv]),
                        reads=[r_vsb[vb]])
            hoff = sum(a for a, b in tm_blocks[:bi])
            for h in range(nr):
                tb = cnt["ptr"] % 2
                cnt["ptr"] += 1
                for st in range(4):
                    P.op("pe", lambda e, bi=bi, st=st, h=h, tb=tb: e.transpose(
                        out=ptr[tb][:, st * 128:(st + 1) * 128], in_=qr[bi][:, st, h * 128:(h + 1) * 128],
                        identity=idf[:]), reads=[r_qr[bi][st], r_c], writes=[r_ptr[tb]])
                qb = cnt["qTs"] % 2
                cnt["qTs"] += 1
                P.op("act", lambda e, qb=qb, tb=tb: e.copy(out=qTs[qb][:], in_=ptr[tb][:]),
                     reads=[r_ptr[tb]], writes=[r_qTs[qb]])
                P.dma(lambda e, qb=qb, hh=hoff + h, G=G: e.dma_start(
                    out=ropeT[hh, :, G * 512:(G + 1) * 512], in_=qTs[qb][:]), reads=[r_qTs[qb]])
            col0 += bw
        for fi, func in enumerate(fm_funcs):
            pb = cnt["pacc"] % 4
            cnt["pacc"] += 1
            c0 = n_tm + fi * 128
            for kc in range(16):
                P.op("pe", lambda e, kc=kc, gb=gb, pb=pb, c0=c0: e.matmul(
                    pacc[pb][:], lhsT=wbf[:, kc, c0:c0 + 128], rhs=xnT[gb][:, kc, :],
                    start=(kc == 0), stop=(kc == 15)),
                    reads=[r_xnT[gb], r_w[kc]], writes=[r_pacc[pb]])
            fb = cnt["fms"] % 2
            cnt["fms"] += 1
            af = AF.Silu if func == "silu" else AF.Copy
            P.op("act", lambda e, fb=fb, pb=pb, af=af: e.activation(out=fms[fb][:], in_=pacc[pb][:], func=af),
                 reads=[r_pacc[pb]], writes=[r_fms[fb]])
            P.dma(lambda e, fb=fb, fi=fi, G=G: e.dma_start(
                out=fmT[fi * 128:(fi + 1) * 128, G * 512:(G + 1) * 512], in_=fms[fb][:]), reads=[r_fms[fb]])
        if n_sg > 0:
            pb = cnt["pacc"] % 4
            cnt["pacc"] += 1
            c0 = n_tm + NF
            for kc in range(16):
                P.op("pe", lambda e, kc=kc, gb=gb, pb=pb, c0=c0: e.matmul(
                    pacc[pb][0:n_sg, :], lhsT=wbf[:, kc, c0:c0 + n_sg], rhs=xnT[gb][:, kc, :],
                    start=(kc == 0), stop=(kc == 15)),
                    reads=[r_xnT[gb], r_w[kc]], writes=[r_pacc[pb]])
            P.op("act", lambda e, pb=pb: e.activation(out=sgs[0:n_sg, :], in_=pacc[pb][0:n_sg, :], func=AF.Sigmoid),
                 reads=[r_pacc[pb]], writes=[r_sgs])
            P.dma(lambda e, G=G: e.dma_start(out=sgT[0:n_sg, G * 512:(G + 1) * 512], in_=sgs[0:n_sg, :]),
                  reads=[r_sgs])

    prep(0)
    for G in range(NG):
        if G + 1 < NG:
            prep(G + 1)
        proj(G)


def build_proj(tm_blocks, fm_funcs, n_sg, S=SEQ):
    nc = bass.Bass("TRN2", target_bir_lowering=False)
    n_tm = sum(a * 128 + b for a, b in tm_blocks)
    NR = sum(a for a, b in tm_blocks)
    NV = sum(b for a, b in tm_blocks)
    NF = 128 * len(fm_funcs)
    ncols = n_tm + NF + n_sg
    x = nc.dram_tensor("x", [S, 2048], F32, kind="ExternalInput").ap()
    io = dict(
        x_tile=lambda T: x[T * 128:(T + 1) * 128, :],
        w=nc.dram_tensor("w", [2048, ncols], F32, kind="ExternalInput").ap(),
        gx=nc.dram_tensor("gx", [128, 16], F32, kind="ExternalInput").ap(),
        gains=nc.dram_tensor("gains", [128, max(n_tm, 1)], F32, kind="ExternalInput").ap(),
        cos=nc.dram_tensor("cos", [S, 512], F32, kind="ExternalInput").ap(),
        sin=nc.dram_tensor("sin", [S, 512], F32, kind="ExternalInput").ap(),
        identf=nc.dram_tensor("identf", [128, 128], F32, kind="ExternalInput").ap(),
        identb=nc.dram_tensor("identb", [128, 128], BF16, kind="ExternalInput").ap(),
        ropeT=nc.dram_tensor("ropeT", [max(NR, 1), 128, S], BF16, kind="ExternalOutput").ap(),
        vtm=nc.dram_tensor("vtm", [S, max(NV, 1)], BF16, kind="ExternalOutput").ap(),
        fmT=nc.dram_tensor("fmT", [max(NF, 1), S], BF16, kind="ExternalOutput").ap(),
        sgT=nc.dram_tensor("sgT", [max(n_sg, 1), S], F32, kind="ExternalOutput").ap())
    P = Prog(nc)
    emit_proj(P, tm_blocks, fm_funcs, n_sg, io, S)
    P.finish()
    return nc


def rope_tables(S=SEQ):
    half = HD // 2
    inv_freq = np.exp(-math.log(10000.0) * np.arange(half, dtype=np.float32) / half).astype(np.float32)
    ang = np.arange(S, dtype=np.float32)[:, None] * inv_freq[None, :]
    c = np.cos(ang).astype(np.float32)
    s = np.sin(ang).astype(np.float32)
    cos128 = np.concatenate([c, c], axis=1)
    sin128 = np.concatenate([-s, s], axis=1)
    return np.tile(cos128, (1, 4)), np.tile(sin128, (1, 4))


class AttnCtx:
    def __init__(self, P, QW, p_dt=BF16, n_s=3):
        self.P = P
        self.QW = QW
        self.S = [P.ps([128, 512], F32, "S%d" % i) for i in range(n_s)]
        self.r_S = [Res() for _ in range(n_s)]
        self.pt = [P.sb([128, QW], p_dt, "pt%d" % i) for i in range(4)]
        self.r_pt = [Res() for _ in range(4)]
        self.o = [P.ps([128, 512], F32, "o%d" % i) for i in range(2)]
        self.r_o = [Res() for _ in range(2)]
        assert 2 * QW <= 512
        self.den = [self.o[i][:, QW:2 * QW] for i in range(2)]
        self.r_den = [Res() for _ in range(2)]
        self.si = 0
        self.pi = 0
        self.oi = 0
        self.ones = P.sb([128, 128], p_dt, "ones")
        self.r_ones = Res()
        P.op("pool", lambda e: e.memset(self.ones[:], 1.0), writes=[self.r_ones])
        self.pt32 = None
        self.ones32 = P.sb([128, 128], F32, "ones32")
        P.op("pool", lambda e: e.memset(self.ones32[:], 1.0), writes=[self.r_ones])
        self.acc = [[P.sb([128, QW], F32, "acc%d_%d" % (i, j)) for j in range(2)] for i in range(2)]
        self.r_acc = [[Res() for j in range(2)] for i in range(2)]

    def add_fp32(self):
        P = self.P
        self.pt32 = [P.sb([128, self.QW], F32, "pt32_%d" % i) for i in range(4)]
        self.r_pt32 = [Res() for _ in range(4)]
        self.pi32 = 0


def softmax_branch(P, A, qT_ap, r_q, ktiles, maskeng="pool", fp32=False):
    QW = A.QW
    ob = A.oi % 2
    A.oi += 1
    n = len(ktiles)
    used = []
    info = []

    def stage1(i):
        kt = ktiles[i]
        nk = kt["nk"]
        sb_ = A.si % len(A.S)
        A.si += 1
        if fp32:
            pb = A.pi32 % 4
            A.pi32 += 1
            pts, r_pts, ones = A.pt32, A.r_pt32, A.ones32
        else:
            pb = A.pi % 4
            A.pi += 1
            pts, r_pts, ones = A.pt, A.r_pt, A.ones
        used.append(pb)
        info.append((pb, pts, r_pts, ones))
        bias = kt.get("bias")
        P.op("pe", lambda e, kt=kt, sb_=sb_, nk=nk, bias=bias: e.matmul(
            A.S[sb_][0:nk, 0:QW], lhsT=kt["kT"], rhs=qT_ap, start=True, stop=(bias is None)),
            reads=[kt["r_k"], r_q], writes=[A.r_S[sb_]])
        if bias is not None:
            P.op("pe", lambda e, sb_=sb_, nk=nk, bias=bias: e.matmul(
                A.S[sb_][0:nk, 0:QW], lhsT=bias[0], rhs=bias[1], start=False, stop=True),
                reads=list(bias[2]), writes=[A.r_S[sb_]])
        P.op("act", lambda e, sb_=sb_, pb=pb, nk=nk, pts=pts: e.activation(
            out=pts[pb][0:nk, :], in_=A.S[sb_][0:nk, 0:QW], func=AF.Exp, scale=SCALE),
            reads=[A.r_S[sb_]], writes=[r_pts[pb]])
        mask = kt.get("mask")
        if mask is not None:
            P.op(maskeng, lambda e, pb=pb, nk=nk, mask=mask, pts=pts: e.tensor_tensor(
                out=pts[pb][0:nk, :], in0=pts[pb][0:nk, :], in1=mask[0], op=ALU.mult),
                reads=[r_pts[pb], mask[1]], writes=[r_pts[pb]])

    def stage2(i):
        kt = ktiles[i]
        nk = kt["nk"]
        pb, pts, r_pts, ones = info[i]
        P.op("pe", lambda e, kt=kt, pb=pb, nk=nk, i=i, pts=pts: e.matmul(
            A.o[ob][:, 0:QW], lhsT=kt["v"], rhs=pts[pb][0:nk, :], start=(i == 0), stop=(i == n - 1)),
            reads=[kt["r_v"], r_pts[pb]], writes=[A.r_o[ob]])
        j = i % 2
        eng = "dve" if j == 0 else "pool"
        acc_nk[j] = nk
        if i < 2:
            P.op(eng, lambda e, pb=pb, nk=nk, pts=pts, j=j: e.tensor_copy(out=A.acc[ob][j][0:nk, :], in_=pts[pb][0:nk, :]),
                 reads=[r_pts[pb]], writes=[A.r_acc[ob][j]])
        else:
            P.op(eng, lambda e, pb=pb, nk=nk, pts=pts, j=j: e.tensor_tensor(
                out=A.acc[ob][j][0:nk, :], in0=A.acc[ob][j][0:nk, :], in1=pts[pb][0:nk, :], op=ALU.add),
                reads=[r_pts[pb], A.r_acc[ob][j]], writes=[A.r_acc[ob][j]])

    acc_nk = {}
    for step in range(n + 2):
        if step < n:
            stage1(step)
        if step >= 2:
            stage2(step - 2)
    js = sorted(acc_nk)
    for t, j in enumerate(js):
        nkj = acc_nk[j]
        P.op("pe", lambda e, j=j, nkj=nkj, t=t: e.matmul(
            A.den[ob][:, 0:QW], lhsT=A.ones32[0:nkj, :], rhs=A.acc[ob][j][0:nkj, :],
            start=(t == 0), stop=(t == len(js) - 1)),
            reads=[A.r_ones, A.r_acc[ob][j]], writes=[A.r_den[ob]])
    return A.o[ob], A.r_o[ob], A.den[ob], A.r_den[ob], used, None


def emit_moba(P, io, S=SEQ):
    ropeT, vtm, gT, identf, eseld, cmaskd, ywrite = (io[k] for k in ("ropeT", "vtm", "gT", "identf", "esel", "cmask", "ywrite"))
    QW = 256
    A = AttnCtx(P, QW, n_s=4)
    idf = P.sb([128, 128], F32, "idf")
    esel = P.sb([16, 16 * 128], BF16, "esel")
    cmask = P.sb([128, 512], BF16, "cmask")
    r_c = Res()
    P.dma(lambda e: e.dma_start(out=idf[:], in_=identf), writes=[r_c], queue="pool")
    P.dma(lambda e: e.dma_start(out=esel[:], in_=eseld), writes=[r_c], queue="pool")
    P.dma(lambda e: e.dma_start(out=cmask[:], in_=cmaskd), writes=[r_c], queue="pool")
    qT = [P.sb([128, S], BF16, "qT%d" % i) for i in range(2)]
    kT = [P.sb([128, S], BF16, "kT%d" % i) for i in range(2)]
    vv = [P.sb([128, S // 128, 128], BF16, "vv%d" % i) for i in range(2)]
    r_q = [Res() for _ in range(2)]
    r_k = [Res() for _ in range(2)]
    r_v = [Res() for _ in range(2)]
    km32 = P.sb([128, 16], F32, "km32")
    kmb = P.sb([128, 16], BF16, "kmb")
    r_km = Res()
    gps = P.ps([128, 512], F32, "gps")
    r_gps = Res()
    tps = P.ps([128, 512], F32, "tps")
    r_tps = Res()
    gm = P.sb([128, 16], F32, "gm")
    top8 = P.sb([128, 8], F32, "top8")
    bia = P.sb([128, 16], F32, "bia")
    r_gm = Res()
    biasT = [P.sb([16, S], BF16, "biasT%d" % i) for i in range(2)]
    r_bT = [Res() for _ in range(2)]
    gt = [P.sb([128, QW], BF16, "gt%d" % i) for i in range(2)]
    r_gt = [Res() for _ in range(2)]
    rd = P.sb([128, QW], F32, "rd")
    r_rd = Res()
    yo = P.sb([128, QW], F32, "yo")
    r_yo = Res()
    yb = [P.sb([128, QW], BF16, "yb%d" % i) for i in range(2)]
    r_yb = [Res() for _ in range(2)]
    cnt = 0
    for h in range(4):
        hb = h % 2
        P.dma(lambda e, h=h, hb=hb: e.dma_start(out=qT[hb][:], in_=ropeT[h]), writes=[r_q[hb]])
        P.dma(lambda e, h=h, hb=hb: e.dma_start(out=kT[hb][:], in_=ropeT[4 + h]), writes=[r_k[hb]])
        P.dma(lambda e, h=h, hb=hb: e.dma_start(
            out=vv[hb][:], in_=vtm[:, h * 128:(h + 1) * 128].rearrange("(t p) d -> p t d", p=128)),
            writes=[r_v[hb]])
        P.op("dve", lambda e, hb=hb: e.tensor_reduce(
            out=km32[:, 0:S // 256], in_=kT[hb][:].rearrange("p (n k) -> p n k", k=256), axis=AX.X, op=ALU.add),
            reads=[r_k[hb]], writes=[r_km])
        P.op("dve", lambda e: e.tensor_scalar(out=kmb[:, 0:S // 256], in0=km32[:, 0:S // 256], scalar1=1.0 / 256,
                                              scalar2=None, op0=ALU.mult), reads=[r_km], writes=[r_km])
        for qg in range(S // 512):
            for j in range(4):
                qi = qg * 4 + j
                cur = qi // 2
                P.op("pe", lambda e, qi=qi, hb=hb: e.matmul(
                    gps[:, 0:S // 256], lhsT=qT[hb][:, qi * 128:(qi + 1) * 128], rhs=kmb[:, 0:S // 256],
                    start=True, stop=True),
                    reads=[r_q[hb], r_km], writes=[r_gps])
                P.op("dve", lambda e: e.memset(gm[:], -1e30), writes=[r_gm])
                if cur > 0:
                    P.op("dve", lambda e, cur=cur: e.tensor_copy(out=gm[:, 0:cur], in_=gps[:, 0:cur]),
                         reads=[r_gps], writes=[r_gm])
                P.op("dve", lambda e: e.max(out=top8[:], in_=gm[:]), reads=[r_gm], writes=[r_gm])
                P.op("dve", lambda e: e.tensor_scalar(
                    out=bia[:], in0=gm[:], scalar1=top8[:, 2:3], scalar2=-30000.0, op0=ALU.is_lt, op1=ALU.mult),
                    reads=[r_gm], writes=[r_gm])
                P.op("pe", lambda e, j=j: e.transpose(
                    out=tps[0:16, j * 128:(j + 1) * 128], in_=bia[:], identity=idf[:]),
                    reads=[r_gm, r_c], writes=[r_tps])
            P.op("act", lambda e, qg=qg, hb=hb: e.copy(out=biasT[hb][:, qg * 512:(qg + 1) * 512], in_=tps[0:16, :]),
                 reads=[r_tps], writes=[r_bT[hb]])
        for c in range(S // QW):
            ktiles = []
            for kt in range(2 * c + 2):
                d = dict(kT=kT[hb][:, kt * 128:(kt + 1) * 128], r_k=r_k[hb], v=vv[hb][:, kt, :], r_v=r_v[hb], nk=128)
                n = kt // 2
                if n < c:
                    d["bias"] = (esel[:, n * 128:(n + 1) * 128], biasT[hb][:, c * QW:(c + 1) * QW], [r_c, r_bT[hb]])
                else:
                    d["mask"] = (cmask[:, (kt - 2 * c) * 256:(kt - 2 * c + 1) * 256], r_c)
                ktiles.append(d)
            gb = cnt % 2
            cnt += 1
            P.dma(lambda e, gb=gb, h=h, c=c: e.dma_start(
                out=gt[gb][:], in_=gT[h * 128:(h + 1) * 128, c * QW:(c + 1) * QW]), writes=[r_gt[gb]], queue="pool")
            o_ps, r_o, d_ps, r_d, _, _ = softmax_branch(P, A, qT[hb][:, c * QW:(c + 1) * QW], r_q[hb], ktiles)
            P.op("dve", lambda e, d_ps=d_ps: e.reciprocal(out=rd[:], in_=d_ps[:, 0:QW]),
                 reads=[r_d], writes=[r_rd])
            P.op("dve", lambda e, o_ps=o_ps: e.tensor_tensor(out=yo[:], in0=o_ps[:, 0:QW], in1=rd[:], op=ALU.mult),
                 reads=[r_o, r_rd], writes=[r_yo])
            P.op("pool", lambda e, gb=gb: e.tensor_tensor(out=yb[gb][:], in0=yo[:], in1=gt[gb][:], op=ALU.mult),
                 reads=[r_yo, r_gt[gb]], writes=[r_yb[gb]])
            P.dma(lambda e, gb=gb, h=h, c=c: e.dma_start(out=ywrite(h, c * QW, QW), in_=yb[gb][:]), reads=[r_yb[gb]])


def moba_consts():
    esel = np.zeros((16, 16 * 128), np.float32)
    for n in range(16):
        esel[n, n * 128:(n + 1) * 128] = 1.0
    k = np.arange(128)[:, None]
    q = np.arange(256)[None, :]
    cm = np.concatenate([(k <= q), (k + 128 <= q)], axis=1).astype(np.float32)
    return {"identf": np.eye(128, dtype=np.float32), "esel": esel.astype(NPBF), "cmask": cm.astype(NPBF)}


def emit_sb(P, io, S=SEQ):
    QW = 512
    fmT, vtm, trid, dmaskd, ywrite = (io[k] for k in ("fmT", "vtm", "tri", "dmask", "ywrite"))
    tri = P.sb([128, 128], BF16, "tri")
    ones = P.sb([128, 128], BF16, "ones")
    dmask = P.sb([128, 4 * QW], BF16, "dmask")
    r_c = Res()
    P.dma(lambda e: e.dma_start(out=tri[:], in_=trid), writes=[r_c], queue="pool")
    P.dma(lambda e: e.dma_start(out=dmask[:], in_=dmaskd), writes=[r_c], queue="pool")
    P.op("pool", lambda e: e.memset(ones[:], 1.0), writes=[r_c])
    qT = [P.sb([128, S], BF16, "qT%d" % i) for i in range(2)]
    kT = [P.sb([128, S], BF16, "kT%d" % i) for i in range(2)]
    vv = [P.sb([128, S // 128, 128], BF16, "vv%d" % i) for i in range(2)]
    r_q = [Res() for _ in range(2)]
    r_k = [Res() for _ in range(2)]
    r_v = [Res() for _ in range(2)]
    zps = [P.ps([128, QW], F32, "z%d" % i) for i in range(3)]
    r_z = [Res() for _ in range(3)]
    ups = [P.ps([128, QW], F32, "u%d" % i) for i in range(2)]
    r_u = [Res() for _ in range(2)]
    wps = [P.ps([128, QW], F32, "w%d" % i) for i in range(2)]
    r_wp = [Res() for _ in range(2)]
    ops_ = [P.ps([128, QW], F32, "o0")] * 2
    r_o = [Res()] * 2
    NBUF = 3
    ex = [P.sb([128, QW], F32, "ex%d" % i) for i in range(NBUF)]
    r_ex = [Res() for _ in range(NBUF)]
    sp = [P.sb([128, QW], F32, "sp%d" % i) for i in range(NBUF)]
    r_sp = [Res() for _ in range(NBUF)]
    hi = [P.sb([128, QW], BF16, "hi%d" % i) for i in range(NBUF)]
    lo = [P.sb([128, QW], BF16, "lo%d" % i) for i in range(NBUF)]
    r_hl = [Res() for _ in range(NBUF)]
    tt = [P.sb([128, QW], F32, "tt%d" % i) for i in range(2)]
    r_tt = [Res() for _ in range(2)]
    aa = [P.sb([128, QW], BF16, "aa%d" % i) for i in range(2)]
    r_aa = [Res() for _ in range(2)]
    C = [P.sb([128, QW], F32, "C%d" % i) for i in range(2)]
    r_C = [Res() for _ in range(2)]
    gt = [P.sb([128, QW], BF16, "gt%d" % i) for i in range(2)]
    r_gt = [Res() for _ in range(2)]
    yb = [P.sb([128, QW], BF16, "yb%d" % i) for i in range(2)]
    r_yb = [Res() for _ in range(2)]
    ti = 0
    qc = 0
    for h in range(4):
        hb = h % 2
        P.dma(lambda e, h=h, hb=hb: e.dma_start(out=qT[hb][:], in_=fmT[h * 128:(h + 1) * 128, :]), writes=[r_q[hb]])
        P.dma(lambda e, h=h, hb=hb: e.dma_start(out=kT[hb][:], in_=fmT[512 + h * 128:512 + (h + 1) * 128, :]),
              writes=[r_k[hb]])
        P.dma(lambda e, h=h, hb=hb: e.dma_start(
            out=vv[hb][:], in_=vtm[:, h * 128:(h + 1) * 128].rearrange("(t p) d -> p t d", p=128)),
            writes=[r_v[hb]])
        for c in range(S // QW):
            cb = qc % 2
            qc += 1
            P.dma(lambda e, cb=cb, h=h, c=c: e.dma_start(
                out=gt[cb][:], in_=fmT[1024 + h * 128:1024 + (h + 1) * 128, c * QW:(c + 1) * QW]),
                writes=[r_gt[cb]], queue="pool")
            P.op("pool", lambda e, cb=cb: e.memset(C[cb][:], 0.0), writes=[r_C[cb]])
            nkt = (c + 1) * (QW // 128)
            kts = list(range(nkt - 1, -1, -1))
            bufs = []

            def stage1a(i, hb=hb, c=c):
                nonlocal ti
                kt = kts[i]
                b = ti % 3
                ti += 1
                bufs.append(b)
                dj = kt - c * (QW // 128)
                P.op("pe", lambda e, b=b, kt=kt: e.matmul(
                    zps[b][:], lhsT=kT[hb][:, kt * 128:(kt + 1) * 128], rhs=qT[hb][:, c * QW:(c + 1) * QW],
                    start=True, stop=True), reads=[r_k[hb], r_q[hb]], writes=[r_z[b]])
                P.op("act", lambda e, b=b: e.activation(out=ex[b][:], in_=zps[b][:], func=AF.Exp, scale=SCALE),
                     reads=[r_z[b]], writes=[r_ex[b]])
                P.op("act", lambda e, b=b: e.activation(out=sp[b][:], in_=ex[b][:], func=AF.Ln, bias=1.0),
                     reads=[r_ex[b]], writes=[r_sp[b]])
                if dj >= 0:
                    P.op("pool", lambda e, b=b, dj=dj: e.tensor_tensor(
                        out=sp[b][:], in0=sp[b][:], in1=dmask[:, dj * QW:(dj + 1) * QW], op=ALU.mult),
                        reads=[r_sp[b], r_c], writes=[r_sp[b]])

            def stage1b(i):
                b = bufs[i]
                u = i % 2
                P.op("act", lambda e, b=b: e.copy(out=hi[b][:], in_=sp[b][:]),
                     reads=[r_sp[b]], writes=[r_hl[b]])
                P.op("pool", lambda e, b=b: e.tensor_tensor(out=lo[b][:], in0=sp[b][:], in1=hi[b][:], op=ALU.subtract),
                     reads=[r_sp[b], r_hl[b]], writes=[r_hl[b]])
                P.op("pe", lambda e, b=b, u=u: e.matmul(ups[u][:], lhsT=tri[:], rhs=hi[b][:], start=True, stop=False),
                     reads=[r_c, r_hl[b]], writes=[r_u[u]])
                P.op("pe", lambda e, b=b, u=u: e.matmul(ups[u][:], lhsT=tri[:], rhs=lo[b][:], start=False, stop=True),
                     reads=[r_c, r_hl[b]], writes=[r_u[u]])
                P.op("pe", lambda e, b=b, u=u: e.matmul(wps[u][:], lhsT=ones[:], rhs=hi[b][:], start=True, stop=False),
                     reads=[r_c, r_hl[b]], writes=[r_wp[u]])
                P.op("pe", lambda e, b=b, u=u: e.matmul(wps[u][:], lhsT=ones[:], rhs=lo[b][:], start=False, stop=True),
                     reads=[r_c, r_hl[b]], writes=[r_wp[u]])

            def stage2(i, hb=hb, c=c, cb=cb, nkt=nkt):
                kt = kts[i]
                b = bufs[i]
                u = i % 2
                dj = kt - c * (QW // 128)
                P.op("dve", lambda e, u=u: e.tensor_tensor(out=tt[u][:], in0=ups[u][:], in1=C[cb][:], op=ALU.add),
                     reads=[r_u[u], r_C[cb]], writes=[r_tt[u]])
                P.op("dve", lambda e, b=b, u=u: e.scalar_tensor_tensor(
                    out=tt[u][:], in0=zps[b][:], scalar=SCALE, in1=tt[u][:], op0=ALU.mult, op1=ALU.subtract),
                    reads=[r_z[b], r_tt[u]], writes=[r_tt[u]])
                P.op("act", lambda e, u=u: e.activation(out=aa[u][:], in_=tt[u][:], func=AF.Exp),
                     reads=[r_tt[u]], writes=[r_aa[u]])
                if dj >= 0:
                    P.op("pool", lambda e, u=u, dj=dj: e.tensor_tensor(
                        out=aa[u][:], in0=aa[u][:], in1=dmask[:, dj * QW:(dj + 1) * QW], op=ALU.mult),
                        reads=[r_aa[u], r_c], writes=[r_aa[u]])
                if kt > 0:
                    P.op("dve", lambda e, u=u: e.tensor_tensor(
                        out=C[cb][:], in0=wps[u][:], in1=C[cb][:], op=ALU.add),
                        reads=[r_wp[u], r_C[cb]], writes=[r_C[cb]])
                P.op("pe", lambda e, u=u, kt=kt, i=i: e.matmul(
                    ops_[cb][:], lhsT=vv[hb][:, kt, :], rhs=aa[u][:], start=(i == 0), stop=(i == nkt - 1)),
                    reads=[r_v[hb], r_aa[u]], writes=[r_o[cb]])

            for step in range(nkt + 2):
                if step < nkt:
                    stage1a(step)
                if 1 <= step <= nkt:
                    stage1b(step - 1)
                if step >= 2:
                    stage2(step - 2)
            P.op("dve", lambda e, cb=cb: e.tensor_tensor(out=yb[cb][:], in0=ops_[cb][:], in1=gt[cb][:], op=ALU.mult),
                 reads=[r_o[cb], r_gt[cb]], writes=[r_yb[cb]])
            P.dma(lambda e, cb=cb, h=h, c=c: e.dma_start(out=ywrite(h, c * QW, QW), in_=yb[cb][:]), reads=[r_yb[cb]])


def sb_consts():
    kp = np.arange(128)[:, None]
    k = np.arange(128)[None, :]
    tri = (kp >= k).astype(np.float32)
    kk = np.arange(128)[:, None]
    q = np.arange(512)[None, :]
    dm = np.concatenate([(128 * j + kk < q) for j in range(4)], axis=1).astype(np.float32)
    return {"tri": tri.astype(NPBF), "dmask": dm.astype(NPBF)}


def emit_nsa(P, io, S=SEQ):
    QW = 256
    NCMP = (S - 32) // 16 + 1
    NSLC = S // 64
    NKT = S // 128
    ntl = [(0, min(128, NCMP))] + ([(1, NCMP - 128)] if NCMP > 128 else [])
    (ropeT, vtm, fmT, sgT, wkd, wvd, posd, kcgd, coscd, sincd, identf, ovld, maddd, cmpmd, e64d, selgd, cmaskd,
     wmaskd, ywrite) = (io[k] for k in ("ropeT", "vtm", "fmT", "sgT", "wk", "wv", "posT", "kcg", "cosc", "sinc", "identf",
                                        "ovl", "madd", "cmpm", "e64", "selg", "cmask", "wmask", "ywrite"))
    A = AttnCtx(P, QW, n_s=4)
    A.add_fp32()
    gbc = P.ps([128, 512], F32, "gbc")
    r_gbc = Res()
    misc = P.ps([128, 512], F32, "misc")
    r_misc = Res()
    r_c = Res()
    idf = P.sb([128, 128], F32, "idf")
    ovl = P.sb([128, 2, NSLC], F32, "ovl_s")
    e64 = P.sb([NSLC, NKT * 128], BF16, "e64_s")
    selg = P.sb([12, 12 * 128], F32, "selg_s")
    cmask = P.sb([128, 512], BF16, "cmask_s")
    wmask = P.sb([128, 512], BF16, "wmask_s")
    kcg = P.sb([128, 128], F32, "kcg_s")
    posT = P.sb([128, 32], F32, "posT_s")
    cosc = P.sb([128, 2, 128], F32, "cosc_s")
    sinc = P.sb([128, 2, 128], F32, "sinc_s")
    sg = [P.sb([12, QW], F32, "sg_s%d" % i) for i in range(2)]
    r_sg = [Res() for _ in range(2)]
    for dst, srcd in ((idf[:], identf), (ovl[:], ovld), (e64[:], e64d), (selg[:], selgd), (cmask[:], cmaskd),
                      (wmask[:], wmaskd), (kcg[:], kcgd), (posT[:], posd),
                      (cosc[:], coscd.rearrange("(t p) d -> p t d", p=128)),
                      (sinc[:], sincd.rearrange("(t p) d -> p t d", p=128))):
        P.dma(lambda e, dst=dst, srcd=srcd: e.dma_start(out=dst, in_=srcd), writes=[r_c], queue="pool")
    qT = P.sb([128, 4, S], BF16, "qT")
    ksT = P.sb([128, S], BF16, "ksT")
    kwT = P.sb([128, S], BF16, "kwT")
    vs = P.sb([128, NKT, 128], BF16, "vs")
    vw = P.sb([128, NKT, 128], BF16, "vw")
    r_in = Res()
    for h in range(4):
        P.dma(lambda e, h=h: e.dma_start(out=qT[:, h, :], in_=ropeT[h]), writes=[r_in])
    P.dma(lambda e: e.dma_start(out=ksT[:], in_=ropeT[4]), writes=[r_in])
    P.dma(lambda e: e.dma_start(out=kwT[:], in_=ropeT[5]), writes=[r_in])
    P.dma(lambda e: e.dma_start(out=vs[:], in_=vtm[:, 0:128].rearrange("(t p) d -> p t d", p=128)), writes=[r_in])
    P.dma(lambda e: e.dma_start(out=vw[:], in_=vtm[:, 128:256].rearrange("(t p) d -> p t d", p=128)), writes=[r_in])

    kcT = P.sb([128, S], BF16, "kcT")
    vcT = P.sb([128, S], BF16, "vcT")
    r_kv = Res()
    P.dma(lambda e: e.dma_start(out=kcT[:], in_=fmT[0:128, :]), writes=[r_kv])
    P.dma(lambda e: e.dma_start(out=vcT[:], in_=fmT[128:256, :]), writes=[r_kv])
    wst = [P.sb([128, 8, 128], F32, "wst%d" % i) for i in range(2)]
    r_wst = [Res() for _ in range(2)]
    wkb = P.sb([128, 32, 128], BF16, "wkb")
    wvb = P.sb([128, 32, 128], BF16, "wvb")
    r_wb = Res()
    wi = 0
    for (wd_, wb_) in ((wkd, wkb), (wvd, wvb)):
        for ch in range(4):
            s_ = wi % 2
            wi += 1
            P.dma(lambda e, s_=s_, wd_=wd_, ch=ch: e.dma_start(out=wst[s_][:], in_=wd_[:, ch * 8:(ch + 1) * 8, :]),
                  writes=[r_wst[s_]])
            P.op("dve", lambda e, s_=s_, wb_=wb_, ch=ch: e.tensor_copy(out=wb_[:, ch * 8:(ch + 1) * 8, :], in_=wst[s_][:]),
                 reads=[r_wst[s_]], writes=[r_wb])
    kcp = P.sb([128, 32, 256], BF16, "kcp")
    vcp = kcp
    r_cp = Res()

    def build_cp(src_):
        for l in range(32):
            eng = "dve" if l % 2 == 0 else "pool"
            P.op(eng, lambda e, l=l: e.tensor_scalar(
                out=kcp[:, l, 0:NCMP], in0=src_[:, l:l + 16 * (NCMP - 1) + 1:16], scalar1=posT[:, l:l + 1],
                scalar2=None, op0=ALU.add), reads=[r_kv, r_c], writes=[r_cp])
    kccT = P.sb([128, 256], BF16, "kccT")
    r_kcc = Res()
    vcc = P.sb([128, 2, 128], F32, "vcc")
    r_vcc = Res()
    csq = P.sb([128, 128], F32, "csq")
    cst = P.sb([128, 4], F32, "cst")
    cqn = P.sb([128, 128], F32, "cqn")
    ct1 = P.sb([128, 128], F32, "ct1")
    ct2 = P.sb([128, 128], F32, "ct2")
    ckr = P.sb([128, 128], F32, "ckr")
    r_cw = Res()
    build_cp(kcT)
    for nt, nk in ntl:
        for l in range(32):
            P.op("pe", lambda e, l=l, nt=nt, nk=nk: e.matmul(
                misc[0:nk, 0:128], lhsT=kcp[:, l, nt * 128:nt * 128 + nk], rhs=wkb[:, l, :],
                start=(l == 0), stop=(l == 31)), reads=[r_cp, r_wb], writes=[r_misc])
        P.op("act", lambda e, nk=nk: e.activation(out=csq[0:nk, :], in_=misc[0:nk, 0:128], func=AF.Square,
                                                  accum_out=cst[0:nk, 0:1]), reads=[r_misc], writes=[r_cw])
        P.op("act", lambda e, nk=nk: e.activation(out=cst[0:nk, 1:2], in_=cst[0:nk, 0:1], func=AF.Sqrt, bias=EPS,
                                                  scale=1.0 / 128), reads=[r_cw], writes=[r_cw])
        P.op("dve", lambda e, nk=nk: e.reciprocal(out=cst[0:nk, 2:3], in_=cst[0:nk, 1:2]), reads=[r_cw], writes=[r_cw])
        P.op("dve", lambda e, nk=nk: e.scalar_tensor_tensor(
            out=cqn[0:nk, :], in0=misc[0:nk, 0:128], scalar=cst[0:nk, 2:3], in1=kcg[0:nk, :],
            op0=ALU.mult, op1=ALU.mult), reads=[r_misc, r_cw, r_c], writes=[r_cw])
        P.op("dve", lambda e, nk=nk, nt=nt: e.tensor_tensor(out=ct1[0:nk, :], in0=cqn[0:nk, :], in1=cosc[0:nk, nt, :],
                                                            op=ALU.mult), reads=[r_cw, r_c], writes=[r_cw])
        for hf in range(2):
            P.op("dve", lambda e, nk=nk, nt=nt, hf=hf: e.tensor_tensor(
                out=ct2[0:nk, hf * 64:(hf + 1) * 64], in0=cqn[0:nk, (1 - hf) * 64:(2 - hf) * 64],
                in1=sinc[0:nk, nt, hf * 64:(hf + 1) * 64], op=ALU.mult), reads=[r_cw, r_c], writes=[r_cw])
        P.op("dve", lambda e, nk=nk: e.tensor_tensor(out=ckr[0:nk, :], in0=ct1[0:nk, :], in1=ct2[0:nk, :], op=ALU.add),
             reads=[r_cw], writes=[r_cw])
        P.op("pe", lambda e, nk=nk: e.transpose(out=misc[:, 128:128 + nk], in_=ckr[0:nk, :], identity=idf[0:nk, 0:nk]),
             reads=[r_cw, r_c], writes=[r_misc])
        P.op("act", lambda e, nk=nk, nt=nt: e.copy(out=kccT[:, nt * 128:nt * 128 + nk], in_=misc[:, 128:128 + nk]),
             reads=[r_misc], writes=[r_kcc])
    build_cp(vcT)
    for nt, nk in ntl:
        for l in range(32):
            P.op("pe", lambda e, l=l, nt=nt, nk=nk: e.matmul(
                misc[0:nk, 256:384], lhsT=vcp[:, l, nt * 128:nt * 128 + nk], rhs=wvb[:, l, :],
                start=(l == 0), stop=(l == 31)), reads=[r_cp, r_wb], writes=[r_misc])
        P.op("act", lambda e, nk=nk, nt=nt: e.copy(out=vcc[0:nk, nt, :], in_=misc[0:nk, 256:384]),
             reads=[r_misc], writes=[r_vcc])


    cmT = [P.sb([128, 2, QW], BF16, "cmT%d" % i) for i in range(2)]
    r_cm = [Res() for _ in range(2)]
    madd = [P.sb([128, 2, NSLC], F32, "madd%d" % i) for i in range(2)]
    r_madd = [Res() for _ in range(2)]
    gt = [P.sb([128, 4, QW], BF16, "gt%d" % i) for i in range(2)]
    r_gt = [Res() for _ in range(2)]
    psumT = P.sb([128, 2, QW], F32, "psumT")
    r_psT = Res()
    pn = P.sb([128, QW], F32, "pn")
    r_pn = Res()
    rd = P.sb([128, QW], F32, "rd")
    r_rd = Res()
    tb = P.sb([128, QW], F32, "tb")
    r_tb = Res()
    yacc = [P.sb([128, QW], F32, "yacc%d" % i) for i in range(4)]
    r_ya = [Res() for _ in range(4)]
    impm = P.sb([128, NSLC], F32, "impm")
    imp2 = P.sb([128, NSLC], F32, "imp2")
    t8a = P.sb([128, 8], F32, "t8a")
    t8b = P.sb([128, 8], F32, "t8b")
    bia = P.sb([128, NSLC], F32, "bia")
    r_sel = Res()
    biasT = P.sb([NSLC, QW], BF16, "biasT")
    r_bT = Res()
    yb = [P.sb([128, QW], BF16, "yb%d" % i) for i in range(2)]
    r_yb = [Res() for _ in range(2)]
    ybi = 0

    def finish_branch(hl, br, o_ps, r_o, d_ps, r_d, c, first, clamp=False):
        if clamp:
            P.op("dve", lambda e: e.tensor_scalar(out=rd[:], in0=d_ps[:, 0:QW], scalar1=1e-30, scalar2=None,
                                                  op0=ALU.max), reads=[r_d], writes=[r_rd])
            P.op("dve", lambda e: e.reciprocal(out=rd[:], in_=rd[:]), reads=[r_rd], writes=[r_rd])
        else:
            P.op("dve", lambda e: e.reciprocal(out=rd[:], in_=d_ps[:, 0:QW]), reads=[r_d], writes=[r_rd])
        P.op("dve", lambda e: e.tensor_tensor(out=tb[:], in0=o_ps[:, 0:QW], in1=rd[:], op=ALU.mult),
             reads=[r_o, r_rd], writes=[r_tb])
        row = br * 4 + hl
        P.op("pe", lambda e: e.matmul(gbc[:, 0:QW], lhsT=selg[:, row * 128:(row + 1) * 128],
                                      rhs=sg[c % 2][:], start=True, stop=True),
             reads=[r_c, r_sg[c % 2]], writes=[r_gbc])
        if first:
            P.op("dve", lambda e: e.tensor_tensor(out=yacc[hl][:], in0=gbc[:, 0:QW], in1=tb[:], op=ALU.mult),
                 reads=[r_gbc, r_tb], writes=[r_ya[hl]])
        else:
            P.op("dve", lambda e: e.tensor_tensor(out=tb[:], in0=gbc[:, 0:QW], in1=tb[:], op=ALU.mult),
                 reads=[r_gbc, r_tb], writes=[r_tb])
            P.op("pool", lambda e: e.tensor_tensor(out=yacc[hl][:], in0=yacc[hl][:], in1=tb[:], op=ALU.add),
                 reads=[r_tb, r_ya[hl]], writes=[r_ya[hl]])

    for c in range(S // QW):
        cb = c % 2
        P.dma(lambda e, cb=cb, c=c: e.dma_start(
            out=cmT[cb][:], in_=cmpmd[:, c * QW:(c + 1) * QW].rearrange("(t p) q -> p t q", p=128)),
            writes=[r_cm[cb]], queue="pool")
        P.dma(lambda e, cb=cb, c=c: e.dma_start(out=sg[cb][:], in_=sgT[:, c * QW:(c + 1) * QW]),
              writes=[r_sg[cb]], queue="pool")
        P.dma(lambda e, cb=cb, c=c: e.dma_start(
            out=madd[cb][:], in_=maddd[c * QW:(c + 1) * QW, :].rearrange("(t p) j -> p t j", p=128)),
            writes=[r_madd[cb]], queue="pool")
        P.dma(lambda e, cb=cb, c=c: e.dma_start(
            out=gt[cb][:], in_=fmT[256:768, c * QW:(c + 1) * QW].rearrange("(h p) q -> p h q", p=128)),
            writes=[r_gt[cb]], queue="pool")
        for hl in range(4):
            ktiles = []
            for nt, nk in ntl:
                ktiles.append(dict(kT=kccT[:, nt * 128:nt * 128 + nk], r_k=r_kcc, v=vcc[0:nk, nt, :], r_v=r_vcc, nk=nk,
                                   mask=(cmT[cb][0:nk, nt, :], r_cm[cb])))
            o_ps, r_o, d_ps, r_d, used, _ = softmax_branch(P, A, qT[:, hl, c * QW:(c + 1) * QW], r_in, ktiles, fp32=True)
            finish_branch(hl, 0, o_ps, r_o, d_ps, r_d, c, first=True, clamp=True)
            for nt, nk in ntl:
                pb = used[nt]
                if hl == 0:
                    P.op("pool", lambda e, nt=nt, nk=nk, pb=pb: e.tensor_tensor(
                        out=psumT[0:nk, nt, :], in0=A.pt32[pb][0:nk, :], in1=rd[0:nk, :], op=ALU.mult),
                        reads=[A.r_pt32[pb], r_rd], writes=[r_psT])
                else:
                    P.op("pool", lambda e, nk=nk, pb=pb: e.tensor_tensor(
                        out=pn[0:nk, :], in0=A.pt32[pb][0:nk, :], in1=rd[0:nk, :], op=ALU.mult),
                        reads=[A.r_pt32[pb], r_rd], writes=[r_pn])
                    P.op("pool", lambda e, nt=nt, nk=nk: e.tensor_tensor(
                        out=psumT[0:nk, nt, :], in0=psumT[0:nk, nt, :], in1=pn[0:nk, :], op=ALU.add),
                        reads=[r_pn, r_psT], writes=[r_psT])
        for qs in range(QW // 128):
            for nt, nk in ntl:
                P.op("pe", lambda e, qs=qs, nt=nt, nk=nk: e.matmul(
                    misc[:, 0:NSLC], lhsT=psumT[0:nk, nt, qs * 128:(qs + 1) * 128], rhs=ovl[0:nk, nt, :],
                    start=(nt == 0), stop=(nt == len(ntl) - 1)), reads=[r_psT, r_c], writes=[r_misc])
            P.op("dve", lambda e, qs=qs, cb=cb: e.tensor_tensor(out=impm[:], in0=misc[:, 0:NSLC], in1=madd[cb][:, qs, :],
                                                                op=ALU.add), reads=[r_misc, r_madd[cb]], writes=[r_sel])
            if NSLC > 16:
                P.op("dve", lambda e: e.max(out=t8a[:], in_=impm[:]), reads=[r_sel], writes=[r_sel])
                P.op("dve", lambda e: e.match_replace(out=imp2[:], in_to_replace=t8a[:], in_values=impm[:],
                                                      imm_value=-3.0e38), reads=[r_sel], writes=[r_sel])
                P.op("dve", lambda e: e.max(out=t8b[:], in_=imp2[:]), reads=[r_sel], writes=[r_sel])
                P.op("dve", lambda e: e.tensor_scalar(out=bia[:], in0=impm[:], scalar1=t8b[:, 7:8], scalar2=-30000.0,
                                                      op0=ALU.is_lt, op1=ALU.mult), reads=[r_sel], writes=[r_sel])
            else:
                P.op("dve", lambda e: e.memset(bia[:], 0.0), writes=[r_sel])
            P.op("pe", lambda e, qs=qs: e.transpose(out=misc[0:NSLC, 128 + qs * 128:256 + qs * 128], in_=bia[:],
                                                    identity=idf[:]), reads=[r_sel, r_c], writes=[r_misc])
        P.op("act", lambda e: e.copy(out=biasT[:], in_=misc[0:NSLC, 128:128 + QW]), reads=[r_misc], writes=[r_bT])
        for hl in range(4):
            ktiles = []
            for kt in range(2 * c + 2):
                d = dict(kT=ksT[:, kt * 128:(kt + 1) * 128], r_k=r_in, v=vs[:, kt, :], r_v=r_in, nk=128,
                         bias=(e64[:, kt * 128:(kt + 1) * 128], biasT[:], [r_c, r_bT]))
                if kt >= 2 * c:
                    d["mask"] = (cmask[:, (kt - 2 * c) * 256:(kt - 2 * c + 1) * 256], r_c)
                ktiles.append(d)
            o_ps, r_o, d_ps, r_d, _, _ = softmax_branch(P, A, qT[:, hl, c * QW:(c + 1) * QW], r_in, ktiles)
            finish_branch(hl, 1, o_ps, r_o, d_ps, r_d, c, first=False)
            ktiles = []
            for r in range(6):
                kt = 2 * c - 4 + r
                if kt < 0:
                    continue
                d = dict(kT=kwT[:, kt * 128:(kt + 1) * 128], r_k=r_in, v=vw[:, kt, :], r_v=r_in, nk=128)
                if r in (0, 1):
                    d["mask"] = (wmask[:, r * 256:(r + 1) * 256], r_c)
                elif r in (4, 5):
                    d["mask"] = (cmask[:, (r - 4) * 256:(r - 3) * 256], r_c)
                ktiles.append(d)
            o_ps, r_o, d_ps, r_d, _, _ = softmax_branch(P, A, qT[:, hl, c * QW:(c + 1) * QW], r_in, ktiles)
            finish_branch(hl, 2, o_ps, r_o, d_ps, r_d, c, first=False)
            ob = ybi % 2
            ybi += 1
            P.op("pool", lambda e, hl=hl, ob=ob, cb=cb: e.tensor_tensor(
                out=yb[ob][:], in0=yacc[hl][:], in1=gt[cb][:, hl, :], op=ALU.mult),
                reads=[r_ya[hl], r_gt[cb]], writes=[r_yb[ob]])
            P.dma(lambda e, hl=hl, ob=ob, c=c: e.dma_start(out=ywrite(hl, c * QW, QW), in_=yb[ob][:]), reads=[r_yb[ob]])


def nsa_consts(S=SEQ):
    NCMP = (S - 32) // 16 + 1
    NSLC = S // 64
    NKT = S // 128
    half = 64
    inv_freq = np.exp(-math.log(10000.0) * np.arange(half, dtype=np.float32) / half).astype(np.float32)
    pos_c = (np.arange(256) * 16 + 31).astype(np.float32)
    ang = pos_c[:, None] * inv_freq[None, :]
    c = np.cos(ang).astype(np.float32)
    s = np.sin(ang).astype(np.float32)
    cosc = np.concatenate([c, c], 1)
    sinc = np.concatenate([-s, s], 1)
    n = np.arange(256)
    j = np.arange(NSLC)
    ov = ((n[:, None] * 16 < j[None, :] * 64 + 64) & (n[:, None] * 16 + 32 > j[None, :] * 64)).astype(np.float32)
    ov[NCMP:] = 0
    ovl = np.ascontiguousarray(ov.reshape(2, 128, NSLC).transpose(1, 0, 2))
    t = np.arange(S)
    cur = t // 64
    forced = (j[None, :] == 0) | (j[None, :] == cur[:, None]) | (j[None, :] == cur[:, None] - 1)
    madd = np.where(forced, 1e30, 0.0).astype(np.float32)
    madd = np.where(j[None, :] <= cur[:, None], madd, -1e30).astype(np.float32)
    cmpm = ((n[:, None] * 16 + 31) <= t[None, :]).astype(np.float32)
    cmpm[NCMP:] = 0
    e64 = np.zeros((NSLC, NKT * 128), np.float32)
    for kt in range(NKT):
        e64[2 * kt, kt * 128:kt * 128 + 64] = 1
        e64[2 * kt + 1, kt * 128 + 64:kt * 128 + 128] = 1
    selg = np.zeros((12, 12 * 128), np.float32)
    for r in range(12):
        selg[r, r * 128:(r + 1) * 128] = 1
    k = np.arange(128)[:, None]
    q = np.arange(256)[None, :]
    cm = np.concatenate([(k <= q), (k + 128 <= q)], axis=1).astype(np.float32)
    wm = np.concatenate([((q - k + 512 - 128 * r >= 0) & (q - k + 512 - 128 * r < 512)) for r in (0, 1)], axis=1)
    return {"cosc": cosc, "sinc": sinc, "identf": np.eye(128, dtype=np.float32), "ovl": ovl, "madd": madd,
            "cmpm": cmpm.astype(NPBF), "e64": e64.astype(NPBF), "selg": selg, "cmask": cm.astype(NPBF),
            "wmask": wm.astype(np.float32).astype(NPBF)}


LAYER_KEYS = [
    dict(norm="l0_norm", w_in="l0_w_in", q_norm="l0_q_norm", k_norm="l0_k_norm", w_out="l0_w_out"),
    dict(norm="l1_norm", w_in="l1_w_in", w_out="l1_w_out"),
    dict(norm="l2_norm", w_in="l2_w_in", q_norm="l2_q_norm", kc_norm="l2_kc_norm", ks_norm="l2_ks_norm",
         kw_norm="l2_kw_norm", cmp_wk="l2_cmp_wk", cmp_wv="l2_cmp_wv", cmp_pos="l2_cmp_pos", w_out="l2_w_out"),
    dict(norm="l3_norm", w_in="l3_w_in", q_norm="l3_q_norm", k_norm="l3_k_norm", w_out="l3_w_out"),
]
CFGS = [
    ([(4, 0), (4, 0), (0, 512)], ["silu"] * 4, 0),
    ([(0, 512)], ["copy"] * 8 + ["silu"] * 4, 0),
    ([(4, 0), (2, 256)], ["copy", "copy"] + ["silu"] * 4, 12),
]
RG = [[0, 1, 2, 3], [4, 5, 6, 7]]
N_LAYERS = 4


def _cfg_dims(cfg):
    tm, fm, nsg = cfg
    n_tm = sum(a * 128 + b for a, b in tm)
    NR = sum(a for a, b in tm)
    NV = sum(b for a, b in tm)
    NF = 128 * len(fm)
    return n_tm, NR, NV, NF, n_tm + NF + nsg


def build_fused(n_layers=N_LAYERS):
    nc = bass.Bass("TRN2", target_bir_lowering=False)
    S = SEQ

    def din(name, shape, dt):
        return nc.dram_tensor(name, list(shape), dt, kind="ExternalInput").ap()

    def dint(name, shape, dt):
        return nc.dram_tensor(name, list(shape), dt).ap()

    xfull = din("xfull", [S, 2048], F32)
    xq0 = din("xq0", [1024, 2048], F32)
    sel = din("sel", [128, 4], F32)
    out = nc.dram_tensor("out", [1024, 2048], F32, kind="ExternalOutput").ap()
    cst = dict(cos=din("cos", [S, 512], F32), sin=din("sin", [S, 512], F32),
               identf=din("identf", [128, 128], F32), identb=din("identb", [128, 128], BF16),
               esel=din("esel", [16, 2048], BF16), cmask=din("cmask", [128, 512], BF16),
               tri=din("tri", [128, 128], BF16), dmask=din("dmask", [128, 2048], BF16),
               cosc=din("cosc", [256, 128], F32), sinc=din("sinc", [256, 128], F32),
               ovl=din("ovl", [128, 2, 64], F32), madd=din("madd", [S, 64], F32), cmpm=din("cmpm", [256, S], BF16),
               e64=din("e64", [64, S], BF16), selg=din("selg", [12, 12 * 128], F32), wmask=din("wmask", [128, 512], BF16))
    P = Prog(nc)
    xg = None
    xqc_prev = None
    next_hook, next_rx = None, None
    for li in range(n_layers):
        kind = li % 3
        cfg = CFGS[kind]
        n_tm, NR, NV, NF, ncols = _cfg_dims(cfg)
        pre = "L%d_" % li
        w = din(pre + "w", [2048, ncols], F32)
        gx = din(pre + "gx", [128, 16], F32)
        gains = din(pre + "gains", [128, n_tm], F32)
        w_out = din(pre + "wout", [2048, 2048], F32)
        ropeT = dint(pre + "ropeT", [max(NR, 1), 128, S], BF16)
        vtm = dint(pre + "vtm", [S, NV], BF16)
        fmT = dint(pre + "fmT", [NF, S], BF16)
        sgT = dint(pre + "sgT", [12, S], F32)
        yTc = [dint(pre + "yTc%d" % j, [512, 1024], BF16) for j in range(4)]
        yG = [dint(pre + "yG%d" % j, [2048, 1024], BF16) for j in range(4)]
        if li == 0:
            x_tile = lambda T: xfull[T * 128:(T + 1) * 128, :]
        else:
            x_tile = lambda T, xg=xg: xg[T % 8][(T // 8) * 128:(T // 8 + 1) * 128, :]
        io = dict(x_tile=x_tile, w=w, gx=gx, gains=gains, cos=cst["cos"], sin=cst["sin"], identf=cst["identf"],
                  identb=cst["identb"], ropeT=ropeT, vtm=vtm, fmT=fmT, sgT=sgT, pre_hook=next_hook, r_x=next_rx)
        emit_proj(P, cfg[0], cfg[1], cfg[2], io, S)
        P.end_phase()
        ywrite = lambda h, c0, n, yTc=yTc: yTc[c0 // 1024][h * 128:(h + 1) * 128, (c0 % 1024):(c0 % 1024) + n]
        if kind == 0:
            emit_moba(P, dict(ropeT=ropeT, vtm=vtm, gT=fmT, identf=cst["identf"], esel=cst["esel"],
                              cmask=cst["cmask"], ywrite=ywrite), S)
        elif kind == 1:
            emit_sb(P, dict(fmT=fmT, vtm=vtm, tri=cst["tri"], dmask=cst["dmask"], ywrite=ywrite), S)
        else:
            io = dict(ropeT=ropeT, vtm=vtm, fmT=fmT, sgT=sgT, wk=din(pre + "wk", [128, 32, 128], F32),
                      wv=din(pre + "wv", [128, 32, 128], F32), posT=din(pre + "posT", [128, 32], F32),
                      kcg=din(pre + "kcg", [128, 128], F32), ywrite=ywrite)
            for k in ("cosc", "sinc", "identf", "ovl", "madd", "cmpm", "e64", "selg", "cmask", "wmask"):
                io[k] = cst[k]
            emit_nsa(P, io, S)
        P.end_phase()
        r_yG = Res()

        def hook1(yTc=yTc, yG=yG, r_yG=r_yG):
            for j in range(4):
                P.cc(lambda e, j=j: e.collective_compute(
                    "AllGather", ALU.bypass, replica_groups=RG, ins=[yTc[j].opt()], outs=[yG[j].opt()]),
                    writes=[r_yG])
        if li == 0:
            x_tile_c = lambda tt: xq0[tt * 128:(tt + 1) * 128, :]
        else:
            x_tile_c = lambda tt, xqc_prev=xqc_prev: xqc_prev[tt]
        if li < n_layers - 1:
            xqc = [dint(pre + "xqc%d" % i, [128, 2048], F32) for i in range(8)]
            out_tile = lambda tt, xqc=xqc: xqc[tt]
        else:
            xqc = None
            out_tile = lambda tt: out[tt * 128:(tt + 1) * 128, :]
        emit_outproj(P, 1024, w_out, x_tile_c, out_tile, yG=yG, sel=sel, pre_hook=hook1, r_yG=r_yG)
        P.end_phase()
        if li < n_layers - 1:
            xg = [dint(pre + "xg%d" % i, [512, 2048], F32) for i in range(8)]
            r_xg = Res()

            def hook2(xqc=xqc, xg=xg, r_xg=r_xg):
                for i in range(8):
                    P.cc(lambda e, i=i: e.collective_compute(
                        "AllGather", ALU.bypass, replica_groups=RG, ins=[xqc[i].opt()], outs=[xg[i].opt()]),
                        writes=[r_xg])
            next_hook, next_rx = hook2, r_xg
            xqc_prev = xqc
    P.finish()
    return nc


_PROGS = {}


def _rep(v):
    return np.ascontiguousarray(np.tile(np.asarray(v, np.float32)[None, :], (128, 1)))


def _layer_inputs(li, LK, core):
    kind = li % 3
    b, hg = divmod(core, 4)
    w_in = LK["w_in"]
    W = 2048
    sl = slice(hg * 512, (hg + 1) * 512)
    pre = "L%d_" % li
    m = {}
    if kind == 0:
        cols = [w_in[:, 0:W][:, sl], w_in[:, W:2 * W][:, sl], w_in[:, 2 * W:3 * W][:, sl], w_in[:, 3 * W:4 * W][:, sl]]
        gains = np.concatenate([np.tile(LK["q_norm"], 4), np.tile(LK["k_norm"], 4), np.ones(512, np.float32)])
    elif kind == 1:
        cols = [w_in[:, 2 * W:3 * W][:, sl], w_in[:, 0:W][:, sl], w_in[:, W:2 * W][:, sl], w_in[:, 3 * W:4 * W][:, sl]]
        gains = np.ones(512, np.float32)
    else:
        g = hg
        h1 = slice(g * 128, (g + 1) * 128)
        bg = [7168 + br * 16 + 4 * g + hl for br in range(3) for hl in range(4)]
        cols = [w_in[:, 0:2048][:, sl], w_in[:, 3072:3584][:, h1], w_in[:, 4096:4608][:, h1],
                w_in[:, 3584:4096][:, h1], w_in[:, 4608:5120][:, h1],
                w_in[:, 2048:2560][:, h1], w_in[:, 2560:3072][:, h1], w_in[:, 5120:7168][:, sl], w_in[:, bg]]
        gains = np.concatenate([np.tile(LK["q_norm"], 4), LK["ks_norm"], LK["kw_norm"], np.ones(256, np.float32)])
        m[pre + "wk"] = np.ascontiguousarray(LK["cmp_wk"].transpose(1, 0, 2))
        m[pre + "wv"] = np.ascontiguousarray(LK["cmp_wv"].transpose(1, 0, 2))
        m[pre + "posT"] = np.ascontiguousarray(LK["cmp_pos"].T)
        m[pre + "kcg"] = _rep(LK["kc_norm"])
    m[pre + "w"] = np.ascontiguousarray(np.concatenate(cols, axis=1))
    m[pre + "gx"] = np.ascontiguousarray(LK["norm"].reshape(16, 128).T)
    m[pre + "gains"] = _rep(gains)
    m[pre + "wout"] = LK["w_out"]
    return m


def kernel(**inputs):
    x = np.asarray(inputs["x"], np.float32)
    cos, sin = rope_tables(SEQ)
    identf = np.eye(128, dtype=np.float32)
    cst = {"cos": cos, "sin": sin, "identf": identf, "identb": identf.astype(NPBF)}
    cst.update(moba_consts())
    cst.update(sb_consts())
    cst.update(nsa_consts())
    LKs = [{k: np.asarray(inputs[v], np.float32) for k, v in LAYER_KEYS[li].items()} for li in range(N_LAYERS)]
    maps = []
    for core in range(NCORES):
        b, sq = divmod(core, 4)
        selv = np.zeros((128, 4), np.float32)
        selv[:, sq] = 1.0
        m = {"xfull": np.ascontiguousarray(x[b]), "xq0": np.ascontiguousarray(x[b, sq * 1024:(sq + 1) * 1024]),
             "sel": selv}
        m.update(cst)
        for li in range(N_LAYERS):
            m.update(_layer_inputs(li, LKs[li], core))
        maps.append(m)
    if "fused" not in _PROGS:
        _PROGS["fused"] = build_fused()
    res = run_bass_kernel_spmd(_PROGS["fused"], maps, core_ids=list(range(NCORES)))
    r = res.results
    out = np.stack([np.concatenate([r[b * 4 + sq]["out"] for sq in range(4)], axis=0) for b in range(BATCH)], axis=0)
    return out.astype(np.float32)
```
